# Optimizing a Trainium2 kernel written in Bass

```python
import jax, jax.numpy as jnp
from jax import lax
import numpy as np

D_MODEL = 1024
BATCH = 16
SEQ = 256
DEPTH = 1
DEC_BATCH = 4
DEC_SEQ = 2048
PAST_LEN = 512

GRID_W = 64
D_A = 512
NH_A = 4
DH_A = D_A // NH_A
CHUNK = 128
D_B = 512
CONV_W = 31
D_FF = 2816
FFN_CONV = 3
N_MOD = 6
SPLIT_SIZES = (D_A, D_A, D_A, D_A, NH_A, NH_A, NH_A, NH_A, 2 * D_B, D_MODEL, D_MODEL)
IN_COLS = 4 * D_A + 4 * NH_A + 2 * D_B + 2 * D_MODEL
FORGET_BIAS = 3.0
EPS = 1e-6

kernel_name = "hybrid_mlstm_conformer_dit_step"


def rmsnorm(x, g):
    xf = x.astype(jnp.float32)
    y = xf * lax.rsqrt(jnp.mean(xf * xf, axis=-1, keepdims=True) + EPS)
    return y.astype(x.dtype) * g


def layernorm(x, g, b):
    xf = x.astype(jnp.float32)
    mu = jnp.mean(xf, axis=-1, keepdims=True)
    var = jnp.mean(jnp.square(xf - mu), axis=-1, keepdims=True)
    return ((xf - mu) * lax.rsqrt(var + EPS)).astype(x.dtype) * g + b


def dwconv1d(x, w, b):
    C = x.shape[-1]
    y = lax.conv_general_dilated(x, w.astype(x.dtype)[:, None, :], window_strides=(1,), padding='SAME',
                                 dimension_numbers=('NWC', 'WIO', 'NWC'), feature_group_count=C)
    return y + b


def dwconv_grid(x, w, b):
    B, T, C = x.shape
    rows = T // GRID_W
    xg = x.reshape(B, rows, GRID_W, C)
    y = lax.conv_general_dilated(xg, w.astype(x.dtype)[:, :, None, :], window_strides=(1, 1), padding='SAME',
                                 dimension_numbers=('NHWC', 'HWIO', 'NHWC'), feature_group_count=C)
    return y.reshape(B, T, C) + b


def mlstm_chunked(q, k, v, ig, fg, C0, n0, m0):
    B, NH, T, DH = q.shape
    nc = T // CHUNK
    f32 = jnp.float32
    q, k, v, ig = (a.astype(f32) for a in (q, k, v, ig))
    logf = jax.nn.log_sigmoid(fg.astype(f32))

    def chunks(a):
        return jnp.moveaxis(a.reshape(B, NH, nc, CHUNK, *a.shape[3:]), 2, 0)

    xs = (chunks(q), chunks(k), chunks(v), chunks(ig), chunks(logf))
    tril = jnp.tril(jnp.ones((CHUNK, CHUNK), dtype=bool))

    def step(carry, xc):
        C, n, m = carry
        qc, kc, vc, ic, lfc = xc
        b = jnp.cumsum(lfc, axis=-1)
        g = b + m[..., None]
        Dm = jnp.where(tril, b[..., :, None] - b[..., None, :] + ic[..., None, :], -jnp.inf)
        mj = jnp.maximum(g, jnp.max(Dm, axis=-1))
        inter = jnp.exp(g - mj)
        S = jnp.einsum('bhld,bhsd->bhls', qc, kc) * jnp.exp(Dm - mj[..., None])
        num = inter[..., None] * jnp.einsum('bhld,bhde->bhle', qc, C) + jnp.einsum('bhls,bhse->bhle', S, vc)
        den = inter * jnp.einsum('bhld,bhd->bhl', qc, n) + jnp.sum(S, axis=-1)
        h = num / jnp.maximum(jnp.abs(den), jnp.exp(-mj))[..., None]
        gL = b[..., -1] + m
        wk = b[..., -1:] - b + ic
        m_new = jnp.maximum(gL, jnp.max(wk, axis=-1))
        decay = jnp.exp(gL - m_new)
        wkn = jnp.exp(wk - m_new[..., None])
        C_new = decay[..., None, None] * C + jnp.einsum('bhs,bhsd,bhse->bhde', wkn, kc, vc)
        n_new = decay[..., None] * n + jnp.einsum('bhs,bhsd->bhd', wkn, kc)
        return (C_new, n_new, m_new), h

    (C, n, m), hs = lax.scan(step, (C0.astype(f32), n0.astype(f32), m0.astype(f32)), xs)
    h = jnp.moveaxis(hs, 0, 2).reshape(B, NH, T, DH)
    return h, C, n, m


def mlstm_bidir(q, k, v, i_f, f_f, i_b, f_b, st_f, st_b):
    h_f, Cf, nf, mf = mlstm_chunked(q, k, v, i_f, f_f, *st_f)
    fl = lambda a: jnp.flip(a, axis=2)
    h_b, Cb, nb, mb = mlstm_chunked(fl(q), fl(k), fl(v), fl(i_b), fl(f_b), *st_b)
    return h_f + fl(h_b), (Cf, nf, mf), (Cb, nb, mb)


def trunk_layer(x, mod, st_f, st_b, on_grid, p):
    B, T, _ = x.shape
    sh1, sc1, g1, sh2, sc2, g2 = jnp.split(mod.astype(x.dtype), N_MOD, axis=-1)

    h = rmsnorm(x, p['norm1_g']) * (1 + sc1) + sh1
    proj = jnp.einsum('btd,dc->btc', h, p['w_in']) + p['b_in']
    offs = [int(o) for o in np.cumsum(SPLIT_SIZES)[:-1]]
    q, k, v, o, i_f, f_f, i_b, f_b, u, ga, gb = jnp.split(proj, offs, axis=-1)
    heads = lambda a: a.reshape(B, T, NH_A, DH_A).transpose(0, 2, 1, 3)
    gts = lambda a: a.transpose(0, 2, 1)
    hA, st_f, st_b = mlstm_bidir(heads(q) * (DH_A ** -0.5), heads(k), heads(v),
                                 gts(i_f), gts(f_f), gts(i_b), gts(f_b), st_f, st_b)
    hA = hA * lax.rsqrt(jnp.mean(hA * hA, axis=-1, keepdims=True) + EPS)
    hA = hA.transpose(0, 2, 1, 3).reshape(B, T, D_A).astype(x.dtype) * p['mlstm_norm_g']
    branch_a = jnp.einsum('bte,ed->btd', jax.nn.sigmoid(o) * hA, p['w_proj_a'])

    ua, ub = jnp.split(u, 2, axis=-1)
    z = dwconv1d(ua * jax.nn.sigmoid(ub), p['conv_dw_w'], p['conv_dw_b'])
    z = jax.nn.silu(layernorm(z, p['conv_ln_g'], p['conv_ln_b']))
    branch_b = jnp.einsum('bte,ed->btd', z, p['w_proj_b'])

    merged = jax.nn.sigmoid(ga) * branch_a + jax.nn.sigmoid(gb) * branch_b
    x = x + g1 * jnp.einsum('btd,de->bte', merged, p['w_out'])

    h2 = rmsnorm(x, p['norm2_g']) * (1 + sc2) + sh2
    gate, val = jnp.split(jnp.einsum('btd,df->btf', h2, p['w_up']), 2, axis=-1)
    if on_grid:
        gate = dwconv_grid(gate, p['ffn_dw_w'], p['ffn_dw_b'])
    else:
        gate = dwconv1d(gate, p['ffn_dw_w'][FFN_CONV // 2], p['ffn_dw_b'])
    x = x + g2 * jnp.einsum('btf,fd->btd', jax.nn.gelu(gate) * val, p['w_down'])
    return x, st_f, st_b


def setup_inputs(seed: int = 0) -> dict:
    key = jax.random.key(seed)
    ks = jax.random.split(key, 32)
    nrm = lambda i, shape, s: jax.random.normal(ks[i], shape, jnp.float32) * s
    D = D_MODEL
    b_in = nrm(10, (DEPTH, IN_COLS), 0.02)
    off_ff = 4 * D_A + NH_A
    off_fb = 4 * D_A + 3 * NH_A
    b_in = b_in.at[:, off_ff:off_ff + NH_A].add(FORGET_BIAS).at[:, off_fb:off_fb + NH_A].add(FORGET_BIAS)
    return {
        'x_prompt': nrm(0, (BATCH, SEQ, D), 1.0),
        'x_sample': nrm(1, (DEC_BATCH, DEC_SEQ, D), 1.0),
        'c': nrm(2, (DEC_BATCH, D), 1.0),
        'state_C': nrm(3, (DEC_BATCH, DEPTH, 2, NH_A, DH_A, DH_A), 0.1),
        'state_n': nrm(4, (DEC_BATCH, DEPTH, 2, NH_A, DH_A), 0.1),
        'state_m': nrm(5, (DEC_BATCH, DEPTH, 2, NH_A), 0.5),
        'c_ctx': nrm(6, (D,), 1.0),
        'w_ada': nrm(7, (DEPTH, D, N_MOD * D), 0.5 * D ** -0.5),
        'b_ada': nrm(8, (DEPTH, N_MOD * D), 0.02),
        'norm1_g': 1.0 + nrm(9, (DEPTH, D), 0.02),
        'w_in': nrm(11, (DEPTH, D, IN_COLS), D ** -0.5),
        'b_in': b_in,
        'mlstm_norm_g': 1.0 + nrm(12, (DEPTH, D_A), 0.02),
        'w_proj_a': nrm(13, (DEPTH, D_A, D), D_A ** -0.5),
        'conv_dw_w': nrm(14, (DEPTH, CONV_W, D_B), CONV_W ** -0.5),
        'conv_dw_b': nrm(15, (DEPTH, D_B), 0.02),
        'conv_ln_g': 1.0 + nrm(16, (DEPTH, D_B), 0.02),
        'conv_ln_b': nrm(17, (DEPTH, D_B), 0.02),
        'w_proj_b': nrm(18, (DEPTH, D_B, D), D_B ** -0.5),
        'w_out': nrm(19, (DEPTH, D, D), D ** -0.5),
        'norm2_g': 1.0 + nrm(20, (DEPTH, D), 0.02),
        'w_up': nrm(21, (DEPTH, D, 2 * D_FF), D ** -0.5),
        'ffn_dw_w': nrm(22, (DEPTH, FFN_CONV, FFN_CONV, D_FF), 1.0 / FFN_CONV),
        'ffn_dw_b': nrm(23, (DEPTH, D_FF), 0.02),
        'w_down': nrm(24, (DEPTH, D_FF, D), D_FF ** -0.5),
        'final_norm_g': 1.0 + nrm(25, (D,), 0.02),
    }


def reference(x_prompt, x_sample, c, state_C, state_n, state_m, c_ctx,
              w_ada, b_ada, norm1_g, w_in, b_in, mlstm_norm_g, w_proj_a,
              conv_dw_w, conv_dw_b, conv_ln_g, conv_ln_b, w_proj_b, w_out,
              norm2_g, w_up, ffn_dw_w, ffn_dw_b, w_down, final_norm_g):
    Bp = x_prompt.shape[0]
    f32 = jnp.float32
    xc = x_prompt
    xl = x_sample
    Cs, ns, ms = [], [], []
    for l in range(DEPTH):
        p = {'norm1_g': norm1_g[l], 'w_in': w_in[l], 'b_in': b_in[l], 'mlstm_norm_g': mlstm_norm_g[l],
             'w_proj_a': w_proj_a[l], 'conv_dw_w': conv_dw_w[l], 'conv_dw_b': conv_dw_b[l],
             'conv_ln_g': conv_ln_g[l], 'conv_ln_b': conv_ln_b[l], 'w_proj_b': w_proj_b[l],
             'w_out': w_out[l], 'norm2_g': norm2_g[l], 'w_up': w_up[l], 'ffn_dw_w': ffn_dw_w[l],
             'ffn_dw_b': ffn_dw_b[l], 'w_down': w_down[l]}
        mod_ctx = (jax.nn.silu(c_ctx) @ w_ada[l] + b_ada[l])[None, None, :]
        zC = jnp.zeros((Bp, NH_A, DH_A, DH_A), f32)
        zn = jnp.zeros((Bp, NH_A, DH_A), f32)
        zm = jnp.zeros((Bp, NH_A), f32)
        xc, sf, sb = trunk_layer(xc, mod_ctx, (zC, zn, zm), (zC, zn, zm), False, p)
        Cs.append(jnp.stack([sf[0], sb[0]], axis=1))
        ns.append(jnp.stack([sf[1], sb[1]], axis=1))
        ms.append(jnp.stack([sf[2], sb[2]], axis=1))
        mod_lat = (jax.nn.silu(c) @ w_ada[l] + b_ada[l])[:, None, :]
        st_f = (state_C[:, l, 0], state_n[:, l, 0], state_m[:, l, 0])
        st_b = (state_C[:, l, 1], state_n[:, l, 1], state_m[:, l, 1])
        xl, _, _ = trunk_layer(xl, mod_lat, st_f, st_b, True, p)
    y_prompt = rmsnorm(xc, final_norm_g)
    y_sample = rmsnorm(xl, final_norm_g)
    new_state_C = jnp.stack(Cs, axis=1)
    new_state_n = jnp.stack(ns, axis=1)
    new_state_m = jnp.stack(ms, axis=1)
    return (y_prompt, y_sample, new_state_C, new_state_n, new_state_m)
```

```python
import numpy as np
import types
from contextlib import ExitStack
import concourse.bass as bass
import concourse.mybir as mybir
from concourse.bass_utils import run_bass_kernel_spmd

F32 = mybir.dt.float32
BF16 = mybir.dt.bfloat16
AF = mybir.ActivationFunctionType
ALU = mybir.AluOpType
AX = mybir.AxisListType

D = 1024; DA = 512; NH = 4; DH = 128; L = 128; DB = 512; CW = 31; DFF = 2816; GW = 64
NF = DFF // 128
EPS = 1e-6
NEG = -1.0e30
IN_COLS = 4 * DA + 4 * NH + 2 * DB + 2 * D
OFF = {}
_o = 0
for _n, _s in [("q", 512), ("k", 512), ("v", 512), ("o", 512), ("i_f", 4), ("f_f", 4), ("i_b", 4), ("f_b", 4),
               ("ua", 512), ("ub", 512), ("ga", 1024), ("gb", 1024)]:
    OFF[_n] = (_o, _o + _s); _o += _s
GOFF = OFF["i_f"][0]

CP = {}
_o = 0
for _n, _s in [("bq", 4), ("bk", 4), ("bo", 4), ("bua", 4), ("bub", 4), ("bga", 8), ("bgb", 8), ("convb", 4), ("lng", 4),
               ("lnb", 4), ("mng", 4), ("convw", 4 * CW), ("ffnw", NF * 9), ("ffnb", NF), ("badac", 48), ("n1gc", 8), ("n2gc", 8)]:
    CP[_n] = _o; _o += _s
NCP = _o
RP = {}
_o = 0
for _n, _s in [("bk", 512), ("bv", 512), ("bg", 16), ("n1g", 1024), ("n2g", 1024), ("fng", 1024), ("bada", 6144)]:
    RP[_n] = _o; _o += _s
NRP = _o

NFULL = 13
NCH = 20
TF = NFULL * 128
TCH = [(0, 512), (512, 512), (1024, 512), (1536, 128)]


def _freeze(fn):
    if fn.__closure__ is None:
        return fn
    cells = []
    for c in fn.__closure__:
        try:
            cells.append(types.CellType(c.cell_contents))
        except ValueError:
            cells.append(c)
    return types.FunctionType(fn.__code__, fn.__globals__, fn.__name__, fn.__defaults__, tuple(cells))


class Sched:
    def __init__(self, nc, n_dma_sems=32):
        self.nc = nc
        self.eng = {"pe": nc.tensor, "act": nc.scalar, "dve": nc.vector, "pool": nc.gpsimd, "sp": nc.sync}
        self.sem = {}
        self.cnt = {}
        self._cms = []
        for e in ["pe", "act", "dve", "pool"]:
            cm = nc.semaphore("s_" + e)
            self.sem[e] = cm.__enter__()
            self._cms.append(cm)
            self.cnt[e] = 0
        self.dma_sems = []
        for i in range(n_dma_sems):
            cm = nc.semaphore("s_dma%d" % i)
            self.dma_sems.append([cm.__enter__(), 0])
            self._cms.append(cm)
        self.dma_rr = 0
        self.dma_rr_q = {}
        self.known = {e: {} for e in self.eng}
        self.lastw = {}
        self.reads = {}
        self.n_inst = 0
        self.inherit = {}
        self.alias = {}
        self.deferred = None

    def _inh(self, e, b):
        if not self.inherit:
            return
        if isinstance(b, tuple):
            nm = b[0]
        elif isinstance(b, str):
            nm = b
        else:
            nm = getattr(b, "name", None)
        nm = self.alias.get(nm, nm)
        evs = self.inherit.get(nm)
        if evs:
            for ev in evs:
                self._wait(e, ev)

    def close(self):
        for cm in reversed(self._cms):
            cm.__exit__(None, None, None)

    @staticmethod
    def _key(b):
        return b if isinstance(b, (str, tuple)) else id(b)

    def _wait(self, e, ev):
        key, semh, val = ev
        if self.known[e].get(key, 0) >= val:
            return
        self.eng[e].wait_ge(semh, val)
        self.known[e][key] = val
        self.n_inst += 1

    def _deps(self, e, reads, writes, skip_self=False):
        for b in reads:
            self._inh(e, b)
        for b in writes:
            self._inh(e, b)
        for b in reads:
            ev = self.lastw.get(self._key(b))
            if ev is not None and not (skip_self and ev[0] == e):
                self._wait(e, ev)
        for b in writes:
            k = self._key(b)
            ev = self.lastw.get(k)
            if ev is not None and not (skip_self and ev[0] == e):
                self._wait(e, ev)
            for ev in self.reads.get(k, ()):
                if not (skip_self and ev[0] == e):
                    self._wait(e, ev)

    def _record(self, ev, reads, writes):
        for b in reads:
            self.reads.setdefault(self._key(b), []).append(ev)
        for b in writes:
            k = self._key(b)
            self.lastw[k] = ev
            self.reads[k] = []

    def begin_defer(self):
        assert self.deferred is None
        self.deferred = []

    def end_defer(self):
        ops = self.deferred
        self.deferred = None
        n = len(ops)
        lastw = {}; readers = {}
        deps = [set() for _ in range(n)]
        for i, o in enumerate(ops):
            rk = [self._key(b) for b in o["reads"]]; wk = [self._key(b) for b in o["writes"]]
            for k in rk:
                if k in lastw:
                    deps[i].add(lastw[k])
            for k in wk:
                if k in lastw:
                    deps[i].add(lastw[k])
                for r in readers.get(k, ()):
                    deps[i].add(r)
            for k in rk:
                readers.setdefault(k, []).append(i)
            for k in wk:
                lastw[k] = i
                readers[k] = []
            deps[i].discard(i)
        users = [[] for _ in range(n)]
        ndep = [len(d) for d in deps]
        for i, d in enumerate(deps):
            for j in d:
                users[j].append(i)
        blevel = [0.0] * n
        for i in range(n - 1, -1, -1):
            m_ = 0.0
            for u in users[i]:
                if blevel[u] > m_:
                    m_ = blevel[u]
            blevel[i] = ops[i]["cost"] + 0.3 + m_
        finish = [0.0] * n
        etime = {}
        ready = [i for i in range(n) if ndep[i] == 0]
        order = []
        while ready:
            sts = {}
            bs0 = None
            for i in ready:
                o = ops[i]
                st = etime.get(o["e"], 0.0)
                for j in deps[i]:
                    lat = 0.15 if ops[j]["e"] == o["e"] else 0.9
                    st = max(st, finish[j] + lat)
                sts[i] = st
                if bs0 is None or st < bs0:
                    bs0 = st
            best = None
            for i in ready:
                if sts[i] <= bs0 + 0.25:
                    if best is None or blevel[i] > blevel[best] + 1e-9 or (abs(blevel[i] - blevel[best]) <= 1e-9 and i < best):
                        best = i
            bs = sts[best]
            o = ops[best]
            ready.remove(best)
            if o["kind"] == "dma":
                etime[o["e"]] = bs + (0.9 if o["e"] == "pool" else 0.2)
                finish[best] = bs + o["cost"]
            else:
                etime[o["e"]] = bs + o["cost"]
                finish[best] = bs + o["cost"] + (0.15 if o["e"] == "pe" else 0.05)
            order.append(best)
            for u in users[best]:
                ndep[u] -= 1
                if ndep[u] == 0:
                    ready.append(u)
        assert len(order) == n
        for i in order:
            o = ops[i]
            if o["kind"] == "dma":
                self.dma(o["e"], o["out"], o["in_"], reads=o["reads"], writes=o["writes"], **o["kw"])
            else:
                self.op(o["e"], o["fn"], reads=o["reads"], writes=o["writes"], skip_self=o["skip_self"])
        return max(finish) if n else 0.0

    def op(self, e, fn, reads=(), writes=(), skip_self=False, c=None):
        if self.deferred is not None:
            self.deferred.append(dict(kind="op", e=e, fn=_freeze(fn), reads=list(reads), writes=list(writes), skip_self=skip_self,
                                      cost=(c if c is not None else {"pe": 0.2, "pool": 1.0}.get(e, 0.5))))
            return None
        self._deps(e, reads, writes, skip_self)
        inst = fn()
        self.cnt[e] += 1
        inst.then_inc(self.sem[e], 1)
        ev = (e, self.sem[e], self.cnt[e])
        self._record(ev, reads, writes)
        self.n_inst += 1
        return ev

    def dma(self, q, out, in_, reads=(), writes=(), **kw):
        if self.deferred is not None:
            self.deferred.append(dict(kind="dma", e=q, out=out, in_=in_, reads=list(reads), writes=list(writes), kw=kw, cost=2.5))
            return None
        half = len(self.dma_sems) // 2
        base = 0 if q == "sp" else half
        rr = self.dma_rr_q.get(q, 0)
        idx = base + rr
        self.dma_rr_q[q] = (rr + 1) % half
        slot = self.dma_sems[idx]
        key = "dma%d" % idx
        if slot[1] > 0:
            self._wait(q, (key, slot[0], slot[1]))
        self._deps(q, reads, writes)
        inst = self.eng[q].dma_start(out=out, in_=in_, **kw)
        slot[1] += 16
        inst.then_inc(slot[0], 16)
        ev = (key, slot[0], slot[1])
        self._record(ev, reads, writes)
        self.n_inst += 1
        return ev

    def _all_events(self):
        evs = [(e, self.sem[e], self.cnt[e]) for e in self.cnt if self.cnt[e] > 0]
        for i, (s, v) in enumerate(self.dma_sems):
            if v > 0:
                evs.append(("dma%d" % i, s, v))
        return evs

    def barrier(self):
        evs = self._all_events()
        for e in ["pe", "act", "dve", "pool", "sp"]:
            for ev in evs:
                if ev[0] != e or e != "pe":
                    self._wait(e, ev)
        self.lastw = {}
        self.reads = {}

    def wait_everything(self, e):
        for ev in self._all_events():
            self._wait(e, ev)


class Tl:
    def __init__(self, ap, name, rng, group):
        self.ap = ap; self.name = name; self.rng = rng; self.group = group

    def __getitem__(self, idx):
        return self.ap[idx]


class Arena:
    def __init__(self, nc, stack, kbytes=207):
        self.total = kbytes * 1024
        self.t = stack.enter_context(nc.sbuf_tensor("arena", [128, self.total // 2], BF16))
        self.free = [(0, self.total)]
        self.live = []
        self.freed = []
        self.sched = None

    def alloc(self, name, shape, dt=F32, group="top", high=False):
        esz = 4 if dt == F32 else 2
        n = 1
        for d_ in shape[1:]:
            n *= d_
        nbytes = (n * esz + 63) // 64 * 64
        order = range(len(self.free) - 1, -1, -1) if high else range(len(self.free))
        for i in order:
            o, sz = self.free[i]
            if sz >= nbytes:
                if high:
                    self.free[i] = (o, sz - nbytes)
                    o = o + sz - nbytes
                else:
                    self.free[i] = (o + nbytes, sz - nbytes)
                break
        else:
            raise MemoryError("arena full allocating %s (%d B); free=%s" % (name, nbytes, self.free))
        ap = self.t[0:shape[0], o // 2:(o + n * esz) // 2]
        if dt == F32:
            ap = ap.bitcast(F32)
        if len(shape) == 3:
            ap = ap.rearrange("p (a b) -> p a b", a=shape[1])
        elif len(shape) == 4:
            ap = ap.rearrange("p (a b c) -> p a b c", a=shape[1], b=shape[2])
        tl = Tl(ap, name, (o, nbytes), group)
        self.live.append(tl)
        evs = {}
        for (fo, fsz, fe) in self.freed:
            if fo < o + nbytes and o < fo + fsz:
                for ev in fe:
                    if ev[0] not in evs or evs[ev[0]][2] < ev[2]:
                        evs[ev[0]] = ev
        if self.sched is not None:
            if evs:
                self.sched.inherit[name] = list(evs.values())
            else:
                self.sched.inherit.pop(name, None)
        return tl

    def free_group(self, group):
        keep = []
        snap = self.sched._all_events() if self.sched is not None else []
        for tl in self.live:
            if tl.group == group:
                self.free.append(tl.rng)
                self.freed.append((tl.rng[0], tl.rng[1], snap))
            else:
                keep.append(tl)
        self.live = keep
        self.free.sort()
        merged = []
        for o, sz in self.free:
            if sz == 0:
                continue
            if merged and merged[-1][0] + merged[-1][1] == o:
                merged[-1] = (merged[-1][0], merged[-1][1] + sz)
            else:
                merged.append((o, sz))
        self.free = merged


def build(dbg=None, stop_after=None):
    nc = bass.Bass("TRN2", target_bir_lowering=False)
    di = lambda n, s: nc.dram_tensor(n, s, F32, kind="ExternalInput").ap()
    do = lambda n, s: nc.dram_tensor(n, s, F32, kind="ExternalOutput").ap()
    xp = di("xp", [512, D]); xs = di("xs", [2048, D]); ccol = di("ccol", [128, 16])
    stCn = di("stCn", [128, 8, 129]); stm = di("stm", [1, 8])
    colpack = di("colpack", [128, NCP]); rowpack = di("rowpack", [1, NRP])
    w_ada = di("w_ada", [D, 6 * D]); w_in = di("w_in", [D, IN_COLS]); w_pa = di("w_proj_a", [DA, D]); w_pb = di("w_proj_b", [DB, D])
    w_out = di("w_out", [D, D]); w_up = di("w_up", [D, 2 * DFF]); w_down = di("w_down", [DFF, D])
    yp = do("yp", [512, D]); ys = do("ys", [1024, D]); oCn = do("oCn", [128, 16, 129]); om = do("om", [1, 16])
    dbg_out = {}

    S = Sched(nc)
    V, A, P, T = nc.vector, nc.scalar, nc.gpsimd, nc.tensor
    top = ExitStack()
    AR = Arena(nc, top)
    AR.sched = S
    sb = AR.alloc

    S.alias.update({"SSb": "SS", "den": None})

    def tap(name, ap, shape, key):
        if dbg is None or name not in dbg:
            return
        o = nc.dram_tensor("dbg_" + name, list(shape), F32, kind="ExternalOutput").ap()
        S.dma("pool", o, ap, reads=[key])
        dbg_out[name] = shape

    def end_phase(*groups):
        S.end_defer()
        for g in groups:
            AR.free_group(g)
        S.begin_defer()

    ps = [top.enter_context(nc.psum_tensor("ps%d" % i, [128, 512], F32)) for i in range(8)]
    ps_rr = [0]

    def psum():
        p = ps[ps_rr[0]]
        ps_rr[0] = (ps_rr[0] + 1) % 8
        return p

    def psum7():
        p = ps[ps_rr[0] % 7]
        ps_rr[0] = (ps_rr[0] + 1) % 7
        return p

    ident_f = sb("ident_f", [128, 128]); ident_b = sb("ident_b", [128, 128], BF16)
    U01 = sb("U01", [128, 128]); L01 = sb("L01", [128, 128]); maskU = sb("maskU", [128, 128]); maskL = sb("maskL", [128, 128])
    ones_f = sb("ones_f", [128, 128]); ones_b = sb("ones_b", [128, 128], BF16)
    cpk = sb("cpk", [128, NCP]); bkB = sb("bkB", [128, 512]); bvB = sb("bvB", [128, 512]); bgB = sb("bgB", [128, 16])
    csil = sb("csil", [128, 16], BF16); ccs = sb("ccs", [128, 16])
    bada_b = sb("bada_b", [1, 512], BF16)
    sm = sb("sm", [128, 64])
    hT = sb("hT", [128, 8, TF], BF16, "hT")
    WBS = {}

    def wb_alloc(nslots, group):
        WBS["t"] = sb("wbs_" + group, [128, nslots * 1024], BF16, group)
        WBS["n"] = nslots; WBS["s"] = 0; WBS["b"] = 0; WBS["g"] = group
        S.alias["wbs"] = "wbs_" + group

    def wsmall():
        i = WBS["s"] % WBS["n"]; WBS["s"] += 1
        return WBS["t"][:, i * 1024:(i + 1) * 1024].rearrange("p (k n) -> p k n", k=8), [("wbs", WBS["g"], i)]

    def wbig():
        nb = WBS["n"] // 4
        j = WBS["b"] % nb; WBS["b"] += 1
        return WBS["t"][:, j * 4096:(j + 1) * 4096].rearrange("p (k n) -> p k n", k=8), [("wbs", WBS["g"], 4 * j + i) for i in range(4)]

    wb_alloc(8, "wb")

    def cp(name, idx=0, n=1):
        return cpk[:, CP[name] + idx:CP[name] + idx + n]

    for t_, val in [(ident_f, 1.0), (U01, 1.0), (L01, 1.0), (maskU, 0.0), (maskL, 0.0), (ones_f, 1.0)]:
        S.op("pool", lambda t_=t_, val=val: P.memset(t_[:], val), writes=[t_])
    S.op("pool", lambda: P.memset(ones_b[:], 1.0), writes=[ones_b])
    S.op("pool", lambda: P.affine_select(ident_f[:], ident_f[:], [[1, 128]], ALU.is_equal, 0.0, base=0, channel_multiplier=-1), reads=[ident_f], writes=[ident_f])
    S.op("pool", lambda: P.affine_select(U01[:], U01[:], [[1, 128]], ALU.is_ge, 0.0, base=0, channel_multiplier=-1), reads=[U01], writes=[U01])
    S.op("pool", lambda: P.affine_select(L01[:], L01[:], [[-1, 128]], ALU.is_ge, 0.0, base=0, channel_multiplier=1), reads=[L01], writes=[L01])
    S.op("pool", lambda: P.affine_select(maskU[:], maskU[:], [[1, 128]], ALU.is_ge, NEG, base=0, channel_multiplier=-1), reads=[maskU], writes=[maskU])
    S.op("pool", lambda: P.affine_select(maskL[:], maskL[:], [[-1, 128]], ALU.is_ge, NEG, base=0, channel_multiplier=1), reads=[maskL], writes=[maskL])
    S.op("dve", lambda: V.tensor_copy(ident_b[:], ident_f[:]), reads=[ident_f], writes=[ident_b])
    S.dma("sp", cpk[:], colpack, writes=[cpk])
    S.dma("sp", bkB[:], rowpack[:, RP["bk"]:RP["bk"] + 512].to_broadcast([128, 512]), writes=[bkB])
    S.dma("sp", bvB[:], rowpack[:, RP["bv"]:RP["bv"] + 512].to_broadcast([128, 512]), writes=[bvB])
    S.dma("sp", bgB[:], rowpack[:, RP["bg"]:RP["bg"] + 16].to_broadcast([128, 16]), writes=[bgB])
    S.dma("sp", ccs[:], ccol, writes=[ccs])
    S.op("act", lambda: A.activation(csil[:], ccs[:], AF.Silu), reads=[ccs], writes=[csil])

    def load_w(dst3, src2, key):
        S.dma("pool", dst3, src2.rearrange("(kt p) n -> p kt n", p=128), writes=(key if isinstance(key, list) else [key]))

    MOD = {}

    def mod_alloc(names, group):
        for n_ in names:
            MOD[n_] = sb(n_, [128, D] if n_ == "ngB" else [128, 2, D], F32, group, high=group.startswith("mod2"))

    last_wada = [None]

    def mod_prefetch(col0, w=None, wk_=None, bb=None):
        if w is None:
            w, wk_ = wbig(); bb = bada_b
        last_wada[0] = wk_
        load_w(w, w_ada[:, col0:col0 + 512], wk_)
        S.dma("pool", bb[:], rowpack[:, RP["bada"] + col0:RP["bada"] + col0 + 512], writes=[bb])
        return (w, wk_, bb)

    def mod_chunk(col0, evac, pre=None):
        w, wk_, bb = pre if pre is not None else mod_prefetch(col0)
        for which in range(2):
            p = psum()
            for kt in range(8):
                S.op("pe", lambda kt=kt, which=which, p=p: T.matmul(p[:], csil[:, which * 8 + kt:which * 8 + kt + 1].to_broadcast([128, 128]), w[:, kt, :], start=(kt == 0), stop=False),
                     reads=[csil] + wk_, writes=[p], skip_self=True)
            S.op("pe", lambda p=p: T.matmul(p[:], ones_b[0:1, :], bb[0:1, :], start=False, stop=True), reads=[ones_b, bb], writes=[p], skip_self=True)
            evac(which, p)

    MODC = sb("MODC", [128, 2, 48])
    GCs = {1: sb("GC1", [128, 2, 8]), 2: sb("GC2", [128, 2, 8])}
    csil_v = csil[:, :].rearrange("p (w k) -> p k w", w=2)

    def mod_cols(col0, pre=None):
        w, wk_, bb = pre if pre is not None else mod_prefetch(col0)
        for jj in range(4):
            j = col0 // 128 + jj
            p = psum()
            for kt in range(8):
                S.op("pe", lambda kt=kt: T.matmul(p[:, 0:2], w[:, kt, jj * 128:(jj + 1) * 128], csil_v[:, kt, :], start=(kt == 0), stop=(kt == 7)),
                     reads=[csil] + wk_, writes=[p], skip_self=True, c=0.08)
            S.op("dve", lambda: V.tensor_scalar(MODC[:, :, j], p[:, 0:2], cp("badac", j), None, ALU.add), reads=[p, cpk], writes=[("MODC", j)], c=0.12)

    def mod_gain(layer):
        base = 0 if layer == 1 else 24
        gc = GCs[layer]
        S.op("dve", lambda: V.tensor_scalar(gc[:], MODC[:, :, base + 8:base + 16], 1.0, None, ALU.add), reads=[("MODC", base + 8 + i) for i in range(8)], writes=[gc], c=0.12)
        S.op("dve", lambda: V.tensor_tensor(gc[:], gc[:], cp("n1gc" if layer == 1 else "n2gc", 0, 8).unsqueeze(1).to_broadcast([128, 2, 8]), ALU.mult), reads=[gc, cpk], writes=[gc], c=0.12)

    def load_ng(name):
        ngB = MOD["ngB"]
        S.dma("sp", ngB[:], rowpack[:, RP[name]:RP[name] + D].to_broadcast([128, D]), writes=[ngB])

    def mod_shift_scale(base_col):
        modA, modB, ngB = MOD["modA"], MOD["modB"], MOD["ngB"]
        for half in range(2):
            c0 = half * 512
            mod_chunk(base_col + c0, lambda which, p, c0=c0: S.op("act", lambda: A.copy(modB[:, which, c0:c0 + 512], p[:]), reads=[p], writes=[modB]))
            mod_chunk(base_col + D + c0, lambda which, p, c0=c0: S.op("dve", lambda: V.scalar_tensor_tensor(modA[:, which, c0:c0 + 512], p[:], 1.0, ngB[:, c0:c0 + 512], ALU.add, ALU.mult), reads=[p, ngB], writes=[modA]))

    def mod_gate(base_col, pres=None):
        modG = MOD["modG"]
        for half in range(2):
            c0 = half * 512
            mod_chunk(base_col + c0, lambda which, p, c0=c0: S.op("act", lambda: A.copy(modG[:, which, c0:c0 + 512], p[:]), reads=[p], writes=[modG]),
                      pre=(pres[half] if pres else None))

    def rsqrt_cols(dst, src, scale):
        S.op("dve", lambda: V.tensor_scalar(dst, src, scale, EPS, ALU.mult, ALU.add), reads=[sm], writes=[sm])
        S.op("act", lambda: A.activation(dst, dst, AF.Ln), reads=[sm], writes=[sm])
        S.op("act", lambda: A.activation(dst, dst, AF.Exp, scale=-0.5), reads=[sm], writes=[sm])

    def chunk_src(gi):
        return (xp[gi * 128:(gi + 1) * 128, :], 0) if gi < 4 else (xs[(gi - 4) * 128:(gi - 3) * 128, :], 1)

    def norm_phase(chunks, get_x, which_of, dstT_of, dkey_of, mod_thunk, group, batched=True, layer=1):
        n_ = len(chunks)
        ssb = sb("ssb_" + group, [128, 64], F32, group)
        S.alias["ssb"] = "ssb_" + group
        junk = sb("junk_" + group, [128, D], BF16, group)
        xns = [sb("xn%d_" % i + group, [128, D], F32, group) for i in range(2)]
        if not batched:
            mod_thunk()
        for i, gi in enumerate(chunks):
            xin, xkey = get_x(gi, 0, i)
            S.op("act", lambda: A.activation(junk[:], xin, AF.Square, accum_out=ssb[:, i:i + 1]), reads=[xkey], writes=[junk, ("ssb", i)], c=0.6)
            if not batched:
                S.op("dve", lambda: V.tensor_scalar(ssb[:, 32 + i:33 + i], ssb[:, i:i + 1], 1.0 / D, EPS, ALU.mult, ALU.add), reads=[("ssb", i)], writes=[("ssb", 32 + i)], c=0.15)
                S.op("act", lambda: A.activation(ssb[:, 32 + i:33 + i], ssb[:, 32 + i:33 + i], AF.Ln), reads=[("ssb", 32 + i)], writes=[("ssb", 32 + i)], c=0.2)
                S.op("act", lambda: A.activation(ssb[:, 32 + i:33 + i], ssb[:, 32 + i:33 + i], AF.Exp, scale=-0.5), reads=[("ssb", 32 + i)], writes=[("ssb", 32 + i)], c=0.2)
        if batched:
            mod_thunk()
        if batched:
            S.op("dve", lambda: V.tensor_scalar(ssb[:, 32:32 + n_], ssb[:, 0:n_], 1.0 / D, EPS, ALU.mult, ALU.add), reads=[("ssb", i) for i in range(n_)], writes=[ssb])
            S.op("act", lambda: A.activation(ssb[:, 32:32 + n_], ssb[:, 32:32 + n_], AF.Ln), reads=[ssb], writes=[ssb])
            S.op("act", lambda: A.activation(ssb[:, 32:32 + n_], ssb[:, 32:32 + n_], AF.Exp, scale=-0.5), reads=[ssb], writes=[ssb])
        gc = GCs[layer]; base = 0 if layer == 1 else 24
        shk = [("MODC", base + i) for i in range(8)]
        for i, gi in enumerate(chunks):
            xin, xkey = get_x(gi, 1, i)
            which = which_of(gi)
            xn = xns[i % 2]
            S.op("dve", lambda: V.tensor_scalar(xn[:], xin, ssb[:, 32 + i:33 + i], None, ALU.mult), reads=[xkey, (ssb if batched else ("ssb", 32 + i))], writes=[xn], c=1.1)
            pp = [psum(), psum()]
            for kt in range(8):
                S.op("pe", lambda kt=kt: T.transpose(pp[kt // 4][:, (kt % 4) * 128:(kt % 4 + 1) * 128], xn[:, kt * 128:(kt + 1) * 128], ident_f[:]), reads=[xn, ident_f], writes=[pp[kt // 4]], skip_self=True, c=0.15)
            dst = dstT_of(gi)
            for kt in range(8):
                src_ = pp[kt // 4][:, (kt % 4) * 128:(kt % 4 + 1) * 128]
                if kt in (1, 4, 6):
                    S.op("dve", lambda kt=kt, src_=src_: V.scalar_tensor_tensor(dst[:, kt, :], src_, gc[:, which, kt:kt + 1], MODC[:, which, base + kt:base + kt + 1].to_broadcast([128, 128]), ALU.mult, ALU.add),
                         reads=[pp[kt // 4], gc] + shk, writes=[dkey_of(gi)], c=0.25)
                else:
                    S.op("act", lambda kt=kt, src_=src_: A.activation(dst[:, kt, :], src_, AF.Identity, bias=MODC[:, which, base + kt:base + kt + 1], scale=gc[:, which, kt:kt + 1]),
                         reads=[pp[kt // 4], gc] + shk, writes=[dkey_of(gi)], c=0.28)

    S.begin_defer()
    hTf = sb("hTf", [128, 8, 7 * 128], BF16, "hTf")

    def hT_chunk(gi):
        return hT[:, :, gi * 128:(gi + 1) * 128] if gi < NFULL else hTf[:, :, (gi - NFULL) * 128:(gi - NFULL + 1) * 128]

    xall = sb("xall", [128, NCH, D], F32, "p1a")

    def get_x_1a(gi, stage, i):
        if stage == 0:
            src, _ = chunk_src(gi)
            S.dma("sp", xall[:, gi, :], src, reads=(last_wada[0] if (gi >= 2 and last_wada[0]) else []), writes=[("xall", gi)])
        return xall[:, gi, :], ("xall", gi)

    def mod_1a():
        for c_ in range(4):
            mod_cols(c_ * 512)
        mod_gain(1)

    norm_phase(list(range(NCH)), get_x_1a, lambda gi: 0 if gi < 4 else 1, hT_chunk, lambda gi: ("hT", gi), mod_1a, "p1a", batched=False)
    tap("hT", hT[:, :, :].rearrange("p k t -> p (k t)"), [128, 8 * TF], ("hT", 0))
    k_tok = sb("k_tok", [128, NCH, 512], BF16, "kv")
    v_tok = sb("v_tok", [128, NCH, 4, 129], BF16, "kv")
    g_tok = sb("g_tok", [128, NCH, 16], F32, "kv")
    wg_ = sb("wg_", [128, 8, 16], BF16, "kv")
    S.op("pool", lambda: P.memset(v_tok[:], 1.0), writes=[v_tok])
    wk, wkk = wbig(); load_w(wk, w_in[:, OFF["k"][0]:OFF["k"][1]], wkk)
    wv, wvk = wbig(); load_w(wv, w_in[:, OFF["v"][0]:OFF["v"][1]], wvk)
    load_w(wg_[:], w_in[:, GOFF:GOFF + 16], wg_)

    def proj_tm(gi, wt, wkeys, ncol, p):
        hc = hT_chunk(gi)
        for kt in range(8):
            S.op("pe", lambda kt=kt: T.matmul(p[:, 0:ncol], hc[:, kt, :], wt[:, kt, 0:ncol], start=(kt == 0), stop=(kt == 7)), reads=[("hT", gi)] + wkeys, writes=[p], skip_self=True, c=ncol / 2300.0 + 0.04)

    for gi in range(NCH):
        p1 = psum(); proj_tm(gi, wk, wkk, 512, p1)
        S.op("dve", lambda: V.tensor_tensor(k_tok[:, gi, :], p1[:], bkB[:], ALU.add), reads=[p1, bkB], writes=[("k_tok", gi)], c=0.7)
        p2 = psum(); proj_tm(gi, wv, wvk, 512, p2)
        S.op("dve", lambda: V.tensor_tensor(v_tok[:, gi, :, 0:128], p2[:].rearrange("p (h e) -> p h e", h=4), bvB[:].rearrange("p (h e) -> p h e", h=4), ALU.add),
             reads=[p2, bvB, v_tok], writes=[("v_tok", gi)], c=0.7)
        p3 = psum(); proj_tm(gi, wg_, [wg_], 16, p3)
        S.op("dve", lambda: V.tensor_tensor(g_tok[:, gi, :], p3[:, 0:16], bgB[:], ALU.add), reads=[p3, bgB], writes=[("g_tok", gi)], c=0.15)
    tap("k_tok", k_tok[:, :, :].rearrange("p g e -> p (g e)"), [128, NCH * 512], ("k_tok", NCH - 1))
    tap("g_tok", g_tok[:, :, :].rearrange("p g e -> p (g e)"), [128, NCH * 16], ("g_tok", NCH - 1))

    end_phase("p1a")

    def hT_keys_for(c0, n):
        return [("hT", g) for g in range(c0 // 128, (c0 + n + 127) // 128)]

    def proj_fm(wtile, wkey, c0, n, p):
        for kt in range(8):
            S.op("pe", lambda kt=kt: T.matmul(p[:, 0:n], wtile[:, kt, :], hT[:, kt, c0:c0 + n], start=(kt == 0), stop=(kt == 7)),
                 reads=(wkey if isinstance(wkey, list) else [wkey]) + hT_keys_for(c0, n), writes=[p], skip_self=True, c=n / 2300.0 + 0.03)

    PREP = sb("PREP", [128, NCH, 40], F32, "kv")
    sc = sb("sc", [128, 96], F32, "kv")
    scall = sb("scall", [128, NCH, 16], F32, "hTf")
    g4 = g_tok[:, :, :].rearrange("p g (d x) -> p g d x", d=2)
    S.op("act", lambda: A.activation(scall[:, :, 0:8].rearrange("p g (d x) -> p g d x", d=2), g4[:, :, :, 4:8], AF.Exp, scale=-1.0), reads=[("g_tok", NCH - 1)], writes=[scall])
    S.op("act", lambda: A.activation(scall[:, :, 8:16], scall[:, :, 0:8], AF.Ln, bias=1.0), reads=[scall], writes=[scall])
    pcm = psum()
    S.op("pe", lambda: T.matmul(pcm[:, 0:80], U01[:], scall[:, :, 8:12], start=True, stop=True), reads=[U01, scall], writes=[pcm], skip_self=True)
    S.op("pe", lambda: T.matmul(pcm[:, 80:160], L01[:], scall[:, :, 12:16], start=True, stop=True), reads=[L01, scall], writes=[pcm], skip_self=True)
    S.op("pe", lambda: T.matmul(pcm[:, 160:320], ones_f[:], scall[:, :, 8:16], start=True, stop=True), reads=[ones_f, scall], writes=[pcm], skip_self=True)
    for d in range(2):
        pv_ = pcm[:, d * 80:(d + 1) * 80].rearrange("p (g x) -> p g x", x=4)
        S.op("dve", lambda d=d, pv_=pv_: V.tensor_tensor(PREP[:, :, d * 4:d * 4 + 4], g_tok[:, :, d * 8:d * 8 + 4], pv_, ALU.add), reads=[("g_tok", NCH - 1), pcm], writes=[("PREP", 0)])
        S.op("act", lambda d=d, pv_=pv_: A.copy(PREP[:, :, 8 + d * 4:12 + d * 4], pv_), reads=[pcm], writes=[("PREP", 0)])
    S.op("act", lambda: A.copy(PREP[:, :, 32:40], pcm[:, 160:320].rearrange("p (g x) -> p g x", x=8)), reads=[pcm], writes=[("PREP", 0)])
    for gi in range(NCH):
        for d in range(2):
            if gi >= NFULL and d == 0:
                continue
            pk = ("PREP", gi + 1)
            pa_ = psum()
            for hd in range(4):
                j = d * 4 + hd
                S.op("pe", lambda j=j, hd=hd: T.matmul(pa_[:, hd * 128:(hd + 1) * 128], PREP[:, gi, j:j + 1].to_broadcast([128, 128]), ident_f[:], start=True, stop=True),
                     reads=[("PREP", 0), ident_f], writes=[pa_], skip_self=True)
            pa3 = pa_[:].rearrange("p (h s) -> p h s", h=4)
            S.op("dve", lambda: V.reduce_max(PREP[:, gi, 24 + d * 4:28 + d * 4], pa3, axis=AX.X), reads=[pa_], writes=[pk])
    PK = lambda gi: [("PREP", 0), ("PREP", gi + 1)]
    tap("PREP", PREP[:, :, :].rearrange("p g e -> p (g e)"), [128, NCH * 40], ("PREP", NCH))
    end_phase("hTf")

    zT = sb("zT", [128, 4, TF], BF16, "zact")
    GP = 286 * 2 + 1182
    SEG0 = [15, 301, 587]
    gluT = sb("gluT", [128, 4, GP], BF16, "p1b")
    zsqT = sb("zsqT", [128, 4, TF], BF16, "p1b")
    diag = sb("diag", [128, 4 * CW, 128], BF16, "p1b")
    sgt = [sb("sgt%d" % i, [128, 512], BF16, "p1b") for i in range(2)]
    lnm = [sb("lnm%d" % i, [128, 512], F32, "p1b") for i in range(2)]; lnr = [sb("lnr%d" % i, [128, 512], F32, "p1b") for i in range(2)]
    lnt = [sb("lnt%d" % i, [128, 512], F32, "p1b") for i in range(2)]
    S.op("pool", lambda: P.memset(gluT[:], 0.0), writes=[gluT])
    for ct in range(4):
        for tap_ in range(CW):
            if tap_ % 2 == 0:
                S.op("dve", lambda tap_=tap_: V.tensor_scalar(diag[:, ct * CW + tap_, :], ident_b[:], cp("convw", ct * CW + tap_), None, ALU.mult), reads=[ident_b, cpk], writes=[("diag", ct, 0)], c=0.29)
            else:
                S.op("act", lambda tap_=tap_: A.activation(diag[:, ct * CW + tap_, :], ident_b[:], AF.Copy, scale=cp("convw", ct * CW + tap_)), reads=[ident_b, cpk], writes=[("diag", ct, 1)], c=0.25)
    for ct in range(4):
        wa, wak = wsmall(); load_w(wa, w_in[:, OFF["ua"][0] + ct * 128:OFF["ua"][0] + (ct + 1) * 128], wak)
        wu, wuk = wsmall(); load_w(wu, w_in[:, OFF["ub"][0] + ct * 128:OFF["ub"][0] + (ct + 1) * 128], wuk)
        for ci, (c0, n) in enumerate(TCH):
            pA = psum(); pB = psum()
            proj_fm(wa, wak, c0, n, pA)
            proj_fm(wu, wuk, c0, n, pB)
            sg = sgt[ci % 2]
            S.op("act", lambda: A.activation(sg[:, 0:n], pB[:, 0:n], AF.Sigmoid, bias=cp("bub", ct)), reads=[pB, cpk], writes=[sg], c=0.6)
            pieces = [(0, 256, SEG0[0]), (256, 256, SEG0[1])] if c0 == 0 else [(0, n, SEG0[2] + c0 - 512)]
            for (o_, n_, d0) in pieces:
                S.op("dve", lambda o_=o_, n_=n_, d0=d0: V.scalar_tensor_tensor(gluT[:, ct, d0:d0 + n_], pA[:, o_:o_ + n_], cp("bua", ct), sg[:, o_:o_ + n_], ALU.add, ALU.mult),
                     reads=[pA, sg, cpk, gluT], writes=[("gluT", ct)], c=0.7)
    tapped = False
    cchunks = [("P", 0, 512), ("S", 512, 512), ("S", 1024, 512), ("S", 1536, 128)]
    for cj, (kind, c0, n) in enumerate(cchunks):
        for ct in range(4):
            p = psum()
            for tap_ in range(CW):
                if kind == "P":
                    rhs = gluT[:, ct, 0:572].rearrange("p (s x) -> p s x", s=2)[:, :, tap_:tap_ + 256]
                    out_ = p[:, 0:512].rearrange("p (s x) -> p s x", s=2)
                else:
                    ds = SEG0[2] + c0 - 512
                    rhs = gluT[:, ct, ds + tap_ - 15:ds + tap_ - 15 + n]
                    out_ = p[:, 0:n]
                S.op("pe", lambda tap_=tap_, rhs=rhs, out_=out_: T.matmul(out_, diag[:, ct * CW + tap_, :], rhs, start=(tap_ == 0), stop=(tap_ == CW - 1)),
                     reads=[("diag", ct, 0), ("diag", ct, 1), gluT, ("gluT", ct)], writes=[p], skip_self=True, c=n / 2300.0 + 0.05)
            S.op("act", lambda: A.activation(zT[:, ct, c0:c0 + n], p[:, 0:n], AF.Identity, bias=cp("convb", ct)), reads=[p, cpk], writes=[("zT", cj, ct)], c=0.6)
            S.op("act", lambda: A.activation(zsqT[:, ct, c0:c0 + n], p[:, 0:n], AF.Square, bias=cp("convb", ct)), reads=[p, cpk], writes=[("zsqT", cj, ct)], c=0.6)
        zk = [("zT", cj, ct) for ct in range(4)]; zqk = [("zsqT", cj, ct) for ct in range(4)]
        lm, lr, lt = lnm[cj % 2], lnr[cj % 2], lnt[cj % 2]
        pm = psum(); pq = psum()
        for ct in range(4):
            S.op("pe", lambda ct=ct: T.matmul(pm[:, 0:n], ones_b[:], zT[:, ct, c0:c0 + n], start=(ct == 0), stop=(ct == 3)), reads=[ones_b] + zk, writes=[pm], skip_self=True, c=n / 2300.0 + 0.05)
        for ct in range(4):
            S.op("pe", lambda ct=ct: T.matmul(pq[:, 0:n], ones_b[:], zsqT[:, ct, c0:c0 + n], start=(ct == 0), stop=(ct == 3)), reads=[ones_b] + zqk, writes=[pq], skip_self=True, c=n / 2300.0 + 0.05)
        S.op("act", lambda: A.mul(lm[:, 0:n], pm[:, 0:n], 1.0 / DB), reads=[pm], writes=[lm], c=0.6)
        S.op("dve", lambda: V.tensor_tensor(lt[:, 0:n], lm[:, 0:n], lm[:, 0:n], ALU.mult), reads=[lm], writes=[lt], c=1.1)
        S.op("dve", lambda: V.scalar_tensor_tensor(lr[:, 0:n], pq[:, 0:n], 1.0 / DB, lt[:, 0:n], ALU.mult, ALU.subtract), reads=[pq, lt], writes=[lr], c=0.7)
        S.op("dve", lambda: V.tensor_scalar(lr[:, 0:n], lr[:, 0:n], EPS, None, ALU.add), reads=[lr], writes=[lr], c=0.6)
        S.op("act", lambda: A.activation(lr[:, 0:n], lr[:, 0:n], AF.Ln), reads=[lr], writes=[lr], c=0.6)
        S.op("act", lambda: A.activation(lr[:, 0:n], lr[:, 0:n], AF.Exp, scale=-0.5), reads=[lr], writes=[lr], c=0.6)
        for ct in range(4):
            S.op("dve", lambda ct=ct: V.tensor_tensor(lt[:, 0:n], zT[:, ct, c0:c0 + n], lm[:, 0:n], ALU.subtract), reads=[("zT", cj, ct), lm], writes=[lt], c=0.8)
            S.op("dve", lambda ct=ct: V.tensor_tensor(lt[:, 0:n], lt[:, 0:n], lr[:, 0:n], ALU.mult), reads=[lt, lr], writes=[lt], c=1.1)
            S.op("act", lambda ct=ct: A.activation(zT[:, ct, c0:c0 + n], lt[:, 0:n], AF.Silu, bias=cp("lnb", ct), scale=cp("lng", ct)), reads=[lt, cpk], writes=[("zT", cj, ct), zT], c=0.7)
    tap("zact", zT[:, :, :].rearrange("p k t -> p (k t)"), [128, 4 * TF], zT)
    qT = sb("qT", [128, 4, TF], BF16, "p1c")
    bqs = sb("bqs", [128, 4], F32, "p1c")
    S.op("dve", lambda: V.tensor_scalar(bqs[:], cp("bq", 0, 4), DH ** -0.5, None, ALU.mult), reads=[cpk], writes=[bqs])
    for hd in range(4):
        w, wk_ = wsmall(); load_w(w, w_in[:, OFF["q"][0] + hd * 128:OFF["q"][0] + (hd + 1) * 128], wk_)
        for (c0, n) in TCH:
            p = psum(); proj_fm(w, wk_, c0, n, p)
            S.op("act", lambda: A.activation(qT[:, hd, c0:c0 + n], p[:, 0:n], AF.Identity, bias=bqs[:, hd:hd + 1], scale=DH ** -0.5), reads=[p, bqs], writes=[qT], c=0.6)
    MIN = sb("MIN", [128, (NCH + 6) * 2, 4], F32, "p1c")
    SS = sb("SS", [128, NCH * 2, 20], F32, "p1c")
    EMJ = sb("EMJ", [128, NFULL * 2, 4], F32, "p1c")
    S.op("pool", lambda: P.memset(SS[:, :, :], 0.0), writes=[SS])
    S.op("pool", lambda: P.memset(MIN[:, :, :], 0.0), writes=[MIN])

    def mslot(gi, d):
        return MIN[:, gi * 2 + d, :]

    chains = []
    for sq in range(2):
        chains.append((0, sq, 1, [2 * sq + 1, 2 * sq], NCH + sq * 2 + 1)); chains.append((0, sq, 0, [2 * sq, 2 * sq + 1], NCH + sq * 2))
    chains.append((1, 2, 1, list(range(NCH - 1, 3, -1)), NCH + 5)); chains.append((1, 2, 0, list(range(4, NFULL)), NCH + 4))
    for (q_, sq, d, lst, fin) in chains:
        if sq == 2:
            S.dma("sp", mslot(lst[0], d), stm[:, d * 4:d * 4 + 4].to_broadcast([128, 4]), reads=[MIN], writes=[("MIN", lst[0], d)])
    for k_ in range(16):
        for (q_, sq, d, lst, fin) in chains:
            if k_ >= len(lst):
                continue
            gi = lst[k_]; nslot = lst[k_ + 1] if k_ + 1 < len(lst) else fin
            s_ = SS[:, gi * 2 + d, :]
            S.op("dve", lambda s_=s_, gi=gi, d=d: V.tensor_tensor(s_[:, 0:4], mslot(gi, d), PREP[:, gi, 24 + d * 4:28 + d * 4], ALU.max), reads=PK(gi) + [("MIN", gi, d), MIN, SS], writes=[("SS", gi, d)], c=0.13)
            S.op("dve", lambda s_=s_, gi=gi, d=d, nslot=nslot: V.tensor_tensor(mslot(nslot, d), s_[:, 0:4], PREP[:, gi, 32 + d * 4:36 + d * 4], ALU.subtract), reads=[("SS", gi, d)] + PK(gi) + [MIN], writes=[("MIN", nslot, d)], c=0.16)
    all_ss = [("SS", gi, d) for gi in range(NCH) for d in range(2) if not (gi >= NFULL and d == 0)]
    all_min = [("MIN", gi, d) for gi in range(NCH) for d in range(2) if not (gi >= NFULL and d == 0)]
    ss4 = lambda lo, hi: SS[:, :, lo:hi].rearrange("p (g d) x -> p g d x", d=2)
    S.op("dve", lambda: V.tensor_tensor(ss4(4, 8), PREP[:, :, 0:8].rearrange("p g (d x) -> p g d x", d=2), ss4(0, 4), ALU.subtract), reads=[("PREP", 0)] + all_ss, writes=[("SSb", 0)])
    S.op("dve", lambda: V.tensor_tensor(SS[:, :, 8:12], MIN[:, 0:NCH * 2, :], SS[:, :, 0:4], ALU.subtract), reads=all_min + all_ss, writes=[("SSb", 1)])
    S.op("act", lambda: A.activation(SS[:, :, 12:20], SS[:, :, 4:12], AF.Exp), reads=[("SSb", 0), ("SSb", 1)], writes=[("SSb", 2)])
    S.op("dve", lambda: V.tensor_tensor(EMJ[:, :, :].rearrange("p (g d) x -> p g d x", d=2), PREP[:, 0:NFULL, 8:16].rearrange("p g (d x) -> p g d x", d=2),
                                        SS[:, 0:NFULL * 2, 0:4].rearrange("p (g d) x -> p g d x", d=2), ALU.subtract), reads=[("PREP", 0)] + all_ss, writes=[EMJ])
    S.op("act", lambda: A.activation(EMJ[:, :, :], EMJ[:, :, :], AF.Exp), reads=[EMJ], writes=[EMJ])
    SSK = [("SSb", 2)]
    end_phase("p1b", "wb")

    sbc = lambda name, shape, dt=F32: sb(name, shape, dt, "p1c")
    kT = sbc("kT", [128, 4, TF], BF16)
    hnT = sb("hnT", [128, 4, TF], BF16, "hnT")
    for gi in range(NFULL):
        pt = psum(); ptb = pt[:].bitcast(BF16)
        for hd in range(4):
            S.op("pe", lambda hd=hd: T.transpose(ptb[:, hd * 128:(hd + 1) * 128], k_tok[:, gi, hd * 128:(hd + 1) * 128], ident_b[:]), reads=[("k_tok", gi), ident_b], writes=[pt], skip_self=True, c=0.12)
        S.op("act", lambda: A.copy(kT[:, :, gi * 128:(gi + 1) * 128], ptb[:, 0:512].rearrange("p (h t) -> p h t", h=4)), reads=[pt], writes=[kT], c=0.6)
    tap("qT", qT[:, :, :].rearrange("p k t -> p (k t)"), [128, 4 * TF], qT)

    CIN = sbc("CIN", [128, NFULL * 2, 4 * 129], BF16)
    Cst = {(q_, d): sbc("Cst%d%d" % (q_, d), [128, 4, 129]) for q_ in range(2) for d in range(2)}
    Cdc = {(q_, d): sbc("Cdc%d%d" % (q_, d), [128, 4, 129]) for q_ in range(2) for d in range(2)}
    VWs = [sbc("VWs%d" % i, [128, 4, 129], BF16) for i in range(4)]
    vwn = [0]

    def cin(gi, d):
        return CIN[:, gi * 2 + d, :].rearrange("p (h e) -> p h e", h=4)


    def make_vw(gi, d, on_pool=False):
        vw = VWs[vwn[0] % len(VWs)]; vwn[0] += 1
        if on_pool:
            S.op("pool", lambda: P.tensor_tensor(vw[:], v_tok[:, gi, :, :], SS[:, gi * 2 + d, 12:16].unsqueeze(2).to_broadcast([128, 4, 129]), ALU.mult), reads=[("v_tok", gi)] + SSK, writes=[vw], c=1.2)
        else:
            for hd in range(4):
                S.op("act", lambda hd=hd: A.activation(vw[:, hd, :], v_tok[:, gi, hd, :], AF.Copy, scale=SS[:, gi * 2 + d, 12 + hd:13 + hd]), reads=[("v_tok", gi)] + SSK, writes=[vw], c=0.45)
        return vw

    def c_step(q_, d, gi):
        C = Cst[(q_, d)]; Cd = Cdc[(q_, d)]
        vw = make_vw(gi, d, on_pool=True)
        pd = [psum7(), psum7()]
        for hd in range(4):
            o_ = (hd % 2) * 129
            S.op("pe", lambda hd=hd, o_=o_: T.matmul(pd[hd // 2][:, o_:o_ + 129], k_tok[:, gi, hd * 128:(hd + 1) * 128], vw[:, hd, :], start=True, stop=True),
                 reads=[vw, ("k_tok", gi)], writes=[pd[hd // 2]], skip_self=True)
        S.op("pool", lambda: P.tensor_tensor(Cd[:], C[:], SS[:, gi * 2 + d, 16:20].unsqueeze(2).to_broadcast([128, 4, 129]), ALU.mult), reads=[C] + SSK, writes=[Cd], c=1.4)
        if gi < NFULL:
            S.op("act", lambda: A.copy(cin(gi, d), Cd[:]), reads=[Cd], writes=[("CIN", gi, d)], c=0.62)
        for b2 in range(2):
            S.op("dve", lambda b2=b2: V.tensor_tensor(C[:, 2 * b2:2 * b2 + 2, :], Cd[:, 2 * b2:2 * b2 + 2, :], pd[b2][:, 0:258].rearrange("p (h e) -> p h e", h=2), ALU.add),
                 reads=[Cd, pd[b2]], writes=[C], c=0.42)

    for (q_, sq, d, lst, fin) in chains:
        if sq == 2:
            S.dma("sp", Cst[(q_, d)][:], stCn[:, d * 4:d * 4 + 4, :], writes=[Cst[(q_, d)]])
    NB_ = 3
    ST = [sbc("ST%d" % i, [128, 4, 128], BF16) for i in range(NB_)]
    hdir = [sbc("hdir%d" % i, [128, 4, 128]) for i in range(4)]
    hnb = [sbc("hnb%d" % i, [128, 4, 128], BF16) for i in range(2)]
    hjunk = sbc("hjunk", [128, 128], BF16)
    scO = [sbc("scO%d" % i, [128, 8]) for i in range(NB_)]
    scF = [sbc("scF%d" % i, [128, 8]) for i in range(2)]
    den_bank = ps[7]
    ps_n = [7]

    units = [(gi, d) for gi in range(NFULL) for d in (1, 0)]
    pqs = {}

    vws = {}

    def u_pq(u):
        gi, d = units[u]; c0 = gi * 128
        pq_ = psum7(); pqs[u] = pq_
        for hd in range(4):
            S.op("pe", lambda hd=hd: T.matmul(pq_[:, hd * 128:(hd + 1) * 128], kT[:, hd, c0:c0 + 128], qT[:, hd, c0:c0 + 128], start=True, stop=True),
                 reads=[kT, qT], writes=[pq_], skip_self=True)
        vws[u] = make_vw(gi, d)

    pns = {}

    def u_st_mm(u):
        gi, d = units[u]; c0 = gi * 128; b_ = u % NB_
        pq_ = pqs[u]; vw = vws[u]
        m01 = U01 if d == 0 else L01
        S.op("dve", lambda: V.tensor_tensor(ST[b_][:], pq_[:].rearrange("p (h s) -> p h s", h=4), m01[:].unsqueeze(1).to_broadcast([128, 4, 128]), ALU.mult),
             reads=[pq_, m01], writes=[ST[b_]], c=0.6)
        pn = psum7(); pns[u] = pn
        dk = den_bank; dcol = (u % 4) * 4
        Cb = cin(gi, d)
        for hd in range(4):
            S.op("pe", lambda hd=hd: T.matmul(pn[:, hd * 128:(hd + 1) * 128], qT[:, hd, c0:c0 + 128], Cb[:, hd, 0:128], start=True, stop=False),
                 reads=[qT, ("CIN", gi, d)], writes=[pn], skip_self=True)
            S.op("pe", lambda hd=hd: T.matmul(pn[:, hd * 128:(hd + 1) * 128], ST[b_][:, hd, :], vw[:, hd, 0:128], start=False, stop=True),
                 reads=[ST[b_], vw], writes=[pn], skip_self=True)
        for hd in range(4):
            S.op("pe", lambda hd=hd: T.matmul(den_bank[:, dcol + hd:dcol + hd + 1], qT[:, hd, c0:c0 + 128], Cb[:, hd, 128:129], start=True, stop=False),
                 reads=[qT, ("CIN", gi, d)], writes=[dk], skip_self=True)
            S.op("pe", lambda hd=hd: T.matmul(den_bank[:, dcol + hd:dcol + hd + 1], ST[b_][:, hd, :], vw[:, hd, 128:129], start=False, stop=True),
                 reads=[ST[b_], vw], writes=[dk], skip_self=True)

    def u_fin(u):
        gi, d = units[u]; b_ = u % NB_
        sc_ = scO[b_]; pn = pns[u]
        dk = den_bank; dcol = (u % 4) * 4
        dn = den_bank[:, dcol:dcol + 4]
        S.op("dve", lambda: V.tensor_scalar(sc_[:, 0:4], dn, -1.0, None, ALU.mult), reads=[dk], writes=[sc_], c=0.1)
        S.op("dve", lambda: V.tensor_tensor(sc_[:, 0:4], sc_[:, 0:4], dn, ALU.max), reads=[dk, sc_], writes=[sc_], c=0.16)
        S.op("dve", lambda: V.tensor_tensor(sc_[:, 0:4], sc_[:, 0:4], EMJ[:, gi * 2 + d, :], ALU.max), reads=[sc_, EMJ], writes=[sc_], c=0.16)
        S.op("dve", lambda: V.reciprocal(sc_[:, 4:8], sc_[:, 0:4]), reads=[sc_], writes=[sc_], c=0.18)
        hd_ = hdir[u % 4]
        S.op("dve", lambda: V.tensor_tensor(hd_[:], pn[:].rearrange("p (h e) -> p h e", h=4), sc_[:, 4:8].unsqueeze(2).to_broadcast([128, 4, 128]), ALU.mult), reads=[pn, sc_], writes=[hd_], c=0.7)
        if d == 0:
            hb_ = hdir[(u - 1) % 4]; hf_ = hd_; c0 = gi * 128; f_ = scF[gi % 2]
            S.op("pool", lambda: P.tensor_tensor(hf_[:], hf_[:], hb_[:], ALU.add), reads=[hf_, hb_], writes=[hf_], c=1.26)
            hq_ = hdir[(u - 1) % 4]
            S.op("pool", lambda: P.tensor_tensor(hq_[:], hf_[:], hf_[:], ALU.mult), reads=[hf_], writes=[hq_], c=1.26)
            S.op("dve", lambda: V.reduce_sum(f_[:, 0:4], hq_[:], axis=AX.X), reads=[hq_], writes=[f_], c=0.69)
            S.op("dve", lambda: V.tensor_scalar(f_[:, 4:8], f_[:, 0:4], 1.0 / DH, EPS, ALU.mult, ALU.add), reads=[f_], writes=[f_], c=0.17)
            S.op("act", lambda: A.activation(f_[:, 4:8], f_[:, 4:8], AF.Ln), reads=[f_], writes=[f_], c=0.21)
            S.op("act", lambda: A.activation(f_[:, 4:8], f_[:, 4:8], AF.Exp, scale=-0.5), reads=[f_], writes=[f_], c=0.21)
            hn_ = hnb[gi % 2]
            S.op("dve", lambda: V.tensor_tensor(hn_[:], hf_[:], f_[:, 4:8].unsqueeze(2).to_broadcast([128, 4, 128]), ALU.mult), reads=[hf_, f_], writes=[hn_], c=0.69)
            pt = psum7(); ptb = pt[:].bitcast(BF16)
            for hd in range(4):
                S.op("pe", lambda hd=hd: T.transpose(ptb[:, hd * 128:(hd + 1) * 128], hn_[:, hd, :], ident_b[:]), reads=[hn_, ident_b], writes=[pt], skip_self=True)
            S.op("act", lambda: A.copy(hnT[:, :, c0:c0 + 128], ptb[:, 0:512].rearrange("p (h t) -> p h t", h=4)), reads=[pt], writes=[hnT], c=0.6)

    def emit_units(glist):
        us = []
        for gi in glist:
            us += [units.index((gi, 1)), units.index((gi, 0))]
        if not us:
            return
        u_pq(us[0])
        for i, u in enumerate(us):
            if i + 1 < len(us):
                u_pq(us[i + 1])
            u_st_mm(u)
            if i >= 1:
                u_fin(us[i - 1])
        u_fin(us[-1])

    ready = {1: [0, 1], 3: [2, 3], 8: [12, 11], 9: [10], 10: [9], 11: [8], 12: [7], 13: [6], 14: [5], 15: [4]}
    for k_ in range(16):
        for (q_, sq, d, lst, fin) in chains:
            if sq == 2:
                kk = k_
            else:
                kk = k_ - 2 * sq
                if kk < 0 or kk >= 2:
                    continue
            if kk >= len(lst):
                continue
            if sq < 2 and kk == 0:
                S.op("pool", lambda q_=q_, d=d: P.memset(Cst[(q_, d)][:], 0.0), writes=[Cst[(q_, d)]], c=0.5)
            c_step(q_, d, lst[kk])
            if sq < 2 and kk == len(lst) - 1:
                j0_ = (sq * 2 + d) * 4
                S.dma("sp", oCn[:, j0_:j0_ + 4, :], Cst[(q_, d)][:], reads=[Cst[(q_, d)]])
                S.dma("sp", om[:, j0_:j0_ + 4], MIN[0:1, fin * 2 + d, :], reads=[("MIN", fin, d)])
        emit_units(ready.get(k_, []))
    tap("hnT", hnT[:, :, :].rearrange("p k t -> p (k t)"), [128, 4 * TF], hnT)
    end_phase("p1c", "kv")
    wb_alloc(8, "wb")

    mergedT = sb("mergedT", [128, 8, TF], BF16, "merged")
    sgt = [sb("sgtc%d" % i, [128, 512], BF16, "p1c2") for i in range(2)]
    tmb = sb("tmb", [128, 512], BF16, "p1c2")
    mod_alloc(["modG"], "modG1")
    wo = sb("wo", [128, 8, D], BF16, "p1d")
    g1pre = []
    wa2s = [(sb("wa2_%d" % half, [128, 8, 512], BF16, "p1c2"), sb("bb2_%d" % half, [1, 512], BF16, "p1c2")) for half in range(2)]
    for hd in range(4):
        w, wk_ = wsmall(); load_w(w, w_in[:, OFF["o"][0] + hd * 128:OFF["o"][0] + (hd + 1) * 128], wk_)
        if hd == 1:
            for half in range(2):
                wa2, bb2 = wa2s[half]
                g1pre.append(mod_prefetch(2 * D + half * 512, wa2[:, :, :], [wa2], bb2))
        if hd == 3:
            load_w(wo[:], w_out, wo)
        for ci, (c0, n) in enumerate(TCH):
            p = psum(); proj_fm(w, wk_, c0, n, p)
            sg = sgt[ci % 2]
            S.op("act", lambda: A.activation(sg[:, 0:n], p[:, 0:n], AF.Sigmoid, bias=cp("bo", hd)), reads=[p, cpk], writes=[sg])
            S.op("dve", lambda: V.scalar_tensor_tensor(hnT[:, hd, c0:c0 + n], hnT[:, hd, c0:c0 + n], cp("mng", hd), sg[:, 0:n], ALU.mult, ALU.mult), reads=[hnT, cpk, sg], writes=[hnT])
    mod_gate(2 * D, g1pre)
    MOD1G = MOD["modG"]
    mod_alloc(["modG"], "mod2g")
    m2cols = [3 * D, 3 * D + 512, 4 * D, 4 * D + 512, 5 * D, 5 * D + 512]
    m2pre = {}

    def m2_prefetch(i):
        w_, k_, b_ = g1pre[i % 2]
        m2pre[i] = mod_prefetch(m2cols[i], w_, k_, b_)

    def m2_compute(i):
        modG2 = MOD["modG"]
        c0 = (i % 2) * 512
        if i < 4:
            mod_cols(m2cols[i], pre=m2pre[i])
            if i == 3:
                mod_gain(2)
        else:
            ev = lambda which, p: S.op("act", lambda: A.copy(modG2[:, which, c0:c0 + 512], p[:]), reads=[p], writes=[modG2])
            mod_chunk(m2cols[i], ev, pre=m2pre[i])

    m2_prefetch(0); m2_prefetch(1)
    for dt_ in range(8):
        if dt_ >= 1 and dt_ - 1 < 6:
            m2_compute(dt_ - 1)
            if dt_ + 1 < 6:
                m2_prefetch(dt_ + 1)
        for br, (gname, bname, wproj, srcT) in enumerate((("gb", "bgb", w_pb, zT), ("ga", "bga", w_pa, hnT))):
            wg, wgk = wsmall(); load_w(wg, w_in[:, OFF[gname][0] + dt_ * 128:OFF[gname][0] + (dt_ + 1) * 128], wgk)
            wp, wpk = wsmall(); load_w(wp[:, 0:4, :], wproj[:, dt_ * 128:(dt_ + 1) * 128], wpk)
            for ci, (c0, n) in enumerate(TCH):
                pA = psum(); pB = psum()
                proj_fm(wg, wgk, c0, n, pA)
                for ct in range(4):
                    S.op("pe", lambda ct=ct: T.matmul(pB[:, 0:n], wp[:, ct, :], srcT[:, ct, c0:c0 + n], start=(ct == 0), stop=(ct == 3)), reads=wpk + [srcT], writes=[pB], skip_self=True)
                sg = sgt[ci % 2]
                S.op("act", lambda: A.activation(sg[:, 0:n], pA[:, 0:n], AF.Sigmoid, bias=cp(bname, dt_)), reads=[pA, cpk], writes=[sg])
                if br == 0:
                    S.op("dve", lambda: V.tensor_tensor(mergedT[:, dt_, c0:c0 + n], pB[:, 0:n], sg[:, 0:n], ALU.mult), reads=[pB, sg], writes=[mergedT])
                else:
                    S.op("dve", lambda: V.tensor_tensor(tmb[:, 0:n], pB[:, 0:n], sg[:, 0:n], ALU.mult), reads=[pB, sg], writes=[tmb])
                    S.op("dve", lambda: V.tensor_tensor(mergedT[:, dt_, c0:c0 + n], mergedT[:, dt_, c0:c0 + n], tmb[:, 0:n], ALU.add), reads=[mergedT, tmb], writes=[mergedT])
    tap("mergedT", mergedT[:, :, :].rearrange("p k t -> p (k t)"), [128, 8 * TF], mergedT)
    end_phase("p1c2", "zact", "hnT")

    x1 = sb("x1", [128, NFULL, D], F32, "x1", high=True)
    modG = MOD1G
    xt = [sb("xtd%d" % i, [128, D], F32, "p1d") for i in range(2)]
    tmpos = [sb("tmpo%d" % i, [128, 512], F32, "p1d") for i in range(2)]
    for gi in range(NFULL):
        src, which = chunk_src(gi)
        x_ = xt[gi % 2]
        S.dma("sp", x_[:], src, writes=[x_])
        for dc in range(2):
            p = psum()
            for kt in range(8):
                S.op("pe", lambda kt=kt: T.matmul(p[:], mergedT[:, kt, gi * 128:(gi + 1) * 128], wo[:, kt, dc * 512:(dc + 1) * 512], start=(kt == 0), stop=(kt == 7)),
                     reads=[mergedT, wo], writes=[p], skip_self=True)
            tmpo = tmpos[dc]
            S.op("dve", lambda: V.tensor_tensor(tmpo[:], p[:], modG[:, which, dc * 512:(dc + 1) * 512], ALU.mult), reads=[p, modG], writes=[tmpo], c=0.7)
            S.op("pool", lambda: P.tensor_tensor(x1[:, gi, dc * 512:(dc + 1) * 512], tmpo[:], x_[:, dc * 512:(dc + 1) * 512], ALU.add), reads=[tmpo, x_], writes=[("x1", gi)], c=1.3)
    tap("x1", x1[:, :, :].rearrange("p g e -> p (g e)"), [128, NFULL * D], ("x1", NFULL - 1))

    norm_phase(list(range(NFULL)), lambda gi, stage, i: (x1[:, gi, :], ("x1", gi)), lambda gi: 0 if gi < 4 else 1,
               lambda gi: hT[:, :, gi * 128:(gi + 1) * 128], lambda gi: ("hT", gi), lambda: None, "p2a", batched=False, layer=2)
    end_phase("p1d", "merged", "modG1", "p2a")

    actT = sb("actT", [128, NF, 1536], BF16, "actT")
    gP = [sb("gP%d" % i, [128, 2, 258], BF16, "p2b") for i in range(2)]
    gS = [sb("gS%d" % i, [128, 18, 66], BF16, "p2b") for i in range(2)]
    dg = [sb("dg%d" % i, [128, 9, 128], BF16, "p2b") for i in range(2)]
    gl = [sb("gl%d" % i, [128, 512], BF16, "p2b") for i in range(2)]
    for i in range(2):
        S.op("pool", lambda i=i: P.memset(gP[i][:], 0.0), writes=[gP[i]])
        S.op("pool", lambda i=i: P.memset(gS[i][:], 0.0), writes=[gS[i]])
    for f in range(NF):
        wgg, wggk = wsmall(); load_w(wgg, w_up[:, f * 128:(f + 1) * 128], wggk)
        wgv, wgvk = wsmall(); load_w(wgv, w_up[:, DFF + f * 128:DFF + (f + 1) * 128], wgvk)
        gp_, gs_, dg_ = gP[f % 2], gS[f % 2], dg[f % 2]
        for t9 in range(9):
            S.op("dve", lambda t9=t9: V.tensor_scalar(dg_[:, t9, :], ident_b[:], cp("ffnw", f * 9 + t9), None, ALU.mult), reads=[ident_b, cpk], writes=[dg_])
        p = psum(); proj_fm(wgg, wggk, 0, 512, p)
        S.op("act", lambda: A.copy(gp_[:, :, 1:257], p[:].rearrange("p (s t) -> p s t", s=2)), reads=[p], writes=[gp_])
        for r0, (c0, n) in ((0, (512, 512)), (8, (1024, 512)), (16, (1536, 64))):
            p = psum(); proj_fm(wgg, wggk, c0, n, p)
            nr = n // 64
            S.op("act", lambda p=p, r0=r0, nr=nr, n=n: A.copy(gs_[:, 1 + r0:1 + r0 + nr, 1:65], p[:, 0:n].rearrange("p (r t) -> p r t", r=nr)), reads=[p], writes=[gs_])
        outs = []
        pc = psum()
        for dc in range(3):
            S.op("pe", lambda dc=dc: T.matmul(pc[:].rearrange("p (s t) -> p s t", s=2), dg_[:, 3 + dc, :], gp_[:, :, dc:dc + 256], start=(dc == 0), stop=(dc == 2)),
                 reads=[dg_, gp_], writes=[pc], skip_self=True)
        outs.append((pc, 0))
        for half in range(2):
            pc = psum()
            for t9 in range(9):
                dr, dc = t9 // 3, t9 % 3
                S.op("pe", lambda t9=t9, dr=dr, dc=dc, pc=pc: T.matmul(pc[:].rearrange("p (r t) -> p r t", r=8), dg_[:, t9, :], gs_[:, half * 8 + dr:half * 8 + dr + 8, dc:dc + 64], start=(t9 == 0), stop=(t9 == 8)),
                     reads=[dg_, gs_], writes=[pc], skip_self=True)
            outs.append((pc, 512 + half * 512))
        for oi, (pc, c0) in enumerate(outs):
            pv = psum(); proj_fm(wgv, wgvk, c0, 512, pv)
            g_ = gl[oi % 2]
            S.op("act", lambda pc=pc, g_=g_: A.activation(g_[:], pc[:], AF.Gelu_apprx_tanh, bias=cp("ffnb", f)), reads=[pc, cpk], writes=[g_])
            S.op("dve", lambda pv=pv, g_=g_, c0=c0: V.tensor_tensor(actT[:, f, c0:c0 + 512], g_[:], pv[:], ALU.mult), reads=[g_, pv], writes=[("actT", f)])
    tap("actT", actT[:, :, :].rearrange("p f t -> p (f t)"), [128, NF * 1536], ("actT", NF - 1))
    wd00 = sb("wd0_0", [128, 11, 512], BF16, "p2c")
    S.dma("pool", wd00[:, :, :], w_down[0:11 * 128, 0:512].rearrange("(f p) n -> p f n", p=128), writes=[wd00])
    end_phase("p2b", "hT", "wb")

    wds = [[wd00, sb("wd0_1", [128, 11, 512], BF16, "p2c")], [sb("wd1_%d" % h_, [128, 11, 512], BF16, "p2c") for h_ in range(2)]]
    mod_alloc(["ngB"], "p2c")
    modG = MOD["modG"]; ngB = MOD["ngB"]
    load_ng("fng")
    tmpos2 = [sb("tmpo2_%d" % i, [128, 512], F32, "p2c") for i in range(2)]
    yo = [sb("yo%d" % i, [128, D], F32, "p2c") for i in range(2)]
    junk = sb("junk3", [128, D], BF16, "p2c")
    fsb = sb("fsb", [128, 32], F32, "p2c")

    def fin_a(gi):
        S.op("act", lambda: A.activation(junk[:], x1[:, gi, :], AF.Square, accum_out=fsb[:, gi:gi + 1]), reads=[("x1", gi)], writes=[junk, ("fsb", gi)])
        S.op("dve", lambda: V.tensor_scalar(fsb[:, 16 + gi:17 + gi], fsb[:, gi:gi + 1], 1.0 / D, EPS, ALU.mult, ALU.add), reads=[("fsb", gi)], writes=[("fsb", gi)])
        S.op("act", lambda: A.activation(fsb[:, 16 + gi:17 + gi], fsb[:, 16 + gi:17 + gi], AF.Ln), reads=[("fsb", gi)], writes=[("fsb", gi)])
        S.op("act", lambda: A.activation(fsb[:, 16 + gi:17 + gi], fsb[:, 16 + gi:17 + gi], AF.Exp, scale=-0.5), reads=[("fsb", gi)], writes=[("fsb", gi)])

    def fin_b(gi):
        y_ = yo[gi % 2]
        S.op("dve", lambda: V.scalar_tensor_tensor(y_[:], x1[:, gi, :], fsb[:, 16 + gi:17 + gi], ngB[:], ALU.mult, ALU.mult), reads=[("x1", gi), ("fsb", gi), ngB], writes=[y_])
        dst = yp[gi * 128:(gi + 1) * 128, :] if gi < 4 else ys[(gi - 4) * 128:(gi - 3) * 128, :]
        S.dma("sp", dst, y_[:], reads=[y_])

    for dc in range(2):
        wd = wds[dc]
        if dc == 1:
            S.dma("pool", wd[0][:, :, :], w_down[0:11 * 128, dc * 512:(dc + 1) * 512].rearrange("(f p) n -> p f n", p=128), writes=[wd[0]])
        S.dma("pool", wd[1][:, :, :], w_down[11 * 128:22 * 128, dc * 512:(dc + 1) * 512].rearrange("(f p) n -> p f n", p=128), writes=[wd[1]])
    for dc in range(2):
        wd = wds[dc]
        for gi in range(12):
            which = 0 if gi < 4 else 1
            p = psum()
            for f in range(NF):
                S.op("pe", lambda f=f: T.matmul(p[:], actT[:, f, gi * 128:(gi + 1) * 128], wd[f // 11][:, f % 11, :], start=(f == 0), stop=(f == NF - 1)),
                     reads=[wd[0], wd[1]], writes=[p], skip_self=True)
            tmpo = tmpos2[gi % 2]
            S.op("dve", lambda: V.tensor_tensor(tmpo[:], p[:], modG[:, which, dc * 512:(dc + 1) * 512], ALU.mult), reads=[p, modG], writes=[tmpo], c=0.7)
            S.op("pool", lambda: P.tensor_tensor(x1[:, gi, dc * 512:(dc + 1) * 512], x1[:, gi, dc * 512:(dc + 1) * 512], tmpo[:], ALU.add), reads=[tmpo, ("x1", gi)], writes=[("x1", gi)], c=1.3)
            if dc == 1:
                fin_a(gi)
                if gi >= 1:
                    fin_b(gi - 1)
    fin_b(11)
    S.end_defer()
    S.barrier()
    S.wait_everything("sp")
    top.close()
    S.close()
    return nc, dbg_out, S.n_inst


def _core_inputs(inp, c):
    j, half = c // 2, c % 2
    flip = half == 1
    f = (lambda a, ax: np.flip(a, axis=ax)) if flip else (lambda a, ax: a)
    m = {}
    m["xp"] = np.ascontiguousarray(f(inp["x_prompt"][2 * c:2 * c + 2], 1)).reshape(512, D)
    m["xs"] = np.ascontiguousarray(f(inp["x_sample"][j], 0))
    cvec = np.stack([inp["c_ctx"], inp["c"][j]], 0)
    m["ccol"] = np.ascontiguousarray(cvec.reshape(2, 8, 128).transpose(2, 0, 1).reshape(128, 16))
    dirs = [1, 0] if flip else [0, 1]
    C = inp["state_C"][j, 0][dirs].reshape(8, 128, 128); n = inp["state_n"][j, 0][dirs].reshape(8, 128)
    m["stCn"] = np.ascontiguousarray(np.concatenate([C.transpose(1, 0, 2), n.T[:, :, None]], axis=2))
    m["stm"] = np.ascontiguousarray(inp["state_m"][j, 0][dirs].reshape(1, 8))
    w_in = inp["w_in"][0]; b_in = inp["b_in"][0]
    if flip:
        perm = np.arange(IN_COLS)
        perm[GOFF:GOFF + 16] = np.concatenate([np.arange(GOFF + 8, GOFF + 16), np.arange(GOFF, GOFF + 8)])
        w_in = w_in[:, perm]; b_in = b_in[perm]
    m["w_in"] = np.ascontiguousarray(w_in)
    conv_w = f(inp["conv_dw_w"][0], 0)
    ffn_w = f(f(inp["ffn_dw_w"][0], 0), 1).reshape(9, DFF)
    colt = lambda v: v.reshape(-1, 128).T
    cols = [colt(b_in[OFF[k][0]:OFF[k][1]]) for k in ("q", "k", "o", "ua", "ub", "ga", "gb")]
    cols += [colt(inp[k][0]) for k in ("conv_dw_b", "conv_ln_g", "conv_ln_b", "mlstm_norm_g")]
    cols.append(conv_w.reshape(CW, 4, 128).transpose(2, 1, 0).reshape(128, 4 * CW))
    cols.append(ffn_w.reshape(9, NF, 128).transpose(2, 1, 0).reshape(128, NF * 9))
    cols.append(colt(inp["ffn_dw_b"][0]))
    cols += [colt(inp["b_ada"][0]), colt(inp["norm1_g"][0]), colt(inp["norm2_g"][0])]
    m["colpack"] = np.ascontiguousarray(np.concatenate(cols, axis=1).astype(np.float32))
    rows = [b_in[OFF["k"][0]:OFF["k"][1]], b_in[OFF["v"][0]:OFF["v"][1]], b_in[GOFF:GOFF + 16], inp["norm1_g"][0], inp["norm2_g"][0],
            inp["final_norm_g"], inp["b_ada"][0]]
    m["rowpack"] = np.ascontiguousarray(np.concatenate(rows)[None, :].astype(np.float32))
    return m


_SHARED = ("w_ada", "w_proj_a", "w_proj_b", "w_out", "w_up", "w_down")


def run(inputs, dbg=None, trace=False):
    inp = {k: np.asarray(v) for k, v in inputs.items()}
    nc, dbg_out, n_inst = build(dbg=dbg)
    shared = {k: np.ascontiguousarray(inp[k][0]) for k in _SHARED}
    in_maps = []
    for c in range(8):
        m = _core_inputs(inp, c)
        m.update(shared)
        in_maps.append(m)
    res = run_bass_kernel_spmd(nc, in_maps, core_ids=list(range(8)), **({"trace": True} if trace else {}))
    return res, dbg_out


def kernel(**inputs):
    res, _ = run(inputs)
    y_prompt = np.zeros((16, 256, D), np.float32); y_sample = np.zeros((4, 2048, D), np.float32)
    nC = np.zeros((16, 1, 2, NH, DH, DH), np.float32); nn = np.zeros((16, 1, 2, NH, DH), np.float32); nm = np.zeros((16, 1, 2, NH), np.float32)
    for c in range(8):
        r = res.results[c]
        j, half = c // 2, c % 2
        yp = r["yp"].reshape(2, 256, D); ys = r["ys"]
        oCn = r["oCn"].reshape(128, 2, 2, 4, 129); om = r["om"].reshape(2, 2, 4)
        if half:
            yp = yp[:, ::-1]; ys = ys[::-1]
            oCn = oCn[:, :, ::-1]; om = om[:, ::-1]
        y_prompt[2 * c:2 * c + 2] = yp
        y_sample[j, 1024 * half:1024 * (half + 1)] = ys
        for s in range(2):
            nC[2 * c + s, 0] = oCn[:, s, :, :, 0:128].transpose(1, 2, 0, 3)
            nn[2 * c + s, 0] = oCn[:, s, :, :, 128].transpose(1, 2, 0)
            nm[2 * c + s, 0] = om[s]
    return (y_prompt, y_sample, nC, nn, nm)
```

```python
import numpy as np
import types
from contextlib import ExitStack
import concourse.bass as bass
import concourse.mybir as mybir
from concourse.bass_utils import run_bass_kernel_spmd

F32 = mybir.dt.float32
BF16 = mybir.dt.bfloat16
AF = mybir.ActivationFunctionType
ALU = mybir.AluOpType
AX = mybir.AxisListType

D = 1024; DA = 512; NH = 4; DH = 128; L = 128; DB = 512; CW = 31; DFF = 2816; GW = 64
NF = DFF // 128
EPS = 1e-6
NEG = -1.0e30
IN_COLS = 4 * DA + 4 * NH + 2 * DB + 2 * D
OFF = {}
_o = 0
for _n, _s in [("q", 512), ("k", 512), ("v", 512), ("o", 512), ("i_f", 4), ("f_f", 4), ("i_b", 4), ("f_b", 4),
               ("ua", 512), ("ub", 512), ("ga", 1024), ("gb", 1024)]:
    OFF[_n] = (_o, _o + _s); _o += _s
GOFF = OFF["i_f"][0]

CP = {}
_o = 0
for _n, _s in [("bq", 4), ("bk", 4), ("bo", 4), ("bua", 4), ("bub", 4), ("bga", 8), ("bgb", 8), ("convb", 4), ("lng", 4),
               ("lnb", 4), ("mng", 4), ("convw", 4 * CW), ("ffnw", NF * 9), ("ffnb", NF), ("badac", 48), ("n1gc", 8), ("n2gc", 8)]:
    CP[_n] = _o; _o += _s
NCP = _o
RP = {}
_o = 0
for _n, _s in [("bk", 512), ("bv", 512), ("bg", 16), ("n1g", 1024), ("n2g", 1024), ("fng", 1024), ("bada", 6144)]:
    RP[_n] = _o; _o += _s
NRP = _o

NFULL = 13
NCH = 20
TF = NFULL * 128
TCH = [(0, 512), (512, 512), (1024, 512), (1536, 128)]


def _freeze(fn):
    if fn.__closure__ is None:
        return fn
    cells = []
    for c in fn.__closure__:
        try:
            cells.append(types.CellType(c.cell_contents))
        except ValueError:
            cells.append(c)
    return types.FunctionType(fn.__code__, fn.__globals__, fn.__name__, fn.__defaults__, tuple(cells))


class Sched:
    def __init__(self, nc, n_dma_sems=32):
        self.nc = nc
        self.eng = {"pe": nc.tensor, "act": nc.scalar, "dve": nc.vector, "pool": nc.gpsimd, "sp": nc.sync}
        self.sem = {}
        self.cnt = {}
        self._cms = []
        for e in ["pe", "act", "dve", "pool"]:
            cm = nc.semaphore("s_" + e)
            self.sem[e] = cm.__enter__()
            self._cms.append(cm)
            self.cnt[e] = 0
        self.dma_sems = []
        for i in range(n_dma_sems):
            cm = nc.semaphore("s_dma%d" % i)
            self.dma_sems.append([cm.__enter__(), 0])
            self._cms.append(cm)
        self.dma_rr = 0
        self.dma_rr_q = {}
        self.known = {e: {} for e in self.eng}
        self.lastw = {}
        self.reads = {}
        self.n_inst = 0
        self.inherit = {}
        self.alias = {}
        self.deferred = None

    def _inh(self, e, b):
        if not self.inherit:
            return
        if isinstance(b, tuple):
            nm = b[0]
        elif isinstance(b, str):
            nm = b
        else:
            nm = getattr(b, "name", None)
        nm = self.alias.get(nm, nm)
        evs = self.inherit.get(nm)
        if evs:
            for ev in evs:
                self._wait(e, ev)

    def close(self):
        for cm in reversed(self._cms):
            cm.__exit__(None, None, None)

    @staticmethod
    def _key(b):
        return b if isinstance(b, (str, tuple)) else id(b)

    def _wait(self, e, ev):
        key, semh, val = ev
        if self.known[e].get(key, 0) >= val:
            return
        self.eng[e].wait_ge(semh, val)
        self.known[e][key] = val
        self.n_inst += 1

    def _deps(self, e, reads, writes, skip_self=False):
        for b in reads:
            self._inh(e, b)
        for b in writes:
            self._inh(e, b)
        for b in reads:
            ev = self.lastw.get(self._key(b))
            if ev is not None and not (skip_self and ev[0] == e):
                self._wait(e, ev)
        for b in writes:
            k = self._key(b)
            ev = self.lastw.get(k)
            if ev is not None and not (skip_self and ev[0] == e):
                self._wait(e, ev)
            for ev in self.reads.get(k, ()):
                if not (skip_self and ev[0] == e):
                    self._wait(e, ev)

    def _record(self, ev, reads, writes):
        for b in reads:
            self.reads.setdefault(self._key(b), []).append(ev)
        for b in writes:
            k = self._key(b)
            self.lastw[k] = ev
            self.reads[k] = []

    def begin_defer(self):
        assert self.deferred is None
        self.deferred = []

    def end_defer(self):
        ops = self.deferred
        self.deferred = None
        n = len(ops)
        lastw = {}; readers = {}
        deps = [set() for _ in range(n)]
        for i, o in enumerate(ops):
            rk = [self._key(b) for b in o["reads"]]; wk = [self._key(b) for b in o["writes"]]
            for k in rk:
                if k in lastw:
                    deps[i].add(lastw[k])
            for k in wk:
                if k in lastw:
                    deps[i].add(lastw[k])
                for r in readers.get(k, ()):
                    deps[i].add(r)
            for k in rk:
                readers.setdefault(k, []).append(i)
            for k in wk:
                lastw[k] = i
                readers[k] = []
            deps[i].discard(i)
        users = [[] for _ in range(n)]
        ndep = [len(d) for d in deps]
        for i, d in enumerate(deps):
            for j in d:
                users[j].append(i)
        blevel = [0.0] * n
        for i in range(n - 1, -1, -1):
            m_ = 0.0
            for u in users[i]:
                if blevel[u] > m_:
                    m_ = blevel[u]
            blevel[i] = ops[i]["cost"] + 0.3 + m_
        finish = [0.0] * n
        etime = {}
        ready = [i for i in range(n) if ndep[i] == 0]
        order = []
        while ready:
            sts = {}
            bs0 = None
            for i in ready:
                o = ops[i]
                st = etime.get(o["e"], 0.0)
                for j in deps[i]:
                    lat = 0.15 if ops[j]["e"] == o["e"] else 0.9
                    st = max(st, finish[j] + lat)
                sts[i] = st
                if bs0 is None or st < bs0:
                    bs0 = st
            best = None
            for i in ready:
                if sts[i] <= bs0 + 0.25:
                    if best is None or blevel[i] > blevel[best] + 1e-9 or (abs(blevel[i] - blevel[best]) <= 1e-9 and i < best):
                        best = i
            bs = sts[best]
            o = ops[best]
            ready.remove(best)
            if o["kind"] == "dma":
                etime[o["e"]] = bs + (0.9 if o["e"] == "pool" else 0.2)
                finish[best] = bs + o["cost"]
            else:
                etime[o["e"]] = bs + o["cost"]
                finish[best] = bs + o["cost"] + (0.15 if o["e"] == "pe" else 0.05)
            order.append(best)
            for u in users[best]:
                ndep[u] -= 1
                if ndep[u] == 0:
                    ready.append(u)
        assert len(order) == n
        for i in order:
            o = ops[i]
            if o["kind"] == "dma":
                self.dma(o["e"], o["out"], o["in_"], reads=o["reads"], writes=o["writes"], **o["kw"])
            else:
                self.op(o["e"], o["fn"], reads=o["reads"], writes=o["writes"], skip_self=o["skip_self"])
        return max(finish) if n else 0.0

    def op(self, e, fn, reads=(), writes=(), skip_self=False, c=None):
        if self.deferred is not None:
            self.deferred.append(dict(kind="op", e=e, fn=_freeze(fn), reads=list(reads), writes=list(writes), skip_self=skip_self,
                                      cost=(c if c is not None else {"pe": 0.2, "pool": 1.0}.get(e, 0.5))))
            return None
        self._deps(e, reads, writes, skip_self)
        inst = fn()
        self.cnt[e] += 1
        inst.then_inc(self.sem[e], 1)
        ev = (e, self.sem[e], self.cnt[e])
        self._record(ev, reads, writes)
        self.n_inst += 1
        return ev

    def dma(self, q, out, in_, reads=(), writes=(), **kw):
        if self.deferred is not None:
            self.deferred.append(dict(kind="dma", e=q, out=out, in_=in_, reads=list(reads), writes=list(writes), kw=kw, cost=2.5))
            return None
        half = len(self.dma_sems) // 2
        base = 0 if q == "sp" else half
        rr = self.dma_rr_q.get(q, 0)
        idx = base + rr
        self.dma_rr_q[q] = (rr + 1) % half
        slot = self.dma_sems[idx]
        key = "dma%d" % idx
        if slot[1] > 0:
            self._wait(q, (key, slot[0], slot[1]))
        self._deps(q, reads, writes)
        inst = self.eng[q].dma_start(out=out, in_=in_, **kw)
        slot[1] += 16
        inst.then_inc(slot[0], 16)
        ev = (key, slot[0], slot[1])
        self._record(ev, reads, writes)
        self.n_inst += 1
        return ev

    def _all_events(self):
        evs = [(e, self.sem[e], self.cnt[e]) for e in self.cnt if self.cnt[e] > 0]
        for i, (s, v) in enumerate(self.dma_sems):
            if v > 0:
                evs.append(("dma%d" % i, s, v))
        return evs

    def barrier(self):
        evs = self._all_events()
        for e in ["pe", "act", "dve", "pool", "sp"]:
            for ev in evs:
                if ev[0] != e or e != "pe":
                    self._wait(e, ev)
        self.lastw = {}
        self.reads = {}

    def wait_everything(self, e):
        for ev in self._all_events():
            self._wait(e, ev)


class Tl:
    def __init__(self, ap, name, rng, group):
        self.ap = ap; self.name = name; self.rng = rng; self.group = group

    def __getitem__(self, idx):
        return self.ap[idx]


class Arena:
    def __init__(self, nc, stack, kbytes=207):
        self.total = kbytes * 1024
        self.t = stack.enter_context(nc.sbuf_tensor("arena", [128, self.total // 2], BF16))
        self.free = [(0, self.total)]
        self.live = []
        self.freed = []
        self.sched = None

    def alloc(self, name, shape, dt=F32, group="top", high=False):
        esz = 4 if dt == F32 else 2
        n = 1
        for d_ in shape[1:]:
            n *= d_
        nbytes = (n * esz + 63) // 64 * 64
        order = range(len(self.free) - 1, -1, -1) if high else range(len(self.free))
        for i in order:
            o, sz = self.free[i]
            if sz >= nbytes:
                if high:
                    self.free[i] = (o, sz - nbytes)
                    o = o + sz - nbytes
                else:
                    self.free[i] = (o + nbytes, sz - nbytes)
                break
        else:
            raise MemoryError("arena full allocating %s (%d B); free=%s" % (name, nbytes, self.free))
        ap = self.t[0:shape[0], o // 2:(o + n * esz) // 2]
        if dt == F32:
            ap = ap.bitcast(F32)
        if len(shape) == 3:
            ap = ap.rearrange("p (a b) -> p a b", a=shape[1])
        elif len(shape) == 4:
            ap = ap.rearrange("p (a b c) -> p a b c", a=shape[1], b=shape[2])
        tl = Tl(ap, name, (o, nbytes), group)
        self.live.append(tl)
        evs = {}
        for (fo, fsz, fe) in self.freed:
            if fo < o + nbytes and o < fo + fsz:
                for ev in fe:
                    if ev[0] not in evs or evs[ev[0]][2] < ev[2]:
                        evs[ev[0]] = ev
        if self.sched is not None:
            if evs:
                self.sched.inherit[name] = list(evs.values())
            else:
                self.sched.inherit.pop(name, None)
        return tl

    def free_group(self, group):
        keep = []
        snap = self.sched._all_events() if self.sched is not None else []
        for tl in self.live:
            if tl.group == group:
                self.free.append(tl.rng)
                self.freed.append((tl.rng[0], tl.rng[1], snap))
            else:
                keep.append(tl)
        self.live = keep
        self.free.sort()
        merged = []
        for o, sz in self.free:
            if sz == 0:
                continue
            if merged and merged[-1][0] + merged[-1][1] == o:
                merged[-1] = (merged[-1][0], merged[-1][1] + sz)
            else:
                merged.append((o, sz))
        self.free = merged


def build(dbg=None, stop_after=None):
    nc = bass.Bass("TRN2", target_bir_lowering=False)
    di = lambda n, s: nc.dram_tensor(n, s, F32, kind="ExternalInput").ap()
    do = lambda n, s: nc.dram_tensor(n, s, F32, kind="ExternalOutput").ap()
    xp = di("xp", [512, D]); xs = di("xs", [2048, D]); ccol = di("ccol", [128, 16])
    stCn = di("stCn", [128, 8, 129]); stm = di("stm", [1, 8])
    colpack = di("colpack", [128, NCP]); rowpack = di("rowpack", [1, NRP])
    w_ada = di("w_ada", [D, 6 * D]); w_in = di("w_in", [D, IN_COLS]); w_pa = di("w_proj_a", [DA, D]); w_pb = di("w_proj_b", [DB, D])
    w_out = di("w_out", [D, D]); w_up = di("w_up", [D, 2 * DFF]); w_down = di("w_down", [DFF, D])
    yp = do("yp", [512, D]); ys = do("ys", [1024, D]); oCn = do("oCn", [128, 16, 129]); om = do("om", [1, 16])
    dbg_out = {}

    S = Sched(nc)
    V, A, P, T = nc.vector, nc.scalar, nc.gpsimd, nc.tensor
    top = ExitStack()
    AR = Arena(nc, top)
    AR.sched = S
    sb = AR.alloc

    S.alias.update({"SSb": "SS", "den": None})

    def tap(name, ap, shape, key):
        if dbg is None or name not in dbg:
            return
        o = nc.dram_tensor("dbg_" + name, list(shape), F32, kind="ExternalOutput").ap()
        S.dma("pool", o, ap, reads=[key])
        dbg_out[name] = shape

    def end_phase(*groups):
        S.end_defer()
        for g in groups:
            AR.free_group(g)
        S.begin_defer()

    ps = [top.enter_context(nc.psum_tensor("ps%d" % i, [128, 512], F32)) for i in range(8)]
    ps_rr = [0]

    def psum():
        p = ps[ps_rr[0]]
        ps_rr[0] = (ps_rr[0] + 1) % 8
        return p

    def psum7():
        p = ps[ps_rr[0] % 7]
        ps_rr[0] = (ps_rr[0] + 1) % 7
        return p

    ident_f = sb("ident_f", [128, 128]); ident_b = sb("ident_b", [128, 128], BF16)
    U01 = sb("U01", [128, 128]); L01 = sb("L01", [128, 128]); maskU = sb("maskU", [128, 128]); maskL = sb("maskL", [128, 128])
    ones_f = sb("ones_f", [128, 128]); ones_b = sb("ones_b", [128, 128], BF16)
    cpk = sb("cpk", [128, NCP]); bkB = sb("bkB", [128, 512]); bvB = sb("bvB", [128, 512]); bgB = sb("bgB", [128, 16])
    csil = sb("csil", [128, 16], BF16); ccs = sb("ccs", [128, 16])
    bada_b = sb("bada_b", [1, 512], BF16)
    sm = sb("sm", [128, 64])
    hT = sb("hT", [128, 8, TF], BF16, "hT")
    WBS = {}

    def wb_alloc(nslots, group):
        WBS["t"] = sb("wbs_" + group, [128, nslots * 1024], BF16, group)
        WBS["n"] = nslots; WBS["s"] = 0; WBS["b"] = 0; WBS["g"] = group
        S.alias["wbs"] = "wbs_" + group

    def wsmall():
        i = WBS["s"] % WBS["n"]; WBS["s"] += 1
        return WBS["t"][:, i * 1024:(i + 1) * 1024].rearrange("p (k n) -> p k n", k=8), [("wbs", WBS["g"], i)]

    def wbig():
        nb = WBS["n"] // 4
        j = WBS["b"] % nb; WBS["b"] += 1
        return WBS["t"][:, j * 4096:(j + 1) * 4096].rearrange("p (k n) -> p k n", k=8), [("wbs", WBS["g"], 4 * j + i) for i in range(4)]

    wb_alloc(8, "wb")

    def cp(name, idx=0, n=1):
        return cpk[:, CP[name] + idx:CP[name] + idx + n]

    for t_, val in [(ident_f, 1.0), (U01, 1.0), (L01, 1.0), (maskU, 0.0), (maskL, 0.0), (ones_f, 1.0)]:
        S.op("pool", lambda t_=t_, val=val: P.memset(t_[:], val), writes=[t_])
    S.op("pool", lambda: P.memset(ones_b[:], 1.0), writes=[ones_b])
    S.op("pool", lambda: P.affine_select(ident_f[:], ident_f[:], [[1, 128]], ALU.is_equal, 0.0, base=0, channel_multiplier=-1), reads=[ident_f], writes=[ident_f])
    S.op("pool", lambda: P.affine_select(U01[:], U01[:], [[1, 128]], ALU.is_ge, 0.0, base=0, channel_multiplier=-1), reads=[U01], writes=[U01])
    S.op("pool", lambda: P.affine_select(L01[:], L01[:], [[-1, 128]], ALU.is_ge, 0.0, base=0, channel_multiplier=1), reads=[L01], writes=[L01])
    S.op("pool", lambda: P.affine_select(maskU[:], maskU[:], [[1, 128]], ALU.is_ge, NEG, base=0, channel_multiplier=-1), reads=[maskU], writes=[maskU])
    S.op("pool", lambda: P.affine_select(maskL[:], maskL[:], [[-1, 128]], ALU.is_ge, NEG, base=0, channel_multiplier=1), reads=[maskL], writes=[maskL])
    S.op("dve", lambda: V.tensor_copy(ident_b[:], ident_f[:]), reads=[ident_f], writes=[ident_b])
    S.dma("sp", cpk[:], colpack, writes=[cpk])
    S.dma("sp", bkB[:], rowpack[:, RP["bk"]:RP["bk"] + 512].to_broadcast([128, 512]), writes=[bkB])
    S.dma("sp", bvB[:], rowpack[:, RP["bv"]:RP["bv"] + 512].to_broadcast([128, 512]), writes=[bvB])
    S.dma("sp", bgB[:], rowpack[:, RP["bg"]:RP["bg"] + 16].to_broadcast([128, 16]), writes=[bgB])
    S.dma("sp", ccs[:], ccol, writes=[ccs])
    S.op("act", lambda: A.activation(csil[:], ccs[:], AF.Silu), reads=[ccs], writes=[csil])

    def load_w(dst3, src2, key):
        S.dma("pool", dst3, src2.rearrange("(kt p) n -> p kt n", p=128), writes=(key if isinstance(key, list) else [key]))

    MOD = {}

    def mod_alloc(names, group):
        for n_ in names:
            MOD[n_] = sb(n_, [128, D] if n_ == "ngB" else [128, 2, D], F32, group, high=group.startswith("mod2"))

    last_wada = [None]

    def mod_prefetch(col0, w=None, wk_=None, bb=None):
        if w is None:
            w, wk_ = wbig(); bb = bada_b
        last_wada[0] = wk_
        load_w(w, w_ada[:, col0:col0 + 512], wk_)
        S.dma("pool", bb[:], rowpack[:, RP["bada"] + col0:RP["bada"] + col0 + 512], writes=[bb])
        return (w, wk_, bb)

    def mod_chunk(col0, evac, pre=None):
        w, wk_, bb = pre if pre is not None else mod_prefetch(col0)
        for which in range(2):
            p = psum()
            for kt in range(8):
                S.op("pe", lambda kt=kt, which=which, p=p: T.matmul(p[:], csil[:, which * 8 + kt:which * 8 + kt + 1].to_broadcast([128, 128]), w[:, kt, :], start=(kt == 0), stop=False),
                     reads=[csil] + wk_, writes=[p], skip_self=True)
            S.op("pe", lambda p=p: T.matmul(p[:], ones_b[0:1, :], bb[0:1, :], start=False, stop=True), reads=[ones_b, bb], writes=[p], skip_self=True)
            evac(which, p)

    MODC = sb("MODC", [128, 2, 48])
    GCs = {1: sb("GC1", [128, 2, 8]), 2: sb("GC2", [128, 2, 8])}
    csil_v = csil[:, :].rearrange("p (w k) -> p k w", w=2)

    def mod_cols(col0, pre=None):
        w, wk_, bb = pre if pre is not None else mod_prefetch(col0)
        for jj in range(4):
            j = col0 // 128 + jj
            p = psum()
            for kt in range(8):
                S.op("pe", lambda kt=kt: T.matmul(p[:, 0:2], w[:, kt, jj * 128:(jj + 1) * 128], csil_v[:, kt, :], start=(kt == 0), stop=(kt == 7)),
                     reads=[csil] + wk_, writes=[p], skip_self=True, c=0.08)
            S.op("dve", lambda: V.tensor_scalar(MODC[:, :, j], p[:, 0:2], cp("badac", j), None, ALU.add), reads=[p, cpk], writes=[("MODC", j)], c=0.12)

    def mod_gain(layer):
        base = 0 if layer == 1 else 24
        gc = GCs[layer]
        S.op("dve", lambda: V.tensor_scalar(gc[:], MODC[:, :, base + 8:base + 16], 1.0, None, ALU.add), reads=[("MODC", base + 8 + i) for i in range(8)], writes=[gc], c=0.12)
        S.op("dve", lambda: V.tensor_tensor(gc[:], gc[:], cp("n1gc" if layer == 1 else "n2gc", 0, 8).unsqueeze(1).to_broadcast([128, 2, 8]), ALU.mult), reads=[gc, cpk], writes=[gc], c=0.12)

    def load_ng(name):
        ngB = MOD["ngB"]
        S.dma("sp", ngB[:], rowpack[:, RP[name]:RP[name] + D].to_broadcast([128, D]), writes=[ngB])

    def mod_shift_scale(base_col):
        modA, modB, ngB = MOD["modA"], MOD["modB"], MOD["ngB"]
        for half in range(2):
            c0 = half * 512
            mod_chunk(base_col + c0, lambda which, p, c0=c0: S.op("act", lambda: A.copy(modB[:, which, c0:c0 + 512], p[:]), reads=[p], writes=[modB]))
            mod_chunk(base_col + D + c0, lambda which, p, c0=c0: S.op("dve", lambda: V.scalar_tensor_tensor(modA[:, which, c0:c0 + 512], p[:], 1.0, ngB[:, c0:c0 + 512], ALU.add, ALU.mult), reads=[p, ngB], writes=[modA]))

    def mod_gate(base_col, pres=None):
        modG = MOD["modG"]
        for half in range(2):
            c0 = half * 512
            mod_chunk(base_col + c0, lambda which, p, c0=c0: S.op("act", lambda: A.copy(modG[:, which, c0:c0 + 512], p[:]), reads=[p], writes=[modG]),
                      pre=(pres[half] if pres else None))

    def rsqrt_cols(dst, src, scale):
        S.op("dve", lambda: V.tensor_scalar(dst, src, scale, EPS, ALU.mult, ALU.add), reads=[sm], writes=[sm])
        S.op("act", lambda: A.activation(dst, dst, AF.Ln), reads=[sm], writes=[sm])
        S.op("act", lambda: A.activation(dst, dst, AF.Exp, scale=-0.5), reads=[sm], writes=[sm])

    def chunk_src(gi):
        return (xp[gi * 128:(gi + 1) * 128, :], 0) if gi < 4 else (xs[(gi - 4) * 128:(gi - 3) * 128, :], 1)

    def norm_phase(chunks, get_x, which_of, dstT_of, dkey_of, mod_thunk, group, batched=True, layer=1):
        n_ = len(chunks)
        ssb = sb("ssb_" + group, [128, 64], F32, group)
        S.alias["ssb"] = "ssb_" + group
        junk = sb("junk_" + group, [128, D], BF16, group)
        xns = [sb("xn%d_" % i + group, [128, D], F32, group) for i in range(2)]
        if not batched:
            mod_thunk()
        for i, gi in enumerate(chunks):
            xin, xkey = get_x(gi, 0, i)
            S.op("act", lambda: A.activation(junk[:], xin, AF.Square, accum_out=ssb[:, i:i + 1]), reads=[xkey], writes=[junk, ("ssb", i)], c=0.6)
            if not batched:
                S.op("dve", lambda: V.tensor_scalar(ssb[:, 32 + i:33 + i], ssb[:, i:i + 1], 1.0 / D, EPS, ALU.mult, ALU.add), reads=[("ssb", i)], writes=[("ssb", 32 + i)], c=0.15)
                S.op("act", lambda: A.activation(ssb[:, 32 + i:33 + i], ssb[:, 32 + i:33 + i], AF.Ln), reads=[("ssb", 32 + i)], writes=[("ssb", 32 + i)], c=0.2)
                S.op("act", lambda: A.activation(ssb[:, 32 + i:33 + i], ssb[:, 32 + i:33 + i], AF.Exp, scale=-0.5), reads=[("ssb", 32 + i)], writes=[("ssb", 32 + i)], c=0.2)
        if batched:
            mod_thunk()
        if batched:
            S.op("dve", lambda: V.tensor_scalar(ssb[:, 32:32 + n_], ssb[:, 0:n_], 1.0 / D, EPS, ALU.mult, ALU.add), reads=[("ssb", i) for i in range(n_)], writes=[ssb])
            S.op("act", lambda: A.activation(ssb[:, 32:32 + n_], ssb[:, 32:32 + n_], AF.Ln), reads=[ssb], writes=[ssb])
            S.op("act", lambda: A.activation(ssb[:, 32:32 + n_], ssb[:, 32:32 + n_], AF.Exp, scale=-0.5), reads=[ssb], writes=[ssb])
        gc = GCs[layer]; base = 0 if layer == 1 else 24
        shk = [("MODC", base + i) for i in range(8)]
        for i, gi in enumerate(chunks):
            xin, xkey = get_x(gi, 1, i)
            which = which_of(gi)
            xn = xns[i % 2]
            S.op("dve", lambda: V.tensor_scalar(xn[:], xin, ssb[:, 32 + i:33 + i], None, ALU.mult), reads=[xkey, (ssb if batched else ("ssb", 32 + i))], writes=[xn], c=1.1)
            pp = [psum(), psum()]
            for kt in range(8):
                S.op("pe", lambda kt=kt: T.transpose(pp[kt // 4][:, (kt % 4) * 128:(kt % 4 + 1) * 128], xn[:, kt * 128:(kt + 1) * 128], ident_f[:]), reads=[xn, ident_f], writes=[pp[kt // 4]], skip_self=True, c=0.15)
            dst = dstT_of(gi)
            for kt in range(8):
                src_ = pp[kt // 4][:, (kt % 4) * 128:(kt % 4 + 1) * 128]
                if kt in (1, 4, 6):
                    S.op("dve", lambda kt=kt, src_=src_: V.scalar_tensor_tensor(dst[:, kt, :], src_, gc[:, which, kt:kt + 1], MODC[:, which, base + kt:base + kt + 1].to_broadcast([128, 128]), ALU.mult, ALU.add),
                         reads=[pp[kt // 4], gc] + shk, writes=[dkey_of(gi)], c=0.25)
                else:
                    S.op("act", lambda kt=kt, src_=src_: A.activation(dst[:, kt, :], src_, AF.Identity, bias=MODC[:, which, base + kt:base + kt + 1], scale=gc[:, which, kt:kt + 1]),
                         reads=[pp[kt // 4], gc] + shk, writes=[dkey_of(gi)], c=0.28)

    S.begin_defer()
    hTf = sb("hTf", [128, 8, 7 * 128], BF16, "hTf0")

    def hT_chunk(gi):
        return hT[:, :, gi * 128:(gi + 1) * 128] if gi < NFULL else hTf[:, :, (gi - NFULL) * 128:(gi - NFULL + 1) * 128]

    xall = sb("xall", [128, NCH, D], F32, "p1a")

    def get_x_1a(gi, stage, i):
        if stage == 0:
            src, _ = chunk_src(gi)
            S.dma("sp", xall[:, gi, :], src, reads=(last_wada[0] if (gi >= 2 and last_wada[0]) else []), writes=[("xall", gi)])
        return xall[:, gi, :], ("xall", gi)

    def mod_1a():
        for c_ in range(4):
            mod_cols(c_ * 512)
        mod_gain(1)

    norm_phase(list(range(NCH)), get_x_1a, lambda gi: 0 if gi < 4 else 1, hT_chunk, lambda gi: ("hT", gi), mod_1a, "p1a", batched=False)
    tap("hT", hT[:, :, :].rearrange("p k t -> p (k t)"), [128, 8 * TF], ("hT", 0))
    k_tok = sb("k_tok", [128, NCH, 512], BF16, "kv")
    v_tok = sb("v_tok", [128, NCH, 4, 129], BF16, "kv")
    g_tok = sb("g_tok", [128, NCH, 16], F32, "kv")
    wg_ = sb("wg_", [128, 8, 16], BF16, "kv")
    S.op("pool", lambda: P.memset(v_tok[:], 1.0), writes=[v_tok])
    wk, wkk = wbig(); load_w(wk, w_in[:, OFF["k"][0]:OFF["k"][1]], wkk)
    wv, wvk = wbig(); load_w(wv, w_in[:, OFF["v"][0]:OFF["v"][1]], wvk)
    load_w(wg_[:], w_in[:, GOFF:GOFF + 16], wg_)

    def proj_tm(gi, wt, wkeys, ncol, p):
        hc = hT_chunk(gi)
        for kt in range(8):
            S.op("pe", lambda kt=kt: T.matmul(p[:, 0:ncol], hc[:, kt, :], wt[:, kt, 0:ncol], start=(kt == 0), stop=(kt == 7)), reads=[("hT", gi)] + wkeys, writes=[p], skip_self=True, c=ncol / 2300.0 + 0.04)

    for gi in range(NCH):
        p1 = psum(); proj_tm(gi, wk, wkk, 512, p1)
        S.op("dve", lambda: V.tensor_tensor(k_tok[:, gi, :], p1[:], bkB[:], ALU.add), reads=[p1, bkB], writes=[("k_tok", gi)], c=0.7)
        p2 = psum(); proj_tm(gi, wv, wvk, 512, p2)
        S.op("dve", lambda: V.tensor_tensor(v_tok[:, gi, :, 0:128], p2[:].rearrange("p (h e) -> p h e", h=4), bvB[:].rearrange("p (h e) -> p h e", h=4), ALU.add),
             reads=[p2, bvB, v_tok], writes=[("v_tok", gi)], c=0.7)
        p3 = psum(); proj_tm(gi, wg_, [wg_], 16, p3)
        S.op("dve", lambda: V.tensor_tensor(g_tok[:, gi, :], p3[:, 0:16], bgB[:], ALU.add), reads=[p3, bgB], writes=[("g_tok", gi)], c=0.15)
    tap("k_tok", k_tok[:, :, :].rearrange("p g e -> p (g e)"), [128, NCH * 512], ("k_tok", NCH - 1))
    tap("g_tok", g_tok[:, :, :].rearrange("p g e -> p (g e)"), [128, NCH * 16], ("g_tok", NCH - 1))

    end_phase("p1a", "hTf0")

    def hT_keys_for(c0, n):
        return [("hT", g) for g in range(c0 // 128, (c0 + n + 127) // 128)]

    def proj_fm(wtile, wkey, c0, n, p):
        for kt in range(8):
            S.op("pe", lambda kt=kt: T.matmul(p[:, 0:n], wtile[:, kt, :], hT[:, kt, c0:c0 + n], start=(kt == 0), stop=(kt == 7)),
                 reads=(wkey if isinstance(wkey, list) else [wkey]) + hT_keys_for(c0, n), writes=[p], skip_self=True, c=n / 2300.0 + 0.03)

    PREP = sb("PREP", [128, NCH, 40], F32, "kv")
    sc = sb("sc", [128, 96], F32, "kv")
    scall = sb("scall", [128, NCH, 16], F32, "p1b")
    g4 = g_tok[:, :, :].rearrange("p g (d x) -> p g d x", d=2)
    S.op("act", lambda: A.activation(scall[:, :, 0:8].rearrange("p g (d x) -> p g d x", d=2), g4[:, :, :, 4:8], AF.Exp, scale=-1.0), reads=[("g_tok", NCH - 1)], writes=[scall])
    S.op("act", lambda: A.activation(scall[:, :, 8:16], scall[:, :, 0:8], AF.Ln, bias=1.0), reads=[scall], writes=[scall])
    pcm = psum()
    S.op("pe", lambda: T.matmul(pcm[:, 0:80], U01[:], scall[:, :, 8:12], start=True, stop=True), reads=[U01, scall], writes=[pcm], skip_self=True)
    S.op("pe", lambda: T.matmul(pcm[:, 80:160], L01[:], scall[:, :, 12:16], start=True, stop=True), reads=[L01, scall], writes=[pcm], skip_self=True)
    S.op("pe", lambda: T.matmul(pcm[:, 160:320], ones_f[:], scall[:, :, 8:16], start=True, stop=True), reads=[ones_f, scall], writes=[pcm], skip_self=True)
    for d in range(2):
        pv_ = pcm[:, d * 80:(d + 1) * 80].rearrange("p (g x) -> p g x", x=4)
        S.op("dve", lambda d=d, pv_=pv_: V.tensor_tensor(PREP[:, :, d * 4:d * 4 + 4], g_tok[:, :, d * 8:d * 8 + 4], pv_, ALU.add), reads=[("g_tok", NCH - 1), pcm], writes=[("PREP", 0)])
        S.op("act", lambda d=d, pv_=pv_: A.copy(PREP[:, :, 8 + d * 4:12 + d * 4], pv_), reads=[pcm], writes=[("PREP", 0)])
    S.op("act", lambda: A.copy(PREP[:, :, 32:40], pcm[:, 160:320].rearrange("p (g x) -> p g x", x=8)), reads=[pcm], writes=[("PREP", 0)])
    for gi in range(NCH):
        for d in range(2):
            if gi >= NFULL and d == 0:
                continue
            pk = ("PREP", gi + 1)
            pa_ = psum()
            for hd in range(4):
                j = d * 4 + hd
                S.op("pe", lambda j=j, hd=hd: T.matmul(pa_[:, hd * 128:(hd + 1) * 128], PREP[:, gi, j:j + 1].to_broadcast([128, 128]), ident_f[:], start=True, stop=True),
                     reads=[("PREP", 0), ident_f], writes=[pa_], skip_self=True)
            pa3 = pa_[:].rearrange("p (h s) -> p h s", h=4)
            S.op("dve", lambda: V.reduce_max(PREP[:, gi, 24 + d * 4:28 + d * 4], pa3, axis=AX.X), reads=[pa_], writes=[pk])
    PK = lambda gi: [("PREP", 0), ("PREP", gi + 1)]
    tap("PREP", PREP[:, :, :].rearrange("p g e -> p (g e)"), [128, NCH * 40], ("PREP", NCH))

    zT = sb("zT", [128, 4, TF], BF16, "zact")
    GP = 286 * 2 + 1182
    SEG0 = [15, 301, 587]
    gluT = sb("gluT", [128, 4, GP], BF16, "p1b")
    zsqT = sb("zsqT", [128, 4, TF], BF16, "p1b")
    diag = sb("diag", [128, 4 * CW, 128], BF16, "p1b")
    sgt = [sb("sgt%d" % i, [128, 512], BF16, "p1b") for i in range(2)]
    lnm = [sb("lnm%d" % i, [128, 512], F32, "p1b") for i in range(2)]; lnr = [sb("lnr%d" % i, [128, 512], F32, "p1b") for i in range(2)]
    lnt = [sb("lnt%d" % i, [128, 512], F32, "p1b") for i in range(2)]
    S.op("pool", lambda: P.memset(gluT[:], 0.0), writes=[gluT])
    for ct in range(4):
        for tap_ in range(CW):
            if tap_ % 2 == 0:
                S.op("dve", lambda tap_=tap_: V.tensor_scalar(diag[:, ct * CW + tap_, :], ident_b[:], cp("convw", ct * CW + tap_), None, ALU.mult), reads=[ident_b, cpk], writes=[("diag", ct, 0)], c=0.29)
            else:
                S.op("act", lambda tap_=tap_: A.activation(diag[:, ct * CW + tap_, :], ident_b[:], AF.Copy, scale=cp("convw", ct * CW + tap_)), reads=[ident_b, cpk], writes=[("diag", ct, 1)], c=0.25)
    for ct in range(4):
        wa, wak = wsmall(); load_w(wa, w_in[:, OFF["ua"][0] + ct * 128:OFF["ua"][0] + (ct + 1) * 128], wak)
        wu, wuk = wsmall(); load_w(wu, w_in[:, OFF["ub"][0] + ct * 128:OFF["ub"][0] + (ct + 1) * 128], wuk)
        for ci, (c0, n) in enumerate(TCH):
            pA = psum(); pB = psum()
            proj_fm(wa, wak, c0, n, pA)
            proj_fm(wu, wuk, c0, n, pB)
            sg = sgt[ci % 2]
            S.op("act", lambda: A.activation(sg[:, 0:n], pB[:, 0:n], AF.Sigmoid, bias=cp("bub", ct)), reads=[pB, cpk], writes=[sg], c=0.6)
            pieces = [(0, 256, SEG0[0]), (256, 256, SEG0[1])] if c0 == 0 else [(0, n, SEG0[2] + c0 - 512)]
            for (o_, n_, d0) in pieces:
                S.op("dve", lambda o_=o_, n_=n_, d0=d0: V.scalar_tensor_tensor(gluT[:, ct, d0:d0 + n_], pA[:, o_:o_ + n_], cp("bua", ct), sg[:, o_:o_ + n_], ALU.add, ALU.mult),
                     reads=[pA, sg, cpk, gluT], writes=[("gluT", ct)], c=0.7)
    tapped = False
    cchunks = [("P", 0, 512), ("S", 512, 512), ("S", 1024, 512), ("S", 1536, 128)]
    for cj, (kind, c0, n) in enumerate(cchunks):
        for ct in range(4):
            p = psum()
            for tap_ in range(CW):
                if kind == "P":
                    rhs = gluT[:, ct, 0:572].rearrange("p (s x) -> p s x", s=2)[:, :, tap_:tap_ + 256]
                    out_ = p[:, 0:512].rearrange("p (s x) -> p s x", s=2)
                else:
                    ds = SEG0[2] + c0 - 512
                    rhs = gluT[:, ct, ds + tap_ - 15:ds + tap_ - 15 + n]
                    out_ = p[:, 0:n]
                S.op("pe", lambda tap_=tap_, rhs=rhs, out_=out_: T.matmul(out_, diag[:, ct * CW + tap_, :], rhs, start=(tap_ == 0), stop=(tap_ == CW - 1)),
                     reads=[("diag", ct, 0), ("diag", ct, 1), gluT, ("gluT", ct)], writes=[p], skip_self=True, c=n / 2300.0 + 0.05)
            S.op("act", lambda: A.activation(zT[:, ct, c0:c0 + n], p[:, 0:n], AF.Identity, bias=cp("convb", ct)), reads=[p, cpk], writes=[("zT", cj, ct)], c=0.6)
            S.op("act", lambda: A.activation(zsqT[:, ct, c0:c0 + n], p[:, 0:n], AF.Square, bias=cp("convb", ct)), reads=[p, cpk], writes=[("zsqT", cj, ct)], c=0.6)
        zk = [("zT", cj, ct) for ct in range(4)]; zqk = [("zsqT", cj, ct) for ct in range(4)]
        lm, lr, lt = lnm[cj % 2], lnr[cj % 2], lnt[cj % 2]
        pm = psum(); pq = psum()
        for ct in range(4):
            S.op("pe", lambda ct=ct: T.matmul(pm[:, 0:n], ones_b[:], zT[:, ct, c0:c0 + n], start=(ct == 0), stop=(ct == 3)), reads=[ones_b] + zk, writes=[pm], skip_self=True, c=n / 2300.0 + 0.05)
        for ct in range(4):
            S.op("pe", lambda ct=ct: T.matmul(pq[:, 0:n], ones_b[:], zsqT[:, ct, c0:c0 + n], start=(ct == 0), stop=(ct == 3)), reads=[ones_b] + zqk, writes=[pq], skip_self=True, c=n / 2300.0 + 0.05)
        S.op("act", lambda: A.mul(lm[:, 0:n], pm[:, 0:n], 1.0 / DB), reads=[pm], writes=[lm], c=0.6)
        S.op("dve", lambda: V.tensor_tensor(lt[:, 0:n], lm[:, 0:n], lm[:, 0:n], ALU.mult), reads=[lm], writes=[lt], c=1.1)
        S.op("dve", lambda: V.scalar_tensor_tensor(lr[:, 0:n], pq[:, 0:n], 1.0 / DB, lt[:, 0:n], ALU.mult, ALU.subtract), reads=[pq, lt], writes=[lr], c=0.7)
        S.op("dve", lambda: V.tensor_scalar(lr[:, 0:n], lr[:, 0:n], EPS, None, ALU.add), reads=[lr], writes=[lr], c=0.6)
        S.op("act", lambda: A.activation(lr[:, 0:n], lr[:, 0:n], AF.Ln), reads=[lr], writes=[lr], c=0.6)
        S.op("act", lambda: A.activation(lr[:, 0:n], lr[:, 0:n], AF.Exp, scale=-0.5), reads=[lr], writes=[lr], c=0.6)
        for ct in range(4):
            S.op("dve", lambda ct=ct: V.tensor_tensor(lt[:, 0:n], zT[:, ct, c0:c0 + n], lm[:, 0:n], ALU.subtract), reads=[("zT", cj, ct), lm], writes=[lt], c=0.8)
            S.op("dve", lambda ct=ct: V.tensor_tensor(lt[:, 0:n], lt[:, 0:n], lr[:, 0:n], ALU.mult), reads=[lt, lr], writes=[lt], c=1.1)
            S.op("act", lambda ct=ct: A.activation(zT[:, ct, c0:c0 + n], lt[:, 0:n], AF.Silu, bias=cp("lnb", ct), scale=cp("lng", ct)), reads=[lt, cpk], writes=[("zT", cj, ct), zT], c=0.7)
    tap("zact", zT[:, :, :].rearrange("p k t -> p (k t)"), [128, 4 * TF], zT)
    qT = sb("qT", [128, 4, TF], BF16, "p1c")
    bqs = sb("bqs", [128, 4], F32, "p1c")
    S.op("dve", lambda: V.tensor_scalar(bqs[:], cp("bq", 0, 4), DH ** -0.5, None, ALU.mult), reads=[cpk], writes=[bqs])
    for hd in range(4):
        w, wk_ = wsmall(); load_w(w, w_in[:, OFF["q"][0] + hd * 128:OFF["q"][0] + (hd + 1) * 128], wk_)
        for (c0, n) in TCH:
            p = psum(); proj_fm(w, wk_, c0, n, p)
            S.op("act", lambda: A.activation(qT[:, hd, c0:c0 + n], p[:, 0:n], AF.Identity, bias=bqs[:, hd:hd + 1], scale=DH ** -0.5), reads=[p, bqs], writes=[qT], c=0.6)
    MIN = sb("MIN", [128, (NCH + 6) * 2, 4], F32, "p1c")
    SS = sb("SS", [128, NCH * 2, 20], F32, "p1c")
    EMJ = sb("EMJ", [128, NFULL * 2, 4], F32, "p1c")
    S.op("pool", lambda: P.memset(SS[:, :, :], 0.0), writes=[SS])
    S.op("pool", lambda: P.memset(MIN[:, :, :], 0.0), writes=[MIN])

    def mslot(gi, d):
        return MIN[:, gi * 2 + d, :]

    chains = []
    for sq in range(2):
        chains.append((0, sq, 1, [2 * sq + 1, 2 * sq], NCH + sq * 2 + 1)); chains.append((0, sq, 0, [2 * sq, 2 * sq + 1], NCH + sq * 2))
    chains.append((1, 2, 1, list(range(NCH - 1, 3, -1)), NCH + 5)); chains.append((1, 2, 0, list(range(4, NFULL)), NCH + 4))
    for (q_, sq, d, lst, fin) in chains:
        if sq == 2:
            S.dma("sp", mslot(lst[0], d), stm[:, d * 4:d * 4 + 4].to_broadcast([128, 4]), reads=[MIN], writes=[("MIN", lst[0], d)])
    for k_ in range(16):
        for (q_, sq, d, lst, fin) in chains:
            if k_ >= len(lst):
                continue
            gi = lst[k_]; nslot = lst[k_ + 1] if k_ + 1 < len(lst) else fin
            s_ = SS[:, gi * 2 + d, :]
            S.op("dve", lambda s_=s_, gi=gi, d=d: V.tensor_tensor(s_[:, 0:4], mslot(gi, d), PREP[:, gi, 24 + d * 4:28 + d * 4], ALU.max), reads=PK(gi) + [("MIN", gi, d), MIN, SS], writes=[("SS", gi, d)], c=0.13)
            S.op("dve", lambda s_=s_, gi=gi, d=d, nslot=nslot: V.tensor_tensor(mslot(nslot, d), s_[:, 0:4], PREP[:, gi, 32 + d * 4:36 + d * 4], ALU.subtract), reads=[("SS", gi, d)] + PK(gi) + [MIN], writes=[("MIN", nslot, d)], c=0.16)
    all_ss = [("SS", gi, d) for gi in range(NCH) for d in range(2) if not (gi >= NFULL and d == 0)]
    all_min = [("MIN", gi, d) for gi in range(NCH) for d in range(2) if not (gi >= NFULL and d == 0)]
    ss4 = lambda lo, hi: SS[:, :, lo:hi].rearrange("p (g d) x -> p g d x", d=2)
    S.op("dve", lambda: V.tensor_tensor(ss4(4, 8), PREP[:, :, 0:8].rearrange("p g (d x) -> p g d x", d=2), ss4(0, 4), ALU.subtract), reads=[("PREP", 0)] + all_ss, writes=[("SSb", 0)])
    S.op("dve", lambda: V.tensor_tensor(SS[:, :, 8:12], MIN[:, 0:NCH * 2, :], SS[:, :, 0:4], ALU.subtract), reads=all_min + all_ss, writes=[("SSb", 1)])
    S.op("act", lambda: A.activation(SS[:, :, 12:20], SS[:, :, 4:12], AF.Exp), reads=[("SSb", 0), ("SSb", 1)], writes=[("SSb", 2)])
    S.op("dve", lambda: V.tensor_tensor(EMJ[:, :, :].rearrange("p (g d) x -> p g d x", d=2), PREP[:, 0:NFULL, 8:16].rearrange("p g (d x) -> p g d x", d=2),
                                        SS[:, 0:NFULL * 2, 0:4].rearrange("p (g d) x -> p g d x", d=2), ALU.subtract), reads=[("PREP", 0)] + all_ss, writes=[EMJ])
    S.op("act", lambda: A.activation(EMJ[:, :, :], EMJ[:, :, :], AF.Exp), reads=[EMJ], writes=[EMJ])
    SSK = [("SSb", 2)]
    end_phase("p1b", "wb")

    sbc = lambda name, shape, dt=F32: sb(name, shape, dt, "p1c")
    kT = sbc("kT", [128, 4, TF], BF16)
    hnT = sb("hnT", [128, 4, TF], BF16, "hnT")
    for gi in range(NFULL):
        pt = psum(); ptb = pt[:].bitcast(BF16)
        for hd in range(4):
            S.op("pe", lambda hd=hd: T.transpose(ptb[:, hd * 128:(hd + 1) * 128], k_tok[:, gi, hd * 128:(hd + 1) * 128], ident_b[:]), reads=[("k_tok", gi), ident_b], writes=[pt], skip_self=True, c=0.12)
        S.op("act", lambda: A.copy(kT[:, :, gi * 128:(gi + 1) * 128], ptb[:, 0:512].rearrange("p (h t) -> p h t", h=4)), reads=[pt], writes=[kT], c=0.6)
    tap("qT", qT[:, :, :].rearrange("p k t -> p (k t)"), [128, 4 * TF], qT)

    CIN = sbc("CIN", [128, NFULL * 2, 4 * 129], BF16)
    Cst = {(q_, d): sbc("Cst%d%d" % (q_, d), [128, 4, 129]) for q_ in range(2) for d in range(2)}
    Cdc = {(q_, d): sbc("Cdc%d%d" % (q_, d), [128, 4, 129]) for q_ in range(2) for d in range(2)}
    VWs = [sbc("VWs%d" % i, [128, 4, 129], BF16) for i in range(4)]
    vwn = [0]

    def cin(gi, d):
        return CIN[:, gi * 2 + d, :].rearrange("p (h e) -> p h e", h=4)


    def make_vw(gi, d, on_pool=False):
        vw = VWs[vwn[0] % len(VWs)]; vwn[0] += 1
        if on_pool:
            S.op("pool", lambda: P.tensor_tensor(vw[:], v_tok[:, gi, :, :], SS[:, gi * 2 + d, 12:16].unsqueeze(2).to_broadcast([128, 4, 129]), ALU.mult), reads=[("v_tok", gi)] + SSK, writes=[vw], c=1.2)
        else:
            for hd in range(4):
                S.op("act", lambda hd=hd: A.activation(vw[:, hd, :], v_tok[:, gi, hd, :], AF.Copy, scale=SS[:, gi * 2 + d, 12 + hd:13 + hd]), reads=[("v_tok", gi)] + SSK, writes=[vw], c=0.45)
        return vw

    def c_step(q_, d, gi):
        C = Cst[(q_, d)]; Cd = Cdc[(q_, d)]
        vw = make_vw(gi, d, on_pool=True)
        pd = [psum7(), psum7()]
        for hd in range(4):
            o_ = (hd % 2) * 129
            S.op("pe", lambda hd=hd, o_=o_: T.matmul(pd[hd // 2][:, o_:o_ + 129], k_tok[:, gi, hd * 128:(hd + 1) * 128], vw[:, hd, :], start=True, stop=True),
                 reads=[vw, ("k_tok", gi)], writes=[pd[hd // 2]], skip_self=True)
        S.op("pool", lambda: P.tensor_tensor(Cd[:], C[:], SS[:, gi * 2 + d, 16:20].unsqueeze(2).to_broadcast([128, 4, 129]), ALU.mult), reads=[C] + SSK, writes=[Cd], c=1.4)
        if gi < NFULL:
            S.op("act", lambda: A.copy(cin(gi, d), Cd[:]), reads=[Cd], writes=[("CIN", gi, d)], c=0.62)
        for b2 in range(2):
            S.op("dve", lambda b2=b2: V.tensor_tensor(C[:, 2 * b2:2 * b2 + 2, :], Cd[:, 2 * b2:2 * b2 + 2, :], pd[b2][:, 0:258].rearrange("p (h e) -> p h e", h=2), ALU.add),
                 reads=[Cd, pd[b2]], writes=[C], c=0.42)

    for (q_, sq, d, lst, fin) in chains:
        if sq == 2:
            S.dma("sp", Cst[(q_, d)][:], stCn[:, d * 4:d * 4 + 4, :], writes=[Cst[(q_, d)]])
    NB_ = 3
    ST = [sbc("ST%d" % i, [128, 4, 128], BF16) for i in range(NB_)]
    hdir = [sbc("hdir%d" % i, [128, 4, 128]) for i in range(4)]
    hnb = [sbc("hnb%d" % i, [128, 4, 128], BF16) for i in range(2)]
    hjunk = sbc("hjunk", [128, 128], BF16)
    scO = [sbc("scO%d" % i, [128, 8]) for i in range(NB_)]
    scF = [sbc("scF%d" % i, [128, 8]) for i in range(2)]
    den_bank = ps[7]
    ps_n = [7]

    units = [(gi, d) for gi in range(NFULL) for d in (1, 0)]
    pqs = {}

    vws = {}

    def u_pq(u):
        gi, d = units[u]; c0 = gi * 128
        pq_ = psum7(); pqs[u] = pq_
        for hd in range(4):
            S.op("pe", lambda hd=hd: T.matmul(pq_[:, hd * 128:(hd + 1) * 128], kT[:, hd, c0:c0 + 128], qT[:, hd, c0:c0 + 128], start=True, stop=True),
                 reads=[kT, qT], writes=[pq_], skip_self=True)
        vws[u] = make_vw(gi, d)

    pns = {}

    def u_st_mm(u):
        gi, d = units[u]; c0 = gi * 128; b_ = u % NB_
        pq_ = pqs[u]; vw = vws[u]
        m01 = U01 if d == 0 else L01
        S.op("dve", lambda: V.tensor_tensor(ST[b_][:], pq_[:].rearrange("p (h s) -> p h s", h=4), m01[:].unsqueeze(1).to_broadcast([128, 4, 128]), ALU.mult),
             reads=[pq_, m01], writes=[ST[b_]], c=0.6)
        pn = psum7(); pns[u] = pn
        dk = den_bank; dcol = (u % 4) * 4
        Cb = cin(gi, d)
        for hd in range(4):
            S.op("pe", lambda hd=hd: T.matmul(pn[:, hd * 128:(hd + 1) * 128], qT[:, hd, c0:c0 + 128], Cb[:, hd, 0:128], start=True, stop=False),
                 reads=[qT, ("CIN", gi, d)], writes=[pn], skip_self=True)
            S.op("pe", lambda hd=hd: T.matmul(pn[:, hd * 128:(hd + 1) * 128], ST[b_][:, hd, :], vw[:, hd, 0:128], start=False, stop=True),
                 reads=[ST[b_], vw], writes=[pn], skip_self=True)
        for hd in range(4):
            S.op("pe", lambda hd=hd: T.matmul(den_bank[:, dcol + hd:dcol + hd + 1], qT[:, hd, c0:c0 + 128], Cb[:, hd, 128:129], start=True, stop=False),
                 reads=[qT, ("CIN", gi, d)], writes=[dk], skip_self=True)
            S.op("pe", lambda hd=hd: T.matmul(den_bank[:, dcol + hd:dcol + hd + 1], ST[b_][:, hd, :], vw[:, hd, 128:129], start=False, stop=True),
                 reads=[ST[b_], vw], writes=[dk], skip_self=True)

    def u_fin(u):
        gi, d = units[u]; b_ = u % NB_
        sc_ = scO[b_]; pn = pns[u]
        dk = den_bank; dcol = (u % 4) * 4
        dn = den_bank[:, dcol:dcol + 4]
        S.op("dve", lambda: V.tensor_scalar(sc_[:, 0:4], dn, -1.0, None, ALU.mult), reads=[dk], writes=[sc_], c=0.1)
        S.op("dve", lambda: V.tensor_tensor(sc_[:, 0:4], sc_[:, 0:4], dn, ALU.max), reads=[dk, sc_], writes=[sc_], c=0.16)
        S.op("dve", lambda: V.tensor_tensor(sc_[:, 0:4], sc_[:, 0:4], EMJ[:, gi * 2 + d, :], ALU.max), reads=[sc_, EMJ], writes=[sc_], c=0.16)
        S.op("dve", lambda: V.reciprocal(sc_[:, 4:8], sc_[:, 0:4]), reads=[sc_], writes=[sc_], c=0.18)
        hd_ = hdir[u % 4]
        S.op("dve", lambda: V.tensor_tensor(hd_[:], pn[:].rearrange("p (h e) -> p h e", h=4), sc_[:, 4:8].unsqueeze(2).to_broadcast([128, 4, 128]), ALU.mult), reads=[pn, sc_], writes=[hd_], c=0.7)
        if d == 0:
            hb_ = hdir[(u - 1) % 4]; hf_ = hd_; c0 = gi * 128; f_ = scF[gi % 2]
            S.op("pool", lambda: P.tensor_tensor(hf_[:], hf_[:], hb_[:], ALU.add), reads=[hf_, hb_], writes=[hf_], c=1.26)
            hq_ = hdir[(u - 1) % 4]
            S.op("pool", lambda: P.tensor_tensor(hq_[:], hf_[:], hf_[:], ALU.mult), reads=[hf_], writes=[hq_], c=1.26)
            S.op("dve", lambda: V.reduce_sum(f_[:, 0:4], hq_[:], axis=AX.X), reads=[hq_], writes=[f_], c=0.69)
            S.op("dve", lambda: V.tensor_scalar(f_[:, 4:8], f_[:, 0:4], 1.0 / DH, EPS, ALU.mult, ALU.add), reads=[f_], writes=[f_], c=0.17)
            S.op("act", lambda: A.activation(f_[:, 4:8], f_[:, 4:8], AF.Ln), reads=[f_], writes=[f_], c=0.21)
            S.op("act", lambda: A.activation(f_[:, 4:8], f_[:, 4:8], AF.Exp, scale=-0.5), reads=[f_], writes=[f_], c=0.21)
            hn_ = hnb[gi % 2]
            S.op("dve", lambda: V.tensor_tensor(hn_[:], hf_[:], f_[:, 4:8].unsqueeze(2).to_broadcast([128, 4, 128]), ALU.mult), reads=[hf_, f_], writes=[hn_], c=0.69)
            pt = psum7(); ptb = pt[:].bitcast(BF16)
            for hd in range(4):
                S.op("pe", lambda hd=hd: T.transpose(ptb[:, hd * 128:(hd + 1) * 128], hn_[:, hd, :], ident_b[:]), reads=[hn_, ident_b], writes=[pt], skip_self=True)
            S.op("act", lambda: A.copy(hnT[:, :, c0:c0 + 128], ptb[:, 0:512].rearrange("p (h t) -> p h t", h=4)), reads=[pt], writes=[hnT], c=0.6)

    def emit_units(glist):
        us = []
        for gi in glist:
            us += [units.index((gi, 1)), units.index((gi, 0))]
        if not us:
            return
        u_pq(us[0])
        for i, u in enumerate(us):
            if i + 1 < len(us):
                u_pq(us[i + 1])
            u_st_mm(u)
            if i >= 1:
                u_fin(us[i - 1])
        u_fin(us[-1])

    ready = {1: [0, 1], 3: [2, 3], 8: [12, 11], 9: [10], 10: [9], 11: [8], 12: [7], 13: [6], 14: [5], 15: [4]}
    for k_ in range(16):
        for (q_, sq, d, lst, fin) in chains:
            if sq == 2:
                kk = k_
            else:
                kk = k_ - 2 * sq
                if kk < 0 or kk >= 2:
                    continue
            if kk >= len(lst):
                continue
            if sq < 2 and kk == 0:
                S.op("pool", lambda q_=q_, d=d: P.memset(Cst[(q_, d)][:], 0.0), writes=[Cst[(q_, d)]], c=0.5)
            c_step(q_, d, lst[kk])
            if sq < 2 and kk == len(lst) - 1:
                j0_ = (sq * 2 + d) * 4
                S.dma("sp", oCn[:, j0_:j0_ + 4, :], Cst[(q_, d)][:], reads=[Cst[(q_, d)]])
                S.dma("sp", om[:, j0_:j0_ + 4], MIN[0:1, fin * 2 + d, :], reads=[("MIN", fin, d)])
        emit_units(ready.get(k_, []))
    tap("hnT", hnT[:, :, :].rearrange("p k t -> p (k t)"), [128, 4 * TF], hnT)
    end_phase("p1c", "kv")
    wb_alloc(8, "wb")

    mergedT = sb("mergedT", [128, 8, TF], BF16, "merged")
    sgt = [sb("sgtc%d" % i, [128, 512], BF16, "p1c2") for i in range(2)]
    tmb = sb("tmb", [128, 512], BF16, "p1c2")
    mod_alloc(["modG"], "modG1")
    wo = sb("wo", [128, 8, D], BF16, "p1d")
    g1pre = []
    wa2s = [(sb("wa2_%d" % half, [128, 8, 512], BF16, "p1c2"), sb("bb2_%d" % half, [1, 512], BF16, "p1c2")) for half in range(2)]
    for hd in range(4):
        w, wk_ = wsmall(); load_w(w, w_in[:, OFF["o"][0] + hd * 128:OFF["o"][0] + (hd + 1) * 128], wk_)
        if hd == 1:
            for half in range(2):
                wa2, bb2 = wa2s[half]
                g1pre.append(mod_prefetch(2 * D + half * 512, wa2[:, :, :], [wa2], bb2))
        if hd == 3:
            load_w(wo[:], w_out, wo)
        for ci, (c0, n) in enumerate(TCH):
            p = psum(); proj_fm(w, wk_, c0, n, p)
            sg = sgt[ci % 2]
            S.op("act", lambda: A.activation(sg[:, 0:n], p[:, 0:n], AF.Sigmoid, bias=cp("bo", hd)), reads=[p, cpk], writes=[sg])
            S.op("dve", lambda: V.scalar_tensor_tensor(hnT[:, hd, c0:c0 + n], hnT[:, hd, c0:c0 + n], cp("mng", hd), sg[:, 0:n], ALU.mult, ALU.mult), reads=[hnT, cpk, sg], writes=[hnT])
    mod_gate(2 * D, g1pre)
    MOD1G = MOD["modG"]
    mod_alloc(["modG"], "mod2g")
    m2cols = [3 * D, 3 * D + 512, 4 * D, 4 * D + 512, 5 * D, 5 * D + 512]
    m2pre = {}

    def m2_prefetch(i):
        w_, k_, b_ = g1pre[i % 2]
        m2pre[i] = mod_prefetch(m2cols[i], w_, k_, b_)

    def m2_compute(i):
        modG2 = MOD["modG"]
        c0 = (i % 2) * 512
        if i < 4:
            mod_cols(m2cols[i], pre=m2pre[i])
            if i == 3:
                mod_gain(2)
        else:
            ev = lambda which, p: S.op("act", lambda: A.copy(modG2[:, which, c0:c0 + 512], p[:]), reads=[p], writes=[modG2])
            mod_chunk(m2cols[i], ev, pre=m2pre[i])

    m2_prefetch(0); m2_prefetch(1)
    for dt_ in range(8):
        if dt_ >= 1 and dt_ - 1 < 6:
            m2_compute(dt_ - 1)
            if dt_ + 1 < 6:
                m2_prefetch(dt_ + 1)
        for br, (gname, bname, wproj, srcT) in enumerate((("gb", "bgb", w_pb, zT), ("ga", "bga", w_pa, hnT))):
            wg, wgk = wsmall(); load_w(wg, w_in[:, OFF[gname][0] + dt_ * 128:OFF[gname][0] + (dt_ + 1) * 128], wgk)
            wp, wpk = wsmall(); load_w(wp[:, 0:4, :], wproj[:, dt_ * 128:(dt_ + 1) * 128], wpk)
            for ci, (c0, n) in enumerate(TCH):
                pA = psum(); pB = psum()
                proj_fm(wg, wgk, c0, n, pA)
                for ct in range(4):
                    S.op("pe", lambda ct=ct: T.matmul(pB[:, 0:n], wp[:, ct, :], srcT[:, ct, c0:c0 + n], start=(ct == 0), stop=(ct == 3)), reads=wpk + [srcT], writes=[pB], skip_self=True)
                sg = sgt[ci % 2]
                S.op("act", lambda: A.activation(sg[:, 0:n], pA[:, 0:n], AF.Sigmoid, bias=cp(bname, dt_)), reads=[pA, cpk], writes=[sg])
                if br == 0:
                    S.op("dve", lambda: V.tensor_tensor(mergedT[:, dt_, c0:c0 + n], pB[:, 0:n], sg[:, 0:n], ALU.mult), reads=[pB, sg], writes=[mergedT])
                else:
                    S.op("dve", lambda: V.tensor_tensor(tmb[:, 0:n], pB[:, 0:n], sg[:, 0:n], ALU.mult), reads=[pB, sg], writes=[tmb])
                    S.op("dve", lambda: V.tensor_tensor(mergedT[:, dt_, c0:c0 + n], mergedT[:, dt_, c0:c0 + n], tmb[:, 0:n], ALU.add), reads=[mergedT, tmb], writes=[mergedT])
    tap("mergedT", mergedT[:, :, :].rearrange("p k t -> p (k t)"), [128, 8 * TF], mergedT)
    end_phase("p1c2", "zact", "hnT")

    x1 = sb("x1", [128, NFULL, D], F32, "x1", high=True)
    modG = MOD1G
    xt = [sb("xtd%d" % i, [128, D], F32, "p1d") for i in range(2)]
    tmpos = [sb("tmpo%d" % i, [128, 512], F32, "p1d") for i in range(2)]
    for gi in range(NFULL):
        src, which = chunk_src(gi)
        x_ = xt[gi % 2]
        S.dma("sp", x_[:], src, writes=[x_])
        for dc in range(2):
            p = psum()
            for kt in range(8):
                S.op("pe", lambda kt=kt: T.matmul(p[:], mergedT[:, kt, gi * 128:(gi + 1) * 128], wo[:, kt, dc * 512:(dc + 1) * 512], start=(kt == 0), stop=(kt == 7)),
                     reads=[mergedT, wo], writes=[p], skip_self=True)
            tmpo = tmpos[dc]
            S.op("dve", lambda: V.tensor_tensor(tmpo[:], p[:], modG[:, which, dc * 512:(dc + 1) * 512], ALU.mult), reads=[p, modG], writes=[tmpo], c=0.7)
            S.op("pool", lambda: P.tensor_tensor(x1[:, gi, dc * 512:(dc + 1) * 512], tmpo[:], x_[:, dc * 512:(dc + 1) * 512], ALU.add), reads=[tmpo, x_], writes=[("x1", gi)], c=1.3)
    tap("x1", x1[:, :, :].rearrange("p g e -> p (g e)"), [128, NFULL * D], ("x1", NFULL - 1))

    norm_phase(list(range(NFULL)), lambda gi, stage, i: (x1[:, gi, :], ("x1", gi)), lambda gi: 0 if gi < 4 else 1,
               lambda gi: hT[:, :, gi * 128:(gi + 1) * 128], lambda gi: ("hT", gi), lambda: None, "p2a", batched=False, layer=2)
    end_phase("p1d", "merged", "modG1", "p2a")

    actT = sb("actT", [128, NF, 1536], BF16, "actT")
    gP = [sb("gP%d" % i, [128, 2, 258], BF16, "p2b") for i in range(2)]
    gS = [sb("gS%d" % i, [128, 18, 66], BF16, "p2b") for i in range(2)]
    dg = [sb("dg%d" % i, [128, 9, 128], BF16, "p2b") for i in range(2)]
    gl = [sb("gl%d" % i, [128, 512], BF16, "p2b") for i in range(2)]
    for i in range(2):
        S.op("pool", lambda i=i: P.memset(gP[i][:], 0.0), writes=[gP[i]])
        S.op("pool", lambda i=i: P.memset(gS[i][:], 0.0), writes=[gS[i]])
    for f in range(NF):
        wgg, wggk = wsmall(); load_w(wgg, w_up[:, f * 128:(f + 1) * 128], wggk)
        wgv, wgvk = wsmall(); load_w(wgv, w_up[:, DFF + f * 128:DFF + (f + 1) * 128], wgvk)
        gp_, gs_, dg_ = gP[f % 2], gS[f % 2], dg[f % 2]
        for t9 in range(9):
            S.op("dve", lambda t9=t9: V.tensor_scalar(dg_[:, t9, :], ident_b[:], cp("ffnw", f * 9 + t9), None, ALU.mult), reads=[ident_b, cpk], writes=[dg_])
        p = psum(); proj_fm(wgg, wggk, 0, 512, p)
        S.op("act", lambda: A.copy(gp_[:, :, 1:257], p[:].rearrange("p (s t) -> p s t", s=2)), reads=[p], writes=[gp_])
        for r0, (c0, n) in ((0, (512, 512)), (8, (1024, 512)), (16, (1536, 64))):
            p = psum(); proj_fm(wgg, wggk, c0, n, p)
            nr = n // 64
            S.op("act", lambda p=p, r0=r0, nr=nr, n=n: A.copy(gs_[:, 1 + r0:1 + r0 + nr, 1:65], p[:, 0:n].rearrange("p (r t) -> p r t", r=nr)), reads=[p], writes=[gs_])
        outs = []
        pc = psum()
        for dc in range(3):
            S.op("pe", lambda dc=dc: T.matmul(pc[:].rearrange("p (s t) -> p s t", s=2), dg_[:, 3 + dc, :], gp_[:, :, dc:dc + 256], start=(dc == 0), stop=(dc == 2)),
                 reads=[dg_, gp_], writes=[pc], skip_self=True)
        outs.append((pc, 0))
        for half in range(2):
            pc = psum()
            for t9 in range(9):
                dr, dc = t9 // 3, t9 % 3
                S.op("pe", lambda t9=t9, dr=dr, dc=dc, pc=pc: T.matmul(pc[:].rearrange("p (r t) -> p r t", r=8), dg_[:, t9, :], gs_[:, half * 8 + dr:half * 8 + dr + 8, dc:dc + 64], start=(t9 == 0), stop=(t9 == 8)),
                     reads=[dg_, gs_], writes=[pc], skip_self=True)
            outs.append((pc, 512 + half * 512))
        for oi, (pc, c0) in enumerate(outs):
            pv = psum(); proj_fm(wgv, wgvk, c0, 512, pv)
            g_ = gl[oi % 2]
            S.op("act", lambda pc=pc, g_=g_: A.activation(g_[:], pc[:], AF.Gelu_apprx_tanh, bias=cp("ffnb", f)), reads=[pc, cpk], writes=[g_])
            S.op("dve", lambda pv=pv, g_=g_, c0=c0: V.tensor_tensor(actT[:, f, c0:c0 + 512], g_[:], pv[:], ALU.mult), reads=[g_, pv], writes=[("actT", f)])
    tap("actT", actT[:, :, :].rearrange("p f t -> p (f t)"), [128, NF * 1536], ("actT", NF - 1))
    wd00 = sb("wd0_0", [128, 11, 512], BF16, "p2c")
    S.dma("pool", wd00[:, :, :], w_down[0:11 * 128, 0:512].rearrange("(f p) n -> p f n", p=128), writes=[wd00])
    end_phase("p2b", "hT", "wb")

    wds = [[wd00, sb("wd0_1", [128, 11, 512], BF16, "p2c")], [sb("wd1_%d" % h_, [128, 11, 512], BF16, "p2c") for h_ in range(2)]]
    mod_alloc(["ngB"], "p2c")
    modG = MOD["modG"]; ngB = MOD["ngB"]
    load_ng("fng")
    tmpos2 = [sb("tmpo2_%d" % i, [128, 512], F32, "p2c") for i in range(2)]
    yo = [sb("yo%d" % i, [128, D], F32, "p2c") for i in range(2)]
    junk = sb("junk3", [128, D], BF16, "p2c")
    fsb = sb("fsb", [128, 32], F32, "p2c")

    def fin_a(gi):
        S.op("act", lambda: A.activation(junk[:], x1[:, gi, :], AF.Square, accum_out=fsb[:, gi:gi + 1]), reads=[("x1", gi)], writes=[junk, ("fsb", gi)])
        S.op("dve", lambda: V.tensor_scalar(fsb[:, 16 + gi:17 + gi], fsb[:, gi:gi + 1], 1.0 / D, EPS, ALU.mult, ALU.add), reads=[("fsb", gi)], writes=[("fsb", gi)])
        S.op("act", lambda: A.activation(fsb[:, 16 + gi:17 + gi], fsb[:, 16 + gi:17 + gi], AF.Ln), reads=[("fsb", gi)], writes=[("fsb", gi)])
        S.op("act", lambda: A.activation(fsb[:, 16 + gi:17 + gi], fsb[:, 16 + gi:17 + gi], AF.Exp, scale=-0.5), reads=[("fsb", gi)], writes=[("fsb", gi)])

    def fin_b(gi):
        y_ = yo[gi % 2]
        S.op("dve", lambda: V.scalar_tensor_tensor(y_[:], x1[:, gi, :], fsb[:, 16 + gi:17 + gi], ngB[:], ALU.mult, ALU.mult), reads=[("x1", gi), ("fsb", gi), ngB], writes=[y_])
        dst = yp[gi * 128:(gi + 1) * 128, :] if gi < 4 else ys[(gi - 4) * 128:(gi - 3) * 128, :]
        S.dma("sp", dst, y_[:], reads=[y_])

    for dc in range(2):
        wd = wds[dc]
        if dc == 1:
            S.dma("pool", wd[0][:, :, :], w_down[0:11 * 128, dc * 512:(dc + 1) * 512].rearrange("(f p) n -> p f n", p=128), writes=[wd[0]])
        S.dma("pool", wd[1][:, :, :], w_down[11 * 128:22 * 128, dc * 512:(dc + 1) * 512].rearrange("(f p) n -> p f n", p=128), writes=[wd[1]])
    for dc in range(2):
        wd = wds[dc]
        for gi in range(12):
            which = 0 if gi < 4 else 1
            p = psum()
            for f in range(NF):
                S.op("pe", lambda f=f: T.matmul(p[:], actT[:, f, gi * 128:(gi + 1) * 128], wd[f // 11][:, f % 11, :], start=(f == 0), stop=(f == NF - 1)),
                     reads=[wd[0], wd[1]], writes=[p], skip_self=True)
            tmpo = tmpos2[gi % 2]
            S.op("dve", lambda: V.tensor_tensor(tmpo[:], p[:], modG[:, which, dc * 512:(dc + 1) * 512], ALU.mult), reads=[p, modG], writes=[tmpo], c=0.7)
            S.op("pool", lambda: P.tensor_tensor(x1[:, gi, dc * 512:(dc + 1) * 512], x1[:, gi, dc * 512:(dc + 1) * 512], tmpo[:], ALU.add), reads=[tmpo, ("x1", gi)], writes=[("x1", gi)], c=1.3)
            if dc == 1:
                fin_a(gi)
                if gi >= 1:
                    fin_b(gi - 1)
    fin_b(11)
    S.end_defer()
    S.barrier()
    S.wait_everything("sp")
    top.close()
    S.close()
    return nc, dbg_out, S.n_inst


def _core_inputs(inp, c):
    j, half = c // 2, c % 2
    flip = half == 1
    f = (lambda a, ax: np.flip(a, axis=ax)) if flip else (lambda a, ax: a)
    m = {}
    m["xp"] = np.ascontiguousarray(f(inp["x_prompt"][2 * c:2 * c + 2], 1)).reshape(512, D)
    m["xs"] = np.ascontiguousarray(f(inp["x_sample"][j], 0))
    cvec = np.stack([inp["c_ctx"], inp["c"][j]], 0)
    m["ccol"] = np.ascontiguousarray(cvec.reshape(2, 8, 128).transpose(2, 0, 1).reshape(128, 16))
    dirs = [1, 0] if flip else [0, 1]
    C = inp["state_C"][j, 0][dirs].reshape(8, 128, 128); n = inp["state_n"][j, 0][dirs].reshape(8, 128)
    m["stCn"] = np.ascontiguousarray(np.concatenate([C.transpose(1, 0, 2), n.T[:, :, None]], axis=2))
    m["stm"] = np.ascontiguousarray(inp["state_m"][j, 0][dirs].reshape(1, 8))
    w_in = inp["w_in"][0]; b_in = inp["b_in"][0]
    if flip:
        perm = np.arange(IN_COLS)
        perm[GOFF:GOFF + 16] = np.concatenate([np.arange(GOFF + 8, GOFF + 16), np.arange(GOFF, GOFF + 8)])
        w_in = w_in[:, perm]; b_in = b_in[perm]
    m["w_in"] = np.ascontiguousarray(w_in)
    conv_w = f(inp["conv_dw_w"][0], 0)
    ffn_w = f(f(inp["ffn_dw_w"][0], 0), 1).reshape(9, DFF)
    colt = lambda v: v.reshape(-1, 128).T
    cols = [colt(b_in[OFF[k][0]:OFF[k][1]]) for k in ("q", "k", "o", "ua", "ub", "ga", "gb")]
    cols += [colt(inp[k][0]) for k in ("conv_dw_b", "conv_ln_g", "conv_ln_b", "mlstm_norm_g")]
    cols.append(conv_w.reshape(CW, 4, 128).transpose(2, 1, 0).reshape(128, 4 * CW))
    cols.append(ffn_w.reshape(9, NF, 128).transpose(2, 1, 0).reshape(128, NF * 9))
    cols.append(colt(inp["ffn_dw_b"][0]))
    cols += [colt(inp["b_ada"][0]), colt(inp["norm1_g"][0]), colt(inp["norm2_g"][0])]
    m["colpack"] = np.ascontiguousarray(np.concatenate(cols, axis=1).astype(np.float32))
    rows = [b_in[OFF["k"][0]:OFF["k"][1]], b_in[OFF["v"][0]:OFF["v"][1]], b_in[GOFF:GOFF + 16], inp["norm1_g"][0], inp["norm2_g"][0],
            inp["final_norm_g"], inp["b_ada"][0]]
    m["rowpack"] = np.ascontiguousarray(np.concatenate(rows)[None, :].astype(np.float32))
    return m


_SHARED = ("w_ada", "w_proj_a", "w_proj_b", "w_out", "w_up", "w_down")


def run(inputs, dbg=None, trace=False):
    inp = {k: np.asarray(v) for k, v in inputs.items()}
    nc, dbg_out, n_inst = build(dbg=dbg)
    shared = {k: np.ascontiguousarray(inp[k][0]) for k in _SHARED}
    in_maps = []
    for c in range(8):
        m = _core_inputs(inp, c)
        m.update(shared)
        in_maps.append(m)
    res = run_bass_kernel_spmd(nc, in_maps, core_ids=list(range(8)), **({"trace": True} if trace else {}))
    return res, dbg_out


def kernel(**inputs):
    res, _ = run(inputs)
    y_prompt = np.zeros((16, 256, D), np.float32); y_sample = np.zeros((4, 2048, D), np.float32)
    nC = np.zeros((16, 1, 2, NH, DH, DH), np.float32); nn = np.zeros((16, 1, 2, NH, DH), np.float32); nm = np.zeros((16, 1, 2, NH), np.float32)
    for c in range(8):
        r = res.results[c]
        j, half = c // 2, c % 2
        yp = r["yp"].reshape(2, 256, D); ys = r["ys"]
        oCn = r["oCn"].reshape(128, 2, 2, 4, 129); om = r["om"].reshape(2, 2, 4)
        if half:
            yp = yp[:, ::-1]; ys = ys[::-1]
            oCn = oCn[:, :, ::-1]; om = om[:, ::-1]
        y_prompt[2 * c:2 * c + 2] = yp
        y_sample[j, 1024 * half:1024 * (half + 1)] = ys
        for s in range(2):
            nC[2 * c + s, 0] = oCn[:, s, :, :, 0:128].transpose(1, 2, 0, 3)
            nn[2 * c + s, 0] = oCn[:, s, :, :, 128].transpose(1, 2, 0)
            nm[2 * c + s, 0] = om[s]
    return (y_prompt, y_sample, nC, nn, nm)
```

```python
import numpy as np
import types
from contextlib import ExitStack
import concourse.bass as bass
import concourse.mybir as mybir
from concourse.bass_utils import run_bass_kernel_spmd

F32 = mybir.dt.float32
BF16 = mybir.dt.bfloat16
AF = mybir.ActivationFunctionType
ALU = mybir.AluOpType
AX = mybir.AxisListType

D = 1024; DA = 512; NH = 4; DH = 128; L = 128; DB = 512; CW = 31; DFF = 2816; GW = 64
NF = DFF // 128
EPS = 1e-6
NEG = -1.0e30
IN_COLS = 4 * DA + 4 * NH + 2 * DB + 2 * D
OFF = {}
_o = 0
for _n, _s in [("q", 512), ("k", 512), ("v", 512), ("o", 512), ("i_f", 4), ("f_f", 4), ("i_b", 4), ("f_b", 4),
               ("ua", 512), ("ub", 512), ("ga", 1024), ("gb", 1024)]:
    OFF[_n] = (_o, _o + _s); _o += _s
GOFF = OFF["i_f"][0]

CP = {}
_o = 0
for _n, _s in [("bq", 4), ("bk", 4), ("bo", 4), ("bua", 4), ("bub", 4), ("bga", 8), ("bgb", 8), ("convb", 4), ("lng", 4),
               ("lnb", 4), ("mng", 4), ("convw", 4 * CW), ("ffnw", NF * 9), ("ffnb", NF), ("badac", 48), ("n1gc", 8), ("n2gc", 8)]:
    CP[_n] = _o; _o += _s
NCP = _o
RP = {}
_o = 0
for _n, _s in [("bk", 512), ("bv", 512), ("bg", 16), ("n1g", 1024), ("n2g", 1024), ("fng", 1024), ("bada", 6144)]:
    RP[_n] = _o; _o += _s
NRP = _o

NFULL = 13
NCH = 20
TF = NFULL * 128
TCH = [(0, 512), (512, 512), (1024, 512), (1536, 128)]


def _freeze(fn):
    if fn.__closure__ is None:
        return fn
    cells = []
    for c in fn.__closure__:
        try:
            cells.append(types.CellType(c.cell_contents))
        except ValueError:
            cells.append(c)
    return types.FunctionType(fn.__code__, fn.__globals__, fn.__name__, fn.__defaults__, tuple(cells))


class Sched:
    def __init__(self, nc, n_dma_sems=32):
        self.nc = nc
        self.eng = {"pe": nc.tensor, "act": nc.scalar, "dve": nc.vector, "pool": nc.gpsimd, "sp": nc.sync}
        self.sem = {}
        self.cnt = {}
        self._cms = []
        for e in ["pe", "act", "dve", "pool"]:
            cm = nc.semaphore("s_" + e)
            self.sem[e] = cm.__enter__()
            self._cms.append(cm)
            self.cnt[e] = 0
        self.dma_sems = []
        for i in range(n_dma_sems):
            cm = nc.semaphore("s_dma%d" % i)
            self.dma_sems.append([cm.__enter__(), 0])
            self._cms.append(cm)
        self.dma_rr = 0
        self.dma_rr_q = {}
        self.known = {e: {} for e in self.eng}
        self.lastw = {}
        self.reads = {}
        self.n_inst = 0
        self.inherit = {}
        self.alias = {}
        self.deferred = None

    def _inh(self, e, b):
        if not self.inherit:
            return
        if isinstance(b, tuple):
            nm = b[0]
        elif isinstance(b, str):
            nm = b
        else:
            nm = getattr(b, "name", None)
        nm = self.alias.get(nm, nm)
        evs = self.inherit.get(nm)
        if evs:
            for ev in evs:
                self._wait(e, ev)

    def close(self):
        for cm in reversed(self._cms):
            cm.__exit__(None, None, None)

    @staticmethod
    def _key(b):
        return b if isinstance(b, (str, tuple)) else id(b)

    def _wait(self, e, ev):
        key, semh, val = ev
        if self.known[e].get(key, 0) >= val:
            return
        self.eng[e].wait_ge(semh, val)
        self.known[e][key] = val
        self.n_inst += 1

    def _deps(self, e, reads, writes, skip_self=False):
        for b in reads:
            self._inh(e, b)
        for b in writes:
            self._inh(e, b)
        for b in reads:
            ev = self.lastw.get(self._key(b))
            if ev is not None and not (skip_self and ev[0] == e):
                self._wait(e, ev)
        for b in writes:
            k = self._key(b)
            ev = self.lastw.get(k)
            if ev is not None and not (skip_self and ev[0] == e):
                self._wait(e, ev)
            for ev in self.reads.get(k, ()):
                if not (skip_self and ev[0] == e):
                    self._wait(e, ev)

    def _record(self, ev, reads, writes):
        for b in reads:
            self.reads.setdefault(self._key(b), []).append(ev)
        for b in writes:
            k = self._key(b)
            self.lastw[k] = ev
            self.reads[k] = []

    def begin_defer(self):
        assert self.deferred is None
        self.deferred = []

    def end_defer(self):
        ops = self.deferred
        self.deferred = None
        n = len(ops)
        lastw = {}; readers = {}
        deps = [set() for _ in range(n)]
        for i, o in enumerate(ops):
            rk = [self._key(b) for b in o["reads"]]; wk = [self._key(b) for b in o["writes"]]
            for k in rk:
                if k in lastw:
                    deps[i].add(lastw[k])
            for k in wk:
                if k in lastw:
                    deps[i].add(lastw[k])
                for r in readers.get(k, ()):
                    deps[i].add(r)
            for k in rk:
                readers.setdefault(k, []).append(i)
            for k in wk:
                lastw[k] = i
                readers[k] = []
            deps[i].discard(i)
        users = [[] for _ in range(n)]
        ndep = [len(d) for d in deps]
        for i, d in enumerate(deps):
            for j in d:
                users[j].append(i)
        blevel = [0.0] * n
        for i in range(n - 1, -1, -1):
            m_ = 0.0
            for u in users[i]:
                if blevel[u] > m_:
                    m_ = blevel[u]
            blevel[i] = ops[i]["cost"] + 0.3 + m_
        finish = [0.0] * n
        etime = {}
        ready = [i for i in range(n) if ndep[i] == 0]
        order = []
        while ready:
            sts = {}
            bs0 = None
            for i in ready:
                o = ops[i]
                st = etime.get(o["e"], 0.0)
                for j in deps[i]:
                    lat = 0.15 if ops[j]["e"] == o["e"] else 0.9
                    st = max(st, finish[j] + lat)
                sts[i] = st
                if bs0 is None or st < bs0:
                    bs0 = st
            best = None
            for i in ready:
                if sts[i] <= bs0 + 0.25:
                    if best is None or blevel[i] > blevel[best] + 1e-9 or (abs(blevel[i] - blevel[best]) <= 1e-9 and i < best):
                        best = i
            bs = sts[best]
            o = ops[best]
            ready.remove(best)
            if o["kind"] == "dma":
                etime[o["e"]] = bs + (0.9 if o["e"] == "pool" else 0.2)
                finish[best] = bs + o["cost"]
            else:
                etime[o["e"]] = bs + o["cost"]
                finish[best] = bs + o["cost"] + (0.15 if o["e"] == "pe" else 0.05)
            order.append(best)
            for u in users[best]:
                ndep[u] -= 1
                if ndep[u] == 0:
                    ready.append(u)
        assert len(order) == n
        for i in order:
            o = ops[i]
            if o["kind"] == "dma":
                self.dma(o["e"], o["out"], o["in_"], reads=o["reads"], writes=o["writes"], **o["kw"])
            else:
                self.op(o["e"], o["fn"], reads=o["reads"], writes=o["writes"], skip_self=o["skip_self"])
        return max(finish) if n else 0.0

    def op(self, e, fn, reads=(), writes=(), skip_self=False, c=None):
        if self.deferred is not None:
            self.deferred.append(dict(kind="op", e=e, fn=_freeze(fn), reads=list(reads), writes=list(writes), skip_self=skip_self,
                                      cost=(c if c is not None else {"pe": 0.2, "pool": 1.0}.get(e, 0.5))))
            return None
        self._deps(e, reads, writes, skip_self)
        inst = fn()
        self.cnt[e] += 1
        inst.then_inc(self.sem[e], 1)
        ev = (e, self.sem[e], self.cnt[e])
        self._record(ev, reads, writes)
        self.n_inst += 1
        return ev

    def dma(self, q, out, in_, reads=(), writes=(), **kw):
        if self.deferred is not None:
            self.deferred.append(dict(kind="dma", e=q, out=out, in_=in_, reads=list(reads), writes=list(writes), kw=kw, cost=2.5))
            return None
        half = len(self.dma_sems) // 2
        base = 0 if q == "sp" else half
        rr = self.dma_rr_q.get(q, 0)
        idx = base + rr
        self.dma_rr_q[q] = (rr + 1) % half
        slot = self.dma_sems[idx]
        key = "dma%d" % idx
        if slot[1] > 0:
            self._wait(q, (key, slot[0], slot[1]))
        self._deps(q, reads, writes)
        inst = self.eng[q].dma_start(out=out, in_=in_, **kw)
        slot[1] += 16
        inst.then_inc(slot[0], 16)
        ev = (key, slot[0], slot[1])
        self._record(ev, reads, writes)
        self.n_inst += 1
        return ev

    def _all_events(self):
        evs = [(e, self.sem[e], self.cnt[e]) for e in self.cnt if self.cnt[e] > 0]
        for i, (s, v) in enumerate(self.dma_sems):
            if v > 0:
                evs.append(("dma%d" % i, s, v))
        return evs

    def barrier(self):
        evs = self._all_events()
        for e in ["pe", "act", "dve", "pool", "sp"]:
            for ev in evs:
                if ev[0] != e or e != "pe":
                    self._wait(e, ev)
        self.lastw = {}
        self.reads = {}

    def wait_everything(self, e):
        for ev in self._all_events():
            self._wait(e, ev)


class Tl:
    def __init__(self, ap, name, rng, group):
        self.ap = ap; self.name = name; self.rng = rng; self.group = group

    def __getitem__(self, idx):
        return self.ap[idx]


class Arena:
    def __init__(self, nc, stack, kbytes=207):
        self.total = kbytes * 1024
        self.t = stack.enter_context(nc.sbuf_tensor("arena", [128, self.total // 2], BF16))
        self.free = [(0, self.total)]
        self.live = []
        self.freed = []
        self.sched = None

    def alloc(self, name, shape, dt=F32, group="top", high=False):
        esz = 4 if dt == F32 else 2
        n = 1
        for d_ in shape[1:]:
            n *= d_
        nbytes = (n * esz + 63) // 64 * 64
        order = range(len(self.free) - 1, -1, -1) if high else range(len(self.free))
        for i in order:
            o, sz = self.free[i]
            if sz >= nbytes:
                if high:
                    self.free[i] = (o, sz - nbytes)
                    o = o + sz - nbytes
                else:
                    self.free[i] = (o + nbytes, sz - nbytes)
                break
        else:
            raise MemoryError("arena full allocating %s (%d B); free=%s" % (name, nbytes, self.free))
        ap = self.t[0:shape[0], o // 2:(o + n * esz) // 2]
        if dt == F32:
            ap = ap.bitcast(F32)
        if len(shape) == 3:
            ap = ap.rearrange("p (a b) -> p a b", a=shape[1])
        elif len(shape) == 4:
            ap = ap.rearrange("p (a b c) -> p a b c", a=shape[1], b=shape[2])
        tl = Tl(ap, name, (o, nbytes), group)
        self.live.append(tl)
        evs = {}
        for (fo, fsz, fe) in self.freed:
            if fo < o + nbytes and o < fo + fsz:
                for ev in fe:
                    if ev[0] not in evs or evs[ev[0]][2] < ev[2]:
                        evs[ev[0]] = ev
        if self.sched is not None:
            if evs:
                self.sched.inherit[name] = list(evs.values())
            else:
                self.sched.inherit.pop(name, None)
        return tl

    def free_group(self, group):
        keep = []
        snap = self.sched._all_events() if self.sched is not None else []
        for tl in self.live:
            if tl.group == group:
                self.free.append(tl.rng)
                self.freed.append((tl.rng[0], tl.rng[1], snap))
            else:
                keep.append(tl)
        self.live = keep
        self.free.sort()
        merged = []
        for o, sz in self.free:
            if sz == 0:
                continue
            if merged and merged[-1][0] + merged[-1][1] == o:
                merged[-1] = (merged[-1][0], merged[-1][1] + sz)
            else:
                merged.append((o, sz))
        self.free = merged


def build(dbg=None, stop_after=None):
    nc = bass.Bass("TRN2", target_bir_lowering=False)
    di = lambda n, s: nc.dram_tensor(n, s, F32, kind="ExternalInput").ap()
    do = lambda n, s: nc.dram_tensor(n, s, F32, kind="ExternalOutput").ap()
    xp = di("xp", [512, D]); xs = di("xs", [2048, D]); ccol = di("ccol", [128, 16])
    stCn = di("stCn", [128, 8, 129]); stm = di("stm", [1, 8])
    colpack = di("colpack", [128, NCP]); rowpack = di("rowpack", [1, NRP])
    w_ada = di("w_ada", [D, 6 * D]); w_in = di("w_in", [D, IN_COLS]); w_pa = di("w_proj_a", [DA, D]); w_pb = di("w_proj_b", [DB, D])
    w_out = di("w_out", [D, D]); w_up = di("w_up", [D, 2 * DFF]); w_down = di("w_down", [DFF, D])
    yp = do("yp", [512, D]); ys = do("ys", [1024, D]); oCn = do("oCn", [128, 16, 129]); om = do("om", [1, 16])
    dbg_out = {}

    S = Sched(nc)
    V, A, P, T = nc.vector, nc.scalar, nc.gpsimd, nc.tensor
    top = ExitStack()
    AR = Arena(nc, top)
    AR.sched = S
    sb = AR.alloc

    S.alias.update({"SSb": "SS", "den": None})

    def tap(name, ap, shape, key):
        if dbg is None or name not in dbg:
            return
        o = nc.dram_tensor("dbg_" + name, list(shape), F32, kind="ExternalOutput").ap()
        S.dma("pool", o, ap, reads=[key])
        dbg_out[name] = shape

    def end_phase(*groups):
        S.end_defer()
        for g in groups:
            AR.free_group(g)
        S.begin_defer()

    ps = [top.enter_context(nc.psum_tensor("ps%d" % i, [128, 512], F32)) for i in range(8)]
    ps_rr = [0]

    def psum():
        p = ps[ps_rr[0]]
        ps_rr[0] = (ps_rr[0] + 1) % 8
        return p

    def psum7():
        p = ps[ps_rr[0] % 7]
        ps_rr[0] = (ps_rr[0] + 1) % 7
        return p

    ident_f = sb("ident_f", [128, 128]); ident_b = sb("ident_b", [128, 128], BF16)
    U01 = sb("U01", [128, 128]); L01 = sb("L01", [128, 128]); maskU = sb("maskU", [128, 128]); maskL = sb("maskL", [128, 128])
    ones_f = sb("ones_f", [128, 128]); ones_b = sb("ones_b", [128, 128], BF16)
    cpk = sb("cpk", [128, NCP]); bkB = sb("bkB", [128, 512]); bvB = sb("bvB", [128, 512]); bgB = sb("bgB", [128, 16])
    csil = sb("csil", [128, 16], BF16); ccs = sb("ccs", [128, 16])
    bada_b = sb("bada_b", [1, 512], BF16)
    sm = sb("sm", [128, 64])
    hT = sb("hT", [128, 8, TF], BF16, "hT")
    WBS = {}

    def wb_alloc(nslots, group):
        WBS["t"] = sb("wbs_" + group, [128, nslots * 1024], BF16, group)
        WBS["n"] = nslots; WBS["s"] = 0; WBS["b"] = 0; WBS["g"] = group
        S.alias["wbs"] = "wbs_" + group

    def wsmall():
        i = WBS["s"] % WBS["n"]; WBS["s"] += 1
        return WBS["t"][:, i * 1024:(i + 1) * 1024].rearrange("p (k n) -> p k n", k=8), [("wbs", WBS["g"], i)]

    def wbig():
        nb = WBS["n"] // 4
        j = WBS["b"] % nb; WBS["b"] += 1
        return WBS["t"][:, j * 4096:(j + 1) * 4096].rearrange("p (k n) -> p k n", k=8), [("wbs", WBS["g"], 4 * j + i) for i in range(4)]

    wb_alloc(8, "wb")

    def cp(name, idx=0, n=1):
        return cpk[:, CP[name] + idx:CP[name] + idx + n]

    for t_, val in [(ident_f, 1.0), (U01, 1.0), (L01, 1.0), (maskU, 0.0), (maskL, 0.0), (ones_f, 1.0)]:
        S.op("pool", lambda t_=t_, val=val: P.memset(t_[:], val), writes=[t_])
    S.op("pool", lambda: P.memset(ones_b[:], 1.0), writes=[ones_b])
    S.op("pool", lambda: P.affine_select(ident_f[:], ident_f[:], [[1, 128]], ALU.is_equal, 0.0, base=0, channel_multiplier=-1), reads=[ident_f], writes=[ident_f])
    S.op("pool", lambda: P.affine_select(U01[:], U01[:], [[1, 128]], ALU.is_ge, 0.0, base=0, channel_multiplier=-1), reads=[U01], writes=[U01])
    S.op("pool", lambda: P.affine_select(L01[:], L01[:], [[-1, 128]], ALU.is_ge, 0.0, base=0, channel_multiplier=1), reads=[L01], writes=[L01])
    S.op("pool", lambda: P.affine_select(maskU[:], maskU[:], [[1, 128]], ALU.is_ge, NEG, base=0, channel_multiplier=-1), reads=[maskU], writes=[maskU])
    S.op("pool", lambda: P.affine_select(maskL[:], maskL[:], [[-1, 128]], ALU.is_ge, NEG, base=0, channel_multiplier=1), reads=[maskL], writes=[maskL])
    S.op("dve", lambda: V.tensor_copy(ident_b[:], ident_f[:]), reads=[ident_f], writes=[ident_b])
    S.dma("sp", cpk[:], colpack, writes=[cpk])
    S.dma("sp", bkB[:], rowpack[:, RP["bk"]:RP["bk"] + 512].to_broadcast([128, 512]), writes=[bkB])
    S.dma("sp", bvB[:], rowpack[:, RP["bv"]:RP["bv"] + 512].to_broadcast([128, 512]), writes=[bvB])
    S.dma("sp", bgB[:], rowpack[:, RP["bg"]:RP["bg"] + 16].to_broadcast([128, 16]), writes=[bgB])
    S.dma("sp", ccs[:], ccol, writes=[ccs])
    S.op("act", lambda: A.activation(csil[:], ccs[:], AF.Silu), reads=[ccs], writes=[csil])

    def load_w(dst3, src2, key):
        S.dma("pool", dst3, src2.rearrange("(kt p) n -> p kt n", p=128), writes=(key if isinstance(key, list) else [key]))

    MOD = {}

    def mod_alloc(names, group):
        for n_ in names:
            MOD[n_] = sb(n_, [128, D] if n_ == "ngB" else [128, 2, D], F32, group, high=group.startswith("mod2"))

    last_wada = [None]

    def mod_prefetch(col0, w=None, wk_=None, bb=None):
        if w is None:
            w, wk_ = wbig(); bb = bada_b
        last_wada[0] = wk_
        load_w(w, w_ada[:, col0:col0 + 512], wk_)
        S.dma("pool", bb[:], rowpack[:, RP["bada"] + col0:RP["bada"] + col0 + 512], writes=[bb])
        return (w, wk_, bb)

    def mod_chunk(col0, evac, pre=None):
        w, wk_, bb = pre if pre is not None else mod_prefetch(col0)
        for which in range(2):
            p = psum()
            for kt in range(8):
                S.op("pe", lambda kt=kt, which=which, p=p: T.matmul(p[:], csil[:, which * 8 + kt:which * 8 + kt + 1].to_broadcast([128, 128]), w[:, kt, :], start=(kt == 0), stop=False),
                     reads=[csil] + wk_, writes=[p], skip_self=True)
            S.op("pe", lambda p=p: T.matmul(p[:], ones_b[0:1, :], bb[0:1, :], start=False, stop=True), reads=[ones_b, bb], writes=[p], skip_self=True)
            evac(which, p)

    MODC = sb("MODC", [128, 2, 48])
    GCs = {1: sb("GC1", [128, 2, 8]), 2: sb("GC2", [128, 2, 8])}
    csil_v = csil[:, :].rearrange("p (w k) -> p k w", w=2)

    def mod_cols(col0, pre=None):
        w, wk_, bb = pre if pre is not None else mod_prefetch(col0)
        for jj in range(4):
            j = col0 // 128 + jj
            p = psum()
            for kt in range(8):
                S.op("pe", lambda kt=kt: T.matmul(p[:, 0:2], w[:, kt, jj * 128:(jj + 1) * 128], csil_v[:, kt, :], start=(kt == 0), stop=(kt == 7)),
                     reads=[csil] + wk_, writes=[p], skip_self=True, c=0.08)
            S.op("dve", lambda: V.tensor_scalar(MODC[:, :, j], p[:, 0:2], cp("badac", j), None, ALU.add), reads=[p, cpk], writes=[("MODC", j)], c=0.12)

    def mod_gain(layer):
        base = 0 if layer == 1 else 24
        gc = GCs[layer]
        S.op("dve", lambda: V.tensor_scalar(gc[:], MODC[:, :, base + 8:base + 16], 1.0, None, ALU.add), reads=[("MODC", base + 8 + i) for i in range(8)], writes=[gc], c=0.12)
        S.op("dve", lambda: V.tensor_tensor(gc[:], gc[:], cp("n1gc" if layer == 1 else "n2gc", 0, 8).unsqueeze(1).to_broadcast([128, 2, 8]), ALU.mult), reads=[gc, cpk], writes=[gc], c=0.12)

    def load_ng(name):
        ngB = MOD["ngB"]
        S.dma("sp", ngB[:], rowpack[:, RP[name]:RP[name] + D].to_broadcast([128, D]), writes=[ngB])

    def mod_shift_scale(base_col):
        modA, modB, ngB = MOD["modA"], MOD["modB"], MOD["ngB"]
        for half in range(2):
            c0 = half * 512
            mod_chunk(base_col + c0, lambda which, p, c0=c0: S.op("act", lambda: A.copy(modB[:, which, c0:c0 + 512], p[:]), reads=[p], writes=[modB]))
            mod_chunk(base_col + D + c0, lambda which, p, c0=c0: S.op("dve", lambda: V.scalar_tensor_tensor(modA[:, which, c0:c0 + 512], p[:], 1.0, ngB[:, c0:c0 + 512], ALU.add, ALU.mult), reads=[p, ngB], writes=[modA]))

    def mod_gate(base_col, pres=None):
        modG = MOD["modG"]
        for half in range(2):
            c0 = half * 512
            mod_chunk(base_col + c0, lambda which, p, c0=c0: S.op("act", lambda: A.copy(modG[:, which, c0:c0 + 512], p[:]), reads=[p], writes=[modG]),
                      pre=(pres[half] if pres else None))

    def rsqrt_cols(dst, src, scale):
        S.op("dve", lambda: V.tensor_scalar(dst, src, scale, EPS, ALU.mult, ALU.add), reads=[sm], writes=[sm])
        S.op("act", lambda: A.activation(dst, dst, AF.Ln), reads=[sm], writes=[sm])
        S.op("act", lambda: A.activation(dst, dst, AF.Exp, scale=-0.5), reads=[sm], writes=[sm])

    def chunk_src(gi):
        return (xp[gi * 128:(gi + 1) * 128, :], 0) if gi < 4 else (xs[(gi - 4) * 128:(gi - 3) * 128, :], 1)

    def norm_phase(chunks, get_x, which_of, dstT_of, dkey_of, mod_thunk, group, batched=True, layer=1):
        n_ = len(chunks)
        ssb = sb("ssb_" + group, [128, 64], F32, group)
        S.alias["ssb"] = "ssb_" + group
        junk = sb("junk_" + group, [128, D], BF16, group)
        xns = [sb("xn%d_" % i + group, [128, D], F32, group) for i in range(2)]
        if not batched:
            mod_thunk()
        for i, gi in enumerate(chunks):
            xin, xkey = get_x(gi, 0, i)
            S.op("act", lambda: A.activation(junk[:], xin, AF.Square, accum_out=ssb[:, i:i + 1]), reads=[xkey], writes=[junk, ("ssb", i)], c=0.6)
            if not batched:
                S.op("dve", lambda: V.tensor_scalar(ssb[:, 32 + i:33 + i], ssb[:, i:i + 1], 1.0 / D, EPS, ALU.mult, ALU.add), reads=[("ssb", i)], writes=[("ssb", 32 + i)], c=0.15)
                S.op("act", lambda: A.activation(ssb[:, 32 + i:33 + i], ssb[:, 32 + i:33 + i], AF.Ln), reads=[("ssb", 32 + i)], writes=[("ssb", 32 + i)], c=0.2)
                S.op("act", lambda: A.activation(ssb[:, 32 + i:33 + i], ssb[:, 32 + i:33 + i], AF.Exp, scale=-0.5), reads=[("ssb", 32 + i)], writes=[("ssb", 32 + i)], c=0.2)
        if batched:
            mod_thunk()
        if batched:
            S.op("dve", lambda: V.tensor_scalar(ssb[:, 32:32 + n_], ssb[:, 0:n_], 1.0 / D, EPS, ALU.mult, ALU.add), reads=[("ssb", i) for i in range(n_)], writes=[ssb])
            S.op("act", lambda: A.activation(ssb[:, 32:32 + n_], ssb[:, 32:32 + n_], AF.Ln), reads=[ssb], writes=[ssb])
            S.op("act", lambda: A.activation(ssb[:, 32:32 + n_], ssb[:, 32:32 + n_], AF.Exp, scale=-0.5), reads=[ssb], writes=[ssb])
        gc = GCs[layer]; base = 0 if layer == 1 else 24
        shk = [("MODC", base + i) for i in range(8)]
        for i, gi in enumerate(chunks):
            xin, xkey = get_x(gi, 1, i)
            which = which_of(gi)
            xn = xns[i % 2]
            S.op("dve", lambda: V.tensor_scalar(xn[:], xin, ssb[:, 32 + i:33 + i], None, ALU.mult), reads=[xkey, (ssb if batched else ("ssb", 32 + i))], writes=[xn], c=1.1)
            pp = [psum(), psum()]
            for kt in range(8):
                S.op("pe", lambda kt=kt: T.transpose(pp[kt // 4][:, (kt % 4) * 128:(kt % 4 + 1) * 128], xn[:, kt * 128:(kt + 1) * 128], ident_f[:]), reads=[xn, ident_f], writes=[pp[kt // 4]], skip_self=True, c=0.15)
            dst = dstT_of(gi)
            for kt in range(8):
                src_ = pp[kt // 4][:, (kt % 4) * 128:(kt % 4 + 1) * 128]
                if kt in (1, 4, 6):
                    S.op("dve", lambda kt=kt, src_=src_: V.scalar_tensor_tensor(dst[:, kt, :], src_, gc[:, which, kt:kt + 1], MODC[:, which, base + kt:base + kt + 1].to_broadcast([128, 128]), ALU.mult, ALU.add),
                         reads=[pp[kt // 4], gc] + shk, writes=[dkey_of(gi)], c=0.25)
                else:
                    S.op("act", lambda kt=kt, src_=src_: A.activation(dst[:, kt, :], src_, AF.Identity, bias=MODC[:, which, base + kt:base + kt + 1], scale=gc[:, which, kt:kt + 1]),
                         reads=[pp[kt // 4], gc] + shk, writes=[dkey_of(gi)], c=0.28)

    S.begin_defer()
    hTf = sb("hTf", [128, 8, 7 * 128], BF16, "hTf0")

    def hT_chunk(gi):
        return hT[:, :, gi * 128:(gi + 1) * 128] if gi < NFULL else hTf[:, :, (gi - NFULL) * 128:(gi - NFULL + 1) * 128]

    xall = sb("xall", [128, NCH, D], F32, "p1a")

    def get_x_1a(gi, stage, i):
        if stage == 0:
            src, _ = chunk_src(gi)
            S.dma("sp", xall[:, gi, :], src, reads=(last_wada[0] if (gi >= 2 and last_wada[0]) else []), writes=[("xall", gi)])
        return xall[:, gi, :], ("xall", gi)

    def mod_1a():
        for c_ in range(4):
            mod_cols(c_ * 512)
        mod_gain(1)

    norm_phase(list(range(NCH)), get_x_1a, lambda gi: 0 if gi < 4 else 1, hT_chunk, lambda gi: ("hT", gi), mod_1a, "p1a", batched=False)
    tap("hT", hT[:, :, :].rearrange("p k t -> p (k t)"), [128, 8 * TF], ("hT", 0))
    k_tok = sb("k_tok", [128, NCH, 512], BF16, "kv")
    v_tok = sb("v_tok", [128, NCH, 4, 129], BF16, "kv")
    g_tok = sb("g_tok", [128, NCH, 16], F32, "kv")
    wg_ = sb("wg_", [128, 8, 16], BF16, "kv")
    S.op("pool", lambda: P.memset(v_tok[:], 1.0), writes=[v_tok])
    wk, wkk = wbig(); load_w(wk, w_in[:, OFF["k"][0]:OFF["k"][1]], wkk)
    wv, wvk = wbig(); load_w(wv, w_in[:, OFF["v"][0]:OFF["v"][1]], wvk)
    load_w(wg_[:], w_in[:, GOFF:GOFF + 16], wg_)

    def proj_tm(gi, wt, wkeys, ncol, p):
        hc = hT_chunk(gi)
        for kt in range(8):
            S.op("pe", lambda kt=kt: T.matmul(p[:, 0:ncol], hc[:, kt, :], wt[:, kt, 0:ncol], start=(kt == 0), stop=(kt == 7)), reads=[("hT", gi)] + wkeys, writes=[p], skip_self=True, c=ncol / 2300.0 + 0.04)

    for gi in range(NCH):
        p1 = psum(); proj_tm(gi, wk, wkk, 512, p1)
        S.op("dve", lambda: V.tensor_tensor(k_tok[:, gi, :], p1[:], bkB[:], ALU.add), reads=[p1, bkB], writes=[("k_tok", gi)], c=0.7)
        p2 = psum(); proj_tm(gi, wv, wvk, 512, p2)
        S.op("dve", lambda: V.tensor_tensor(v_tok[:, gi, :, 0:128], p2[:].rearrange("p (h e) -> p h e", h=4), bvB[:].rearrange("p (h e) -> p h e", h=4), ALU.add),
             reads=[p2, bvB, v_tok], writes=[("v_tok", gi)], c=0.7)
        p3 = psum(); proj_tm(gi, wg_, [wg_], 16, p3)
        S.op("dve", lambda: V.tensor_tensor(g_tok[:, gi, :], p3[:, 0:16], bgB[:], ALU.add), reads=[p3, bgB], writes=[("g_tok", gi)], c=0.15)
    tap("k_tok", k_tok[:, :, :].rearrange("p g e -> p (g e)"), [128, NCH * 512], ("k_tok", NCH - 1))
    tap("g_tok", g_tok[:, :, :].rearrange("p g e -> p (g e)"), [128, NCH * 16], ("g_tok", NCH - 1))

    end_phase("p1a", "hTf0")

    def hT_keys_for(c0, n):
        return [("hT", g) for g in range(c0 // 128, (c0 + n + 127) // 128)]

    def proj_fm(wtile, wkey, c0, n, p):
        for kt in range(8):
            S.op("pe", lambda kt=kt: T.matmul(p[:, 0:n], wtile[:, kt, :], hT[:, kt, c0:c0 + n], start=(kt == 0), stop=(kt == 7)),
                 reads=(wkey if isinstance(wkey, list) else [wkey]) + hT_keys_for(c0, n), writes=[p], skip_self=True, c=n / 2300.0 + 0.03)

    PREP = sb("PREP", [128, NCH, 40], F32, "kv")
    sc = sb("sc", [128, 96], F32, "kv")
    scall = sb("scall", [128, NCH, 16], F32, "p1b")
    g4 = g_tok[:, :, :].rearrange("p g (d x) -> p g d x", d=2)
    S.op("act", lambda: A.activation(scall[:, :, 0:8].rearrange("p g (d x) -> p g d x", d=2), g4[:, :, :, 4:8], AF.Exp, scale=-1.0), reads=[("g_tok", NCH - 1)], writes=[scall])
    S.op("act", lambda: A.activation(scall[:, :, 8:16], scall[:, :, 0:8], AF.Ln, bias=1.0), reads=[scall], writes=[scall])
    pcm = psum()
    S.op("pe", lambda: T.matmul(pcm[:, 0:80], U01[:], scall[:, :, 8:12], start=True, stop=True), reads=[U01, scall], writes=[pcm], skip_self=True)
    S.op("pe", lambda: T.matmul(pcm[:, 80:160], L01[:], scall[:, :, 12:16], start=True, stop=True), reads=[L01, scall], writes=[pcm], skip_self=True)
    S.op("pe", lambda: T.matmul(pcm[:, 160:320], ones_f[:], scall[:, :, 8:16], start=True, stop=True), reads=[ones_f, scall], writes=[pcm], skip_self=True)
    for d in range(2):
        pv_ = pcm[:, d * 80:(d + 1) * 80].rearrange("p (g x) -> p g x", x=4)
        S.op("dve", lambda d=d, pv_=pv_: V.tensor_tensor(PREP[:, :, d * 4:d * 4 + 4], g_tok[:, :, d * 8:d * 8 + 4], pv_, ALU.add), reads=[("g_tok", NCH - 1), pcm], writes=[("PREP", 0)])
        S.op("act", lambda d=d, pv_=pv_: A.copy(PREP[:, :, 8 + d * 4:12 + d * 4], pv_), reads=[pcm], writes=[("PREP", 0)])
    S.op("act", lambda: A.copy(PREP[:, :, 32:40], pcm[:, 160:320].rearrange("p (g x) -> p g x", x=8)), reads=[pcm], writes=[("PREP", 0)])
    for gi in range(NCH):
        for d in range(2):
            if gi >= NFULL and d == 0:
                continue
            pk = ("PREP", gi + 1)
            pa_ = psum()
            for hd in range(4):
                j = d * 4 + hd
                S.op("pe", lambda j=j, hd=hd: T.matmul(pa_[:, hd * 128:(hd + 1) * 128], PREP[:, gi, j:j + 1].to_broadcast([128, 128]), ident_f[:], start=True, stop=True),
                     reads=[("PREP", 0), ident_f], writes=[pa_], skip_self=True)
            pa3 = pa_[:].rearrange("p (h s) -> p h s", h=4)
            S.op("dve", lambda: V.reduce_max(PREP[:, gi, 24 + d * 4:28 + d * 4], pa3, axis=AX.X), reads=[pa_], writes=[pk])
    PK = lambda gi: [("PREP", 0), ("PREP", gi + 1)]
    tap("PREP", PREP[:, :, :].rearrange("p g e -> p (g e)"), [128, NCH * 40], ("PREP", NCH))

    zT = sb("zT", [128, 4, TF], BF16, "zact")
    GP = 286 * 2 + 1182
    SEG0 = [15, 301, 587]
    gluT = sb("gluT", [128, 4, GP], BF16, "p1b")
    zsqT = sb("zsqT", [128, 4, TF], BF16, "p1b")
    diag = sb("diag", [128, 4 * CW, 128], BF16, "p1b")
    sgt = [sb("sgt%d" % i, [128, 512], BF16, "p1b") for i in range(2)]
    lnm = [sb("lnm%d" % i, [128, 512], F32, "p1b") for i in range(2)]; lnr = [sb("lnr%d" % i, [128, 512], F32, "p1b") for i in range(2)]
    lnt = [sb("lnt%d" % i, [128, 512], F32, "p1b") for i in range(2)]
    S.op("pool", lambda: P.memset(gluT[:], 0.0), writes=[gluT])
    for ct in range(4):
        for tap_ in range(CW):
            if tap_ % 2 == 0:
                S.op("dve", lambda tap_=tap_: V.tensor_scalar(diag[:, ct * CW + tap_, :], ident_b[:], cp("convw", ct * CW + tap_), None, ALU.mult), reads=[ident_b, cpk], writes=[("diag", ct, 0)], c=0.29)
            else:
                S.op("act", lambda tap_=tap_: A.activation(diag[:, ct * CW + tap_, :], ident_b[:], AF.Copy, scale=cp("convw", ct * CW + tap_)), reads=[ident_b, cpk], writes=[("diag", ct, 1)], c=0.25)
    for ct in range(4):
        wa, wak = wsmall(); load_w(wa, w_in[:, OFF["ua"][0] + ct * 128:OFF["ua"][0] + (ct + 1) * 128], wak)
        wu, wuk = wsmall(); load_w(wu, w_in[:, OFF["ub"][0] + ct * 128:OFF["ub"][0] + (ct + 1) * 128], wuk)
        for ci, (c0, n) in enumerate(TCH):
            pA = psum(); pB = psum()
            proj_fm(wa, wak, c0, n, pA)
            proj_fm(wu, wuk, c0, n, pB)
            sg = sgt[ci % 2]
            S.op("act", lambda: A.activation(sg[:, 0:n], pB[:, 0:n], AF.Sigmoid, bias=cp("bub", ct)), reads=[pB, cpk], writes=[sg], c=0.6)
            pieces = [(0, 256, SEG0[0]), (256, 256, SEG0[1])] if c0 == 0 else [(0, n, SEG0[2] + c0 - 512)]
            for (o_, n_, d0) in pieces:
                S.op("dve", lambda o_=o_, n_=n_, d0=d0: V.scalar_tensor_tensor(gluT[:, ct, d0:d0 + n_], pA[:, o_:o_ + n_], cp("bua", ct), sg[:, o_:o_ + n_], ALU.add, ALU.mult),
                     reads=[pA, sg, cpk, gluT], writes=[("gluT", ct)], c=0.7)
    tapped = False
    cchunks = [("P", 0, 512), ("S", 512, 512), ("S", 1024, 512), ("S", 1536, 128)]
    for cj, (kind, c0, n) in enumerate(cchunks):
        for ct in range(4):
            p = psum()
            for tap_ in range(CW):
                if kind == "P":
                    rhs = gluT[:, ct, 0:572].rearrange("p (s x) -> p s x", s=2)[:, :, tap_:tap_ + 256]
                    out_ = p[:, 0:512].rearrange("p (s x) -> p s x", s=2)
                else:
                    ds = SEG0[2] + c0 - 512
                    rhs = gluT[:, ct, ds + tap_ - 15:ds + tap_ - 15 + n]
                    out_ = p[:, 0:n]
                S.op("pe", lambda tap_=tap_, rhs=rhs, out_=out_: T.matmul(out_, diag[:, ct * CW + tap_, :], rhs, start=(tap_ == 0), stop=(tap_ == CW - 1)),
                     reads=[("diag", ct, 0), ("diag", ct, 1), gluT, ("gluT", ct)], writes=[p], skip_self=True, c=n / 2300.0 + 0.05)
            S.op("act", lambda: A.activation(zT[:, ct, c0:c0 + n], p[:, 0:n], AF.Identity, bias=cp("convb", ct)), reads=[p, cpk], writes=[("zT", cj, ct)], c=0.6)
            S.op("act", lambda: A.activation(zsqT[:, ct, c0:c0 + n], p[:, 0:n], AF.Square, bias=cp("convb", ct)), reads=[p, cpk], writes=[("zsqT", cj, ct)], c=0.6)
        zk = [("zT", cj, ct) for ct in range(4)]; zqk = [("zsqT", cj, ct) for ct in range(4)]
        lm, lr, lt = lnm[cj % 2], lnr[cj % 2], lnt[cj % 2]
        pm = psum(); pq = psum()
        for ct in range(4):
            S.op("pe", lambda ct=ct: T.matmul(pm[:, 0:n], ones_b[:], zT[:, ct, c0:c0 + n], start=(ct == 0), stop=(ct == 3)), reads=[ones_b] + zk, writes=[pm], skip_self=True, c=n / 2300.0 + 0.05)
        for ct in range(4):
            S.op("pe", lambda ct=ct: T.matmul(pq[:, 0:n], ones_b[:], zsqT[:, ct, c0:c0 + n], start=(ct == 0), stop=(ct == 3)), reads=[ones_b] + zqk, writes=[pq], skip_self=True, c=n / 2300.0 + 0.05)
        S.op("act", lambda: A.mul(lm[:, 0:n], pm[:, 0:n], 1.0 / DB), reads=[pm], writes=[lm], c=0.6)
        S.op("dve", lambda: V.tensor_tensor(lt[:, 0:n], lm[:, 0:n], lm[:, 0:n], ALU.mult), reads=[lm], writes=[lt], c=1.1)
        S.op("dve", lambda: V.scalar_tensor_tensor(lr[:, 0:n], pq[:, 0:n], 1.0 / DB, lt[:, 0:n], ALU.mult, ALU.subtract), reads=[pq, lt], writes=[lr], c=0.7)
        S.op("dve", lambda: V.tensor_scalar(lr[:, 0:n], lr[:, 0:n], EPS, None, ALU.add), reads=[lr], writes=[lr], c=0.6)
        S.op("act", lambda: A.activation(lr[:, 0:n], lr[:, 0:n], AF.Ln), reads=[lr], writes=[lr], c=0.6)
        S.op("act", lambda: A.activation(lr[:, 0:n], lr[:, 0:n], AF.Exp, scale=-0.5), reads=[lr], writes=[lr], c=0.6)
        for ct in range(4):
            S.op("dve", lambda ct=ct: V.tensor_tensor(lt[:, 0:n], zT[:, ct, c0:c0 + n], lm[:, 0:n], ALU.subtract), reads=[("zT", cj, ct), lm], writes=[lt], c=0.8)
            S.op("dve", lambda ct=ct: V.tensor_tensor(lt[:, 0:n], lt[:, 0:n], lr[:, 0:n], ALU.mult), reads=[lt, lr], writes=[lt], c=1.1)
            S.op("act", lambda ct=ct: A.activation(zT[:, ct, c0:c0 + n], lt[:, 0:n], AF.Silu, bias=cp("lnb", ct), scale=cp("lng", ct)), reads=[lt, cpk], writes=[("zT", cj, ct), zT], c=0.7)
    tap("zact", zT[:, :, :].rearrange("p k t -> p (k t)"), [128, 4 * TF], zT)
    qT = sb("qT", [128, 4, TF], BF16, "p1c")
    bqs = sb("bqs", [128, 4], F32, "p1c")
    S.op("dve", lambda: V.tensor_scalar(bqs[:], cp("bq", 0, 4), DH ** -0.5, None, ALU.mult), reads=[cpk], writes=[bqs])
    for hd in range(4):
        w, wk_ = wsmall(); load_w(w, w_in[:, OFF["q"][0] + hd * 128:OFF["q"][0] + (hd + 1) * 128], wk_)
        for (c0, n) in TCH:
            p = psum(); proj_fm(w, wk_, c0, n, p)
            S.op("act", lambda: A.activation(qT[:, hd, c0:c0 + n], p[:, 0:n], AF.Identity, bias=bqs[:, hd:hd + 1], scale=DH ** -0.5), reads=[p, bqs], writes=[qT], c=0.6)
    MIN = sb("MIN", [128, (NCH + 6) * 2, 4], F32, "p1c")
    SS = sb("SS", [128, NCH * 2, 20], F32, "p1c")
    EMJ = sb("EMJ", [128, NFULL * 2, 4], F32, "p1c")
    S.op("pool", lambda: P.memset(SS[:, :, :], 0.0), writes=[SS])
    S.op("pool", lambda: P.memset(MIN[:, :, :], 0.0), writes=[MIN])

    def mslot(gi, d):
        return MIN[:, gi * 2 + d, :]

    chains = []
    for sq in range(2):
        chains.append((0, sq, 1, [2 * sq + 1, 2 * sq], NCH + sq * 2 + 1)); chains.append((0, sq, 0, [2 * sq, 2 * sq + 1], NCH + sq * 2))
    chains.append((1, 2, 1, list(range(NCH - 1, 3, -1)), NCH + 5)); chains.append((1, 2, 0, list(range(4, NFULL)), NCH + 4))
    for (q_, sq, d, lst, fin) in chains:
        if sq == 2:
            S.dma("sp", mslot(lst[0], d), stm[:, d * 4:d * 4 + 4].to_broadcast([128, 4]), reads=[MIN], writes=[("MIN", lst[0], d)])
    for k_ in range(16):
        for (q_, sq, d, lst, fin) in chains:
            if k_ >= len(lst):
                continue
            gi = lst[k_]; nslot = lst[k_ + 1] if k_ + 1 < len(lst) else fin
            s_ = SS[:, gi * 2 + d, :]
            S.op("dve", lambda s_=s_, gi=gi, d=d: V.tensor_tensor(s_[:, 0:4], mslot(gi, d), PREP[:, gi, 24 + d * 4:28 + d * 4], ALU.max), reads=PK(gi) + [("MIN", gi, d), MIN, SS], writes=[("SS", gi, d)], c=0.13)
            S.op("dve", lambda s_=s_, gi=gi, d=d, nslot=nslot: V.tensor_tensor(mslot(nslot, d), s_[:, 0:4], PREP[:, gi, 32 + d * 4:36 + d * 4], ALU.subtract), reads=[("SS", gi, d)] + PK(gi) + [MIN], writes=[("MIN", nslot, d)], c=0.16)
    all_ss = [("SS", gi, d) for gi in range(NCH) for d in range(2) if not (gi >= NFULL and d == 0)]
    all_min = [("MIN", gi, d) for gi in range(NCH) for d in range(2) if not (gi >= NFULL and d == 0)]
    ss4 = lambda lo, hi: SS[:, :, lo:hi].rearrange("p (g d) x -> p g d x", d=2)
    S.op("dve", lambda: V.tensor_tensor(ss4(4, 8), PREP[:, :, 0:8].rearrange("p g (d x) -> p g d x", d=2), ss4(0, 4), ALU.subtract), reads=[("PREP", 0)] + all_ss, writes=[("SSb", 0)])
    S.op("dve", lambda: V.tensor_tensor(SS[:, :, 8:12], MIN[:, 0:NCH * 2, :], SS[:, :, 0:4], ALU.subtract), reads=all_min + all_ss, writes=[("SSb", 1)])
    S.op("act", lambda: A.activation(SS[:, :, 12:20], SS[:, :, 4:12], AF.Exp), reads=[("SSb", 0), ("SSb", 1)], writes=[("SSb", 2)])
    S.op("dve", lambda: V.tensor_tensor(EMJ[:, :, :].rearrange("p (g d) x -> p g d x", d=2), PREP[:, 0:NFULL, 8:16].rearrange("p g (d x) -> p g d x", d=2),
                                        SS[:, 0:NFULL * 2, 0:4].rearrange("p (g d) x -> p g d x", d=2), ALU.subtract), reads=[("PREP", 0)] + all_ss, writes=[EMJ])
    S.op("act", lambda: A.activation(EMJ[:, :, :], EMJ[:, :, :], AF.Exp), reads=[EMJ], writes=[EMJ])
    SSK = [("SSb", 2)]
    end_phase("p1b", "wb")

    sbc = lambda name, shape, dt=F32: sb(name, shape, dt, "p1c")
    kT = sbc("kT", [128, 4, TF], BF16)
    hnT = sb("hnT", [128, 4, TF], BF16, "hnT")
    for gi in range(NFULL):
        pt = psum(); ptb = pt[:].bitcast(BF16)
        for hd in range(4):
            S.op("pe", lambda hd=hd: T.transpose(ptb[:, hd * 128:(hd + 1) * 128], k_tok[:, gi, hd * 128:(hd + 1) * 128], ident_b[:]), reads=[("k_tok", gi), ident_b], writes=[pt], skip_self=True, c=0.12)
        S.op("act", lambda: A.copy(kT[:, :, gi * 128:(gi + 1) * 128], ptb[:, 0:512].rearrange("p (h t) -> p h t", h=4)), reads=[pt], writes=[kT], c=0.6)
    tap("qT", qT[:, :, :].rearrange("p k t -> p (k t)"), [128, 4 * TF], qT)

    CIN = sbc("CIN", [128, NFULL * 2, 4 * 129], BF16)
    Cst = {(q_, d): sbc("Cst%d%d" % (q_, d), [128, 4, 129]) for q_ in range(2) for d in range(2)}
    Cdc = {(q_, d): sbc("Cdc%d%d" % (q_, d), [128, 4, 129]) for q_ in range(2) for d in range(2)}
    VWs = [sbc("VWs%d" % i, [128, 4, 129], BF16) for i in range(4)]
    vwn = [0]

    def cin(gi, d):
        return CIN[:, gi * 2 + d, :].rearrange("p (h e) -> p h e", h=4)


    def make_vw(gi, d, on_pool=False):
        vw = VWs[vwn[0] % len(VWs)]; vwn[0] += 1
        if on_pool:
            S.op("pool", lambda: P.tensor_tensor(vw[:], v_tok[:, gi, :, :], SS[:, gi * 2 + d, 12:16].unsqueeze(2).to_broadcast([128, 4, 129]), ALU.mult), reads=[("v_tok", gi)] + SSK, writes=[vw], c=1.2)
        else:
            for hd in range(4):
                S.op("act", lambda hd=hd: A.activation(vw[:, hd, :], v_tok[:, gi, hd, :], AF.Copy, scale=SS[:, gi * 2 + d, 12 + hd:13 + hd]), reads=[("v_tok", gi)] + SSK, writes=[vw], c=0.45)
        return vw

    def c_step(q_, d, gi):
        C = Cst[(q_, d)]; Cd = Cdc[(q_, d)]
        vw = make_vw(gi, d, on_pool=True)
        pd = [psum7(), psum7()]
        for hd in range(4):
            o_ = (hd % 2) * 129
            S.op("pe", lambda hd=hd, o_=o_: T.matmul(pd[hd // 2][:, o_:o_ + 129], k_tok[:, gi, hd * 128:(hd + 1) * 128], vw[:, hd, :], start=True, stop=True),
                 reads=[vw, ("k_tok", gi)], writes=[pd[hd // 2]], skip_self=True, c=0.1)
        S.op("pool", lambda: P.tensor_tensor(Cd[:], C[:], SS[:, gi * 2 + d, 16:20].unsqueeze(2).to_broadcast([128, 4, 129]), ALU.mult), reads=[C] + SSK, writes=[Cd], c=1.4)
        if gi < NFULL:
            S.op("act", lambda: A.copy(cin(gi, d), Cd[:]), reads=[Cd], writes=[("CIN", gi, d)], c=0.62)
        for b2 in range(2):
            S.op("dve", lambda b2=b2: V.tensor_tensor(C[:, 2 * b2:2 * b2 + 2, :], Cd[:, 2 * b2:2 * b2 + 2, :], pd[b2][:, 0:258].rearrange("p (h e) -> p h e", h=2), ALU.add),
                 reads=[Cd, pd[b2]], writes=[C], c=0.42)

    for (q_, sq, d, lst, fin) in chains:
        if sq == 2:
            S.dma("sp", Cst[(q_, d)][:], stCn[:, d * 4:d * 4 + 4, :], writes=[Cst[(q_, d)]])
    NB_ = 3
    ST = [sbc("ST%d" % i, [128, 4, 128], BF16) for i in range(NB_)]
    hdir = [sbc("hdir%d" % i, [128, 4, 128]) for i in range(4)]
    hnb = [sbc("hnb%d" % i, [128, 4, 128], BF16) for i in range(2)]
    hjunk = sbc("hjunk", [128, 128], BF16)
    scO = [sbc("scO%d" % i, [128, 8]) for i in range(NB_)]
    scF = [sbc("scF%d" % i, [128, 8]) for i in range(2)]
    den_bank = ps[7]
    ps_n = [7]

    units = [(gi, d) for gi in range(NFULL) for d in (1, 0)]
    pqs = {}

    vws = {}

    def u_pq(u):
        gi, d = units[u]; c0 = gi * 128
        pq_ = psum7(); pqs[u] = pq_
        for hd in range(4):
            S.op("pe", lambda hd=hd: T.matmul(pq_[:, hd * 128:(hd + 1) * 128], kT[:, hd, c0:c0 + 128], qT[:, hd, c0:c0 + 128], start=True, stop=True),
                 reads=[kT, qT], writes=[pq_], skip_self=True, c=0.1)
        vws[u] = make_vw(gi, d)

    pns = {}

    def u_st_mm(u):
        gi, d = units[u]; c0 = gi * 128; b_ = u % NB_
        pq_ = pqs[u]; vw = vws[u]
        m01 = U01 if d == 0 else L01
        S.op("dve", lambda: V.tensor_tensor(ST[b_][:], pq_[:].rearrange("p (h s) -> p h s", h=4), m01[:].unsqueeze(1).to_broadcast([128, 4, 128]), ALU.mult),
             reads=[pq_, m01], writes=[ST[b_]], c=0.6)
        pn = psum7(); pns[u] = pn
        dk = den_bank; dcol = (u % 4) * 4
        Cb = cin(gi, d)
        for hd in range(4):
            S.op("pe", lambda hd=hd: T.matmul(pn[:, hd * 128:(hd + 1) * 128], qT[:, hd, c0:c0 + 128], Cb[:, hd, 0:128], start=True, stop=False),
                 reads=[qT, ("CIN", gi, d)], writes=[pn], skip_self=True, c=0.1)
            S.op("pe", lambda hd=hd: T.matmul(pn[:, hd * 128:(hd + 1) * 128], ST[b_][:, hd, :], vw[:, hd, 0:128], start=False, stop=True),
                 reads=[ST[b_], vw], writes=[pn], skip_self=True, c=0.1)
        for hd in range(4):
            S.op("pe", lambda hd=hd: T.matmul(den_bank[:, dcol + hd:dcol + hd + 1], qT[:, hd, c0:c0 + 128], Cb[:, hd, 128:129], start=True, stop=False),
                 reads=[qT, ("CIN", gi, d)], writes=[dk], skip_self=True, c=0.1)
            S.op("pe", lambda hd=hd: T.matmul(den_bank[:, dcol + hd:dcol + hd + 1], ST[b_][:, hd, :], vw[:, hd, 128:129], start=False, stop=True),
                 reads=[ST[b_], vw], writes=[dk], skip_self=True, c=0.1)

    def u_fin(u):
        gi, d = units[u]; b_ = u % NB_
        sc_ = scO[b_]; pn = pns[u]
        dk = den_bank; dcol = (u % 4) * 4
        dn = den_bank[:, dcol:dcol + 4]
        S.op("dve", lambda: V.tensor_scalar(sc_[:, 0:4], dn, -1.0, None, ALU.mult), reads=[dk], writes=[sc_], c=0.1)
        S.op("dve", lambda: V.tensor_tensor(sc_[:, 0:4], sc_[:, 0:4], dn, ALU.max), reads=[dk, sc_], writes=[sc_], c=0.16)
        S.op("dve", lambda: V.tensor_tensor(sc_[:, 0:4], sc_[:, 0:4], EMJ[:, gi * 2 + d, :], ALU.max), reads=[sc_, EMJ], writes=[sc_], c=0.16)
        S.op("dve", lambda: V.reciprocal(sc_[:, 4:8], sc_[:, 0:4]), reads=[sc_], writes=[sc_], c=0.18)
        hd_ = hdir[u % 4]
        S.op("dve", lambda: V.tensor_tensor(hd_[:], pn[:].rearrange("p (h e) -> p h e", h=4), sc_[:, 4:8].unsqueeze(2).to_broadcast([128, 4, 128]), ALU.mult), reads=[pn, sc_], writes=[hd_], c=0.7)
        if d == 0:
            hb_ = hdir[(u - 1) % 4]; hf_ = hd_; c0 = gi * 128; f_ = scF[gi % 2]
            S.op("pool", lambda: P.tensor_tensor(hf_[:], hf_[:], hb_[:], ALU.add), reads=[hf_, hb_], writes=[hf_], c=1.26)
            hq_ = hdir[(u - 1) % 4]
            S.op("pool", lambda: P.tensor_tensor(hq_[:], hf_[:], hf_[:], ALU.mult), reads=[hf_], writes=[hq_], c=1.26)
            S.op("dve", lambda: V.reduce_sum(f_[:, 0:4], hq_[:], axis=AX.X), reads=[hq_], writes=[f_], c=0.69)
            S.op("dve", lambda: V.tensor_scalar(f_[:, 4:8], f_[:, 0:4], 1.0 / DH, EPS, ALU.mult, ALU.add), reads=[f_], writes=[f_], c=0.17)
            S.op("act", lambda: A.activation(f_[:, 4:8], f_[:, 4:8], AF.Ln), reads=[f_], writes=[f_], c=0.21)
            S.op("act", lambda: A.activation(f_[:, 4:8], f_[:, 4:8], AF.Exp, scale=-0.5), reads=[f_], writes=[f_], c=0.21)
            hn_ = hnb[gi % 2]
            S.op("dve", lambda: V.tensor_tensor(hn_[:], hf_[:], f_[:, 4:8].unsqueeze(2).to_broadcast([128, 4, 128]), ALU.mult), reads=[hf_, f_], writes=[hn_], c=0.69)
            pt = psum7(); ptb = pt[:].bitcast(BF16)
            for hd in range(4):
                S.op("pe", lambda hd=hd: T.transpose(ptb[:, hd * 128:(hd + 1) * 128], hn_[:, hd, :], ident_b[:]), reads=[hn_, ident_b], writes=[pt], skip_self=True, c=0.1)
            S.op("act", lambda: A.copy(hnT[:, :, c0:c0 + 128], ptb[:, 0:512].rearrange("p (h t) -> p h t", h=4)), reads=[pt], writes=[hnT], c=0.6)

    def emit_units(glist):
        us = []
        for gi in glist:
            us += [units.index((gi, 1)), units.index((gi, 0))]
        if not us:
            return
        u_pq(us[0])
        for i, u in enumerate(us):
            if i + 1 < len(us):
                u_pq(us[i + 1])
            u_st_mm(u)
            if i >= 1:
                u_fin(us[i - 1])
        u_fin(us[-1])

    ready = {1: [0, 1], 3: [2, 3], 8: [12, 11], 9: [10], 10: [9], 11: [8], 12: [7], 13: [6], 14: [5], 15: [4]}
    for k_ in range(16):
        for (q_, sq, d, lst, fin) in chains:
            if sq == 2:
                kk = k_
            else:
                kk = k_ - 2 * sq
                if kk < 0 or kk >= 2:
                    continue
            if kk >= len(lst):
                continue
            if sq < 2 and kk == 0:
                S.op("pool", lambda q_=q_, d=d: P.memset(Cst[(q_, d)][:], 0.0), writes=[Cst[(q_, d)]], c=0.5)
            c_step(q_, d, lst[kk])
            if sq < 2 and kk == len(lst) - 1:
                j0_ = (sq * 2 + d) * 4
                S.dma("sp", oCn[:, j0_:j0_ + 4, :], Cst[(q_, d)][:], reads=[Cst[(q_, d)]])
                S.dma("sp", om[:, j0_:j0_ + 4], MIN[0:1, fin * 2 + d, :], reads=[("MIN", fin, d)])
        emit_units(ready.get(k_, []))
    tap("hnT", hnT[:, :, :].rearrange("p k t -> p (k t)"), [128, 4 * TF], hnT)
    end_phase("p1c", "kv")
    wb_alloc(8, "wb")

    mergedT = sb("mergedT", [128, 8, TF], BF16, "merged")
    sgt = [sb("sgtc%d" % i, [128, 512], BF16, "p1c2") for i in range(2)]
    tmb = sb("tmb", [128, 512], BF16, "p1c2")
    mod_alloc(["modG"], "modG1")
    wo = sb("wo", [128, 8, D], BF16, "p1d")
    g1pre = []
    wa2s = [(sb("wa2_%d" % half, [128, 8, 512], BF16, "p1c2"), sb("bb2_%d" % half, [1, 512], BF16, "p1c2")) for half in range(2)]
    for hd in range(4):
        w, wk_ = wsmall(); load_w(w, w_in[:, OFF["o"][0] + hd * 128:OFF["o"][0] + (hd + 1) * 128], wk_)
        if hd == 1:
            for half in range(2):
                wa2, bb2 = wa2s[half]
                g1pre.append(mod_prefetch(2 * D + half * 512, wa2[:, :, :], [wa2], bb2))
        if hd == 3:
            load_w(wo[:], w_out, wo)
        for ci, (c0, n) in enumerate(TCH):
            p = psum(); proj_fm(w, wk_, c0, n, p)
            sg = sgt[ci % 2]
            S.op("act", lambda: A.activation(sg[:, 0:n], p[:, 0:n], AF.Sigmoid, bias=cp("bo", hd)), reads=[p, cpk], writes=[sg])
            S.op("dve", lambda: V.scalar_tensor_tensor(hnT[:, hd, c0:c0 + n], hnT[:, hd, c0:c0 + n], cp("mng", hd), sg[:, 0:n], ALU.mult, ALU.mult), reads=[hnT, cpk, sg], writes=[hnT])
    mod_gate(2 * D, g1pre)
    MOD1G = MOD["modG"]
    mod_alloc(["modG"], "mod2g")
    m2cols = [3 * D, 3 * D + 512, 4 * D, 4 * D + 512, 5 * D, 5 * D + 512]
    m2pre = {}

    def m2_prefetch(i):
        w_, k_, b_ = g1pre[i % 2]
        m2pre[i] = mod_prefetch(m2cols[i], w_, k_, b_)

    def m2_compute(i):
        modG2 = MOD["modG"]
        c0 = (i % 2) * 512
        if i < 4:
            mod_cols(m2cols[i], pre=m2pre[i])
            if i == 3:
                mod_gain(2)
        else:
            ev = lambda which, p: S.op("act", lambda: A.copy(modG2[:, which, c0:c0 + 512], p[:]), reads=[p], writes=[modG2])
            mod_chunk(m2cols[i], ev, pre=m2pre[i])

    m2_prefetch(0); m2_prefetch(1)
    for dt_ in range(8):
        if dt_ >= 1 and dt_ - 1 < 6:
            m2_compute(dt_ - 1)
            if dt_ + 1 < 6:
                m2_prefetch(dt_ + 1)
        for br, (gname, bname, wproj, srcT) in enumerate((("gb", "bgb", w_pb, zT), ("ga", "bga", w_pa, hnT))):
            wg, wgk = wsmall(); load_w(wg, w_in[:, OFF[gname][0] + dt_ * 128:OFF[gname][0] + (dt_ + 1) * 128], wgk)
            wp, wpk = wsmall(); load_w(wp[:, 0:4, :], wproj[:, dt_ * 128:(dt_ + 1) * 128], wpk)
            for ci, (c0, n) in enumerate(TCH):
                pA = psum(); pB = psum()
                proj_fm(wg, wgk, c0, n, pA)
                for ct in range(4):
                    S.op("pe", lambda ct=ct: T.matmul(pB[:, 0:n], wp[:, ct, :], srcT[:, ct, c0:c0 + n], start=(ct == 0), stop=(ct == 3)), reads=wpk + [srcT], writes=[pB], skip_self=True)
                sg = sgt[ci % 2]
                S.op("act", lambda: A.activation(sg[:, 0:n], pA[:, 0:n], AF.Sigmoid, bias=cp(bname, dt_)), reads=[pA, cpk], writes=[sg])
                if br == 0:
                    S.op("dve", lambda: V.tensor_tensor(mergedT[:, dt_, c0:c0 + n], pB[:, 0:n], sg[:, 0:n], ALU.mult), reads=[pB, sg], writes=[mergedT])
                else:
                    S.op("dve", lambda: V.tensor_tensor(tmb[:, 0:n], pB[:, 0:n], sg[:, 0:n], ALU.mult), reads=[pB, sg], writes=[tmb])
                    S.op("dve", lambda: V.tensor_tensor(mergedT[:, dt_, c0:c0 + n], mergedT[:, dt_, c0:c0 + n], tmb[:, 0:n], ALU.add), reads=[mergedT, tmb], writes=[mergedT])
    tap("mergedT", mergedT[:, :, :].rearrange("p k t -> p (k t)"), [128, 8 * TF], mergedT)
    end_phase("p1c2", "zact", "hnT")

    x1 = sb("x1", [128, NFULL, D], F32, "x1", high=True)
    modG = MOD1G
    xt = [sb("xtd%d" % i, [128, D], F32, "p1d") for i in range(2)]
    tmpos = [sb("tmpo%d" % i, [128, 512], F32, "p1d") for i in range(2)]
    for gi in range(NFULL):
        src, which = chunk_src(gi)
        x_ = xt[gi % 2]
        S.dma("sp", x_[:], src, writes=[x_])
        for dc in range(2):
            p = psum()
            for kt in range(8):
                S.op("pe", lambda kt=kt: T.matmul(p[:], mergedT[:, kt, gi * 128:(gi + 1) * 128], wo[:, kt, dc * 512:(dc + 1) * 512], start=(kt == 0), stop=(kt == 7)),
                     reads=[mergedT, wo], writes=[p], skip_self=True)
            tmpo = tmpos[dc]
            S.op("dve", lambda: V.tensor_tensor(tmpo[:], p[:], modG[:, which, dc * 512:(dc + 1) * 512], ALU.mult), reads=[p, modG], writes=[tmpo], c=0.7)
            S.op("pool", lambda: P.tensor_tensor(x1[:, gi, dc * 512:(dc + 1) * 512], tmpo[:], x_[:, dc * 512:(dc + 1) * 512], ALU.add), reads=[tmpo, x_], writes=[("x1", gi)], c=1.3)
    tap("x1", x1[:, :, :].rearrange("p g e -> p (g e)"), [128, NFULL * D], ("x1", NFULL - 1))

    norm_phase(list(range(NFULL)), lambda gi, stage, i: (x1[:, gi, :], ("x1", gi)), lambda gi: 0 if gi < 4 else 1,
               lambda gi: hT[:, :, gi * 128:(gi + 1) * 128], lambda gi: ("hT", gi), lambda: None, "p2a", batched=False, layer=2)
    end_phase("p1d", "merged", "modG1", "p2a")

    actT = sb("actT", [128, NF, 1536], BF16, "actT")
    gP = [sb("gP%d" % i, [128, 2, 258], BF16, "p2b") for i in range(2)]
    gS = [sb("gS%d" % i, [128, 18, 66], BF16, "p2b") for i in range(2)]
    dg = [sb("dg%d" % i, [128, 9, 128], BF16, "p2b") for i in range(2)]
    gl = [sb("gl%d" % i, [128, 512], BF16, "p2b") for i in range(2)]
    for i in range(2):
        S.op("pool", lambda i=i: P.memset(gP[i][:], 0.0), writes=[gP[i]])
        S.op("pool", lambda i=i: P.memset(gS[i][:], 0.0), writes=[gS[i]])
    for f in range(NF):
        wgg, wggk = wsmall(); load_w(wgg, w_up[:, f * 128:(f + 1) * 128], wggk)
        wgv, wgvk = wsmall(); load_w(wgv, w_up[:, DFF + f * 128:DFF + (f + 1) * 128], wgvk)
        gp_, gs_, dg_ = gP[f % 2], gS[f % 2], dg[f % 2]
        for t9 in range(9):
            S.op("dve", lambda t9=t9: V.tensor_scalar(dg_[:, t9, :], ident_b[:], cp("ffnw", f * 9 + t9), None, ALU.mult), reads=[ident_b, cpk], writes=[dg_])
        p = psum(); proj_fm(wgg, wggk, 0, 512, p)
        S.op("act", lambda: A.copy(gp_[:, :, 1:257], p[:].rearrange("p (s t) -> p s t", s=2)), reads=[p], writes=[gp_])
        for r0, (c0, n) in ((0, (512, 512)), (8, (1024, 512)), (16, (1536, 64))):
            p = psum(); proj_fm(wgg, wggk, c0, n, p)
            nr = n // 64
            S.op("act", lambda p=p, r0=r0, nr=nr, n=n: A.copy(gs_[:, 1 + r0:1 + r0 + nr, 1:65], p[:, 0:n].rearrange("p (r t) -> p r t", r=nr)), reads=[p], writes=[gs_])
        outs = []
        pc = psum()
        for dc in range(3):
            S.op("pe", lambda dc=dc: T.matmul(pc[:].rearrange("p (s t) -> p s t", s=2), dg_[:, 3 + dc, :], gp_[:, :, dc:dc + 256], start=(dc == 0), stop=(dc == 2)),
                 reads=[dg_, gp_], writes=[pc], skip_self=True)
        outs.append((pc, 0))
        for half in range(2):
            pc = psum()
            for t9 in range(9):
                dr, dc = t9 // 3, t9 % 3
                S.op("pe", lambda t9=t9, dr=dr, dc=dc, pc=pc: T.matmul(pc[:].rearrange("p (r t) -> p r t", r=8), dg_[:, t9, :], gs_[:, half * 8 + dr:half * 8 + dr + 8, dc:dc + 64], start=(t9 == 0), stop=(t9 == 8)),
                     reads=[dg_, gs_], writes=[pc], skip_self=True)
            outs.append((pc, 512 + half * 512))
        for oi, (pc, c0) in enumerate(outs):
            pv = psum(); proj_fm(wgv, wgvk, c0, 512, pv)
            g_ = gl[oi % 2]
            S.op("act", lambda pc=pc, g_=g_: A.activation(g_[:], pc[:], AF.Gelu_apprx_tanh, bias=cp("ffnb", f)), reads=[pc, cpk], writes=[g_])
            S.op("dve", lambda pv=pv, g_=g_, c0=c0: V.tensor_tensor(actT[:, f, c0:c0 + 512], g_[:], pv[:], ALU.mult), reads=[g_, pv], writes=[("actT", f)])
    tap("actT", actT[:, :, :].rearrange("p f t -> p (f t)"), [128, NF * 1536], ("actT", NF - 1))
    wd00 = sb("wd0_0", [128, 11, 512], BF16, "p2c")
    S.dma("pool", wd00[:, :, :], w_down[0:11 * 128, 0:512].rearrange("(f p) n -> p f n", p=128), writes=[wd00])
    end_phase("p2b", "hT", "wb")

    wds = [[wd00, sb("wd0_1", [128, 11, 512], BF16, "p2c")], [sb("wd1_%d" % h_, [128, 11, 512], BF16, "p2c") for h_ in range(2)]]
    mod_alloc(["ngB"], "p2c")
    modG = MOD["modG"]; ngB = MOD["ngB"]
    load_ng("fng")
    tmpos2 = [sb("tmpo2_%d" % i, [128, 512], F32, "p2c") for i in range(2)]
    yo = [sb("yo%d" % i, [128, D], F32, "p2c") for i in range(2)]
    junk = sb("junk3", [128, D], BF16, "p2c")
    fsb = sb("fsb", [128, 32], F32, "p2c")

    def fin_a(gi):
        S.op("act", lambda: A.activation(junk[:], x1[:, gi, :], AF.Square, accum_out=fsb[:, gi:gi + 1]), reads=[("x1", gi)], writes=[junk, ("fsb", gi)])
        S.op("dve", lambda: V.tensor_scalar(fsb[:, 16 + gi:17 + gi], fsb[:, gi:gi + 1], 1.0 / D, EPS, ALU.mult, ALU.add), reads=[("fsb", gi)], writes=[("fsb", gi)])
        S.op("act", lambda: A.activation(fsb[:, 16 + gi:17 + gi], fsb[:, 16 + gi:17 + gi], AF.Ln), reads=[("fsb", gi)], writes=[("fsb", gi)])
        S.op("act", lambda: A.activation(fsb[:, 16 + gi:17 + gi], fsb[:, 16 + gi:17 + gi], AF.Exp, scale=-0.5), reads=[("fsb", gi)], writes=[("fsb", gi)])

    def fin_b(gi):
        y_ = yo[gi % 2]
        S.op("dve", lambda: V.scalar_tensor_tensor(y_[:], x1[:, gi, :], fsb[:, 16 + gi:17 + gi], ngB[:], ALU.mult, ALU.mult), reads=[("x1", gi), ("fsb", gi), ngB], writes=[y_])
        dst = yp[gi * 128:(gi + 1) * 128, :] if gi < 4 else ys[(gi - 4) * 128:(gi - 3) * 128, :]
        S.dma("sp", dst, y_[:], reads=[y_])

    for dc in range(2):
        wd = wds[dc]
        if dc == 1:
            S.dma("pool", wd[0][:, :, :], w_down[0:11 * 128, dc * 512:(dc + 1) * 512].rearrange("(f p) n -> p f n", p=128), writes=[wd[0]])
        S.dma("pool", wd[1][:, :, :], w_down[11 * 128:22 * 128, dc * 512:(dc + 1) * 512].rearrange("(f p) n -> p f n", p=128), writes=[wd[1]])
    for dc in range(2):
        wd = wds[dc]
        for gi in range(12):
            which = 0 if gi < 4 else 1
            p = psum()
            for f in range(NF):
                S.op("pe", lambda f=f: T.matmul(p[:], actT[:, f, gi * 128:(gi + 1) * 128], wd[f // 11][:, f % 11, :], start=(f == 0), stop=(f == NF - 1)),
                     reads=[wd[0], wd[1]], writes=[p], skip_self=True)
            tmpo = tmpos2[gi % 2]
            S.op("dve", lambda: V.tensor_tensor(tmpo[:], p[:], modG[:, which, dc * 512:(dc + 1) * 512], ALU.mult), reads=[p, modG], writes=[tmpo], c=0.7)
            S.op("pool", lambda: P.tensor_tensor(x1[:, gi, dc * 512:(dc + 1) * 512], x1[:, gi, dc * 512:(dc + 1) * 512], tmpo[:], ALU.add), reads=[tmpo, ("x1", gi)], writes=[("x1", gi)], c=1.3)
            if dc == 1:
                fin_a(gi)
                if gi >= 1:
                    fin_b(gi - 1)
    fin_b(11)
    S.end_defer()
    S.barrier()
    S.wait_everything("sp")
    top.close()
    S.close()
    return nc, dbg_out, S.n_inst


def _core_inputs(inp, c):
    j, half = c // 2, c % 2
    flip = half == 1
    f = (lambda a, ax: np.flip(a, axis=ax)) if flip else (lambda a, ax: a)
    m = {}
    m["xp"] = np.ascontiguousarray(f(inp["x_prompt"][2 * c:2 * c + 2], 1)).reshape(512, D)
    m["xs"] = np.ascontiguousarray(f(inp["x_sample"][j], 0))
    cvec = np.stack([inp["c_ctx"], inp["c"][j]], 0)
    m["ccol"] = np.ascontiguousarray(cvec.reshape(2, 8, 128).transpose(2, 0, 1).reshape(128, 16))
    dirs = [1, 0] if flip else [0, 1]
    C = inp["state_C"][j, 0][dirs].reshape(8, 128, 128); n = inp["state_n"][j, 0][dirs].reshape(8, 128)
    m["stCn"] = np.ascontiguousarray(np.concatenate([C.transpose(1, 0, 2), n.T[:, :, None]], axis=2))
    m["stm"] = np.ascontiguousarray(inp["state_m"][j, 0][dirs].reshape(1, 8))
    w_in = inp["w_in"][0]; b_in = inp["b_in"][0]
    if flip:
        perm = np.arange(IN_COLS)
        perm[GOFF:GOFF + 16] = np.concatenate([np.arange(GOFF + 8, GOFF + 16), np.arange(GOFF, GOFF + 8)])
        w_in = w_in[:, perm]; b_in = b_in[perm]
    m["w_in"] = np.ascontiguousarray(w_in)
    conv_w = f(inp["conv_dw_w"][0], 0)
    ffn_w = f(f(inp["ffn_dw_w"][0], 0), 1).reshape(9, DFF)
    colt = lambda v: v.reshape(-1, 128).T
    cols = [colt(b_in[OFF[k][0]:OFF[k][1]]) for k in ("q", "k", "o", "ua", "ub", "ga", "gb")]
    cols += [colt(inp[k][0]) for k in ("conv_dw_b", "conv_ln_g", "conv_ln_b", "mlstm_norm_g")]
    cols.append(conv_w.reshape(CW, 4, 128).transpose(2, 1, 0).reshape(128, 4 * CW))
    cols.append(ffn_w.reshape(9, NF, 128).transpose(2, 1, 0).reshape(128, NF * 9))
    cols.append(colt(inp["ffn_dw_b"][0]))
    cols += [colt(inp["b_ada"][0]), colt(inp["norm1_g"][0]), colt(inp["norm2_g"][0])]
    m["colpack"] = np.ascontiguousarray(np.concatenate(cols, axis=1).astype(np.float32))
    rows = [b_in[OFF["k"][0]:OFF["k"][1]], b_in[OFF["v"][0]:OFF["v"][1]], b_in[GOFF:GOFF + 16], inp["norm1_g"][0], inp["norm2_g"][0],
            inp["final_norm_g"], inp["b_ada"][0]]
    m["rowpack"] = np.ascontiguousarray(np.concatenate(rows)[None, :].astype(np.float32))
    return m


_SHARED = ("w_ada", "w_proj_a", "w_proj_b", "w_out", "w_up", "w_down")


def run(inputs, dbg=None, trace=False):
    inp = {k: np.asarray(v) for k, v in inputs.items()}
    nc, dbg_out, n_inst = build(dbg=dbg)
    shared = {k: np.ascontiguousarray(inp[k][0]) for k in _SHARED}
    in_maps = []
    for c in range(8):
        m = _core_inputs(inp, c)
        m.update(shared)
        in_maps.append(m)
    res = run_bass_kernel_spmd(nc, in_maps, core_ids=list(range(8)), **({"trace": True} if trace else {}))
    return res, dbg_out


def kernel(**inputs):
    res, _ = run(inputs)
    y_prompt = np.zeros((16, 256, D), np.float32); y_sample = np.zeros((4, 2048, D), np.float32)
    nC = np.zeros((16, 1, 2, NH, DH, DH), np.float32); nn = np.zeros((16, 1, 2, NH, DH), np.float32); nm = np.zeros((16, 1, 2, NH), np.float32)
    for c in range(8):
        r = res.results[c]
        j, half = c // 2, c % 2
        yp = r["yp"].reshape(2, 256, D); ys = r["ys"]
        oCn = r["oCn"].reshape(128, 2, 2, 4, 129); om = r["om"].reshape(2, 2, 4)
        if half:
            yp = yp[:, ::-1]; ys = ys[::-1]
            oCn = oCn[:, :, ::-1]; om = om[:, ::-1]
        y_prompt[2 * c:2 * c + 2] = yp
        y_sample[j, 1024 * half:1024 * (half + 1)] = ys
        for s in range(2):
            nC[2 * c + s, 0] = oCn[:, s, :, :, 0:128].transpose(1, 2, 0, 3)
            nn[2 * c + s, 0] = oCn[:, s, :, :, 128].transpose(1, 2, 0)
            nm[2 * c + s, 0] = om[s]
    return (y_prompt, y_sample, nC, nn, nm)
```
